# Optimizing a Trainium2 kernel written in Bass

```python
import jax, jax.numpy as jnp
from jax import lax
import numpy as np

D_MODEL = 1024
BATCH = 32
SEQ = 2048
DEPTH = 1
DEC_BATCH = 2
DEC_SEQ = 16384
PAST_LEN = 128

HEAD_DIM = 64
A_Q_HEADS = 8
A_KV_HEADS = 2
B_Q_HEADS = 8
B_KV_HEADS = 2
A_WIDTH = A_Q_HEADS * HEAD_DIM
B_WIDTH = B_Q_HEADS * HEAD_DIM
MIX_WIDTH = A_WIDTH + B_WIDTH
A_KV_WIDTH = A_KV_HEADS * HEAD_DIM
B_KV_WIDTH = B_KV_HEADS * HEAD_DIM
IN_COLS = A_WIDTH + 2 * A_KV_WIDTH + B_WIDTH + 2 * B_KV_WIDTH
IN_SPLITS = (A_WIDTH,
             A_WIDTH + A_KV_WIDTH,
             A_WIDTH + 2 * A_KV_WIDTH,
             2 * A_WIDTH + 2 * A_KV_WIDTH,
             2 * A_WIDTH + 2 * A_KV_WIDTH + B_KV_WIDTH)
WINDOW = 128
BLOCK = 128
ROPE_THETA = 10000.0
GRID_W = 64
N_MEM = 256
X_HEADS = 4
X_HEAD_DIM = 128
X_WIDTH = X_HEADS * X_HEAD_DIM
PEER_HEADS = 8
PEER_NKEYS = 128
PEER_EXPERTS = PEER_NKEYS * PEER_NKEYS
PEER_QDIM = 256
PEER_TOPK = 16
PEER_CHUNK = 128
EPS = 1e-6
NEG = -1e30

kernel_name = "hymba_peer_bidir_encoder"


def rms_norm(x, g):
    x32 = x.astype(jnp.float32)
    y = x32 * lax.rsqrt(jnp.mean(x32 * x32, axis=-1, keepdims=True) + EPS)
    return (y * g.astype(jnp.float32)).astype(x.dtype)


def rope(x, pos):
    d = x.shape[-1]
    inv = ROPE_THETA ** (-jnp.arange(0, d, 2, dtype=jnp.float32) / d)
    ang = pos.astype(jnp.float32)[:, None] * inv[None, :]
    cos = jnp.cos(ang)[:, None, :]
    sin = jnp.sin(ang)[:, None, :]
    x32 = x.astype(jnp.float32)
    x1, x2 = x32[..., : d // 2], x32[..., d // 2:]
    return jnp.concatenate([x1 * cos - x2 * sin, x2 * cos + x1 * sin], axis=-1).astype(x.dtype)


def axial_rope(x):
    S = x.shape[1]
    rows = S // GRID_W
    row = jnp.repeat(jnp.arange(rows, dtype=jnp.int32), GRID_W)
    col = jnp.tile(jnp.arange(GRID_W, dtype=jnp.int32), rows)
    h = x.shape[-1] // 2
    return jnp.concatenate([rope(x[..., :h], row), rope(x[..., h:], col)], axis=-1)


def window_sink_attention(q, k, v, sink):
    Bn, S, HQ, d = q.shape
    HKV = k.shape[2]
    G = HQ // HKV
    span = BLOCK + 2 * WINDOW
    pad = ((0, 0), (WINDOW, WINDOW), (0, 0), (0, 0))
    kp = jnp.pad(k, pad)
    vp = jnp.pad(v, pad)
    qg = q.reshape(Bn, S, HKV, G, d)
    sink_g = sink.astype(jnp.float32).reshape(HKV, G)[None, :, :, None]
    scale = d ** -0.5

    def one_block(i):
        start = i * BLOCK
        qb = lax.dynamic_slice_in_dim(qg, start, BLOCK, axis=1)
        kb = lax.dynamic_slice_in_dim(kp, start, span, axis=1)
        vb = lax.dynamic_slice_in_dim(vp, start, span, axis=1)
        qpos = start + jnp.arange(BLOCK)
        kpos = start - WINDOW + jnp.arange(span)
        valid = ((jnp.abs(qpos[:, None] - kpos[None, :]) <= WINDOW)
                 & (kpos >= 0)[None, :] & (kpos < S)[None, :])
        s = jnp.einsum('bqkgd,bjkd->bkgqj', qb, kb,
                       preferred_element_type=jnp.float32) * scale
        s = jnp.where(valid, s, NEG)
        m = jnp.maximum(s.max(axis=-1), sink_g)
        p = jnp.exp(s - m[..., None])
        den = p.sum(axis=-1) + jnp.exp(sink_g - m)
        p = p / den[..., None]
        o = jnp.einsum('bkgqj,bjkd->bqkgd', p, vb.astype(jnp.float32))
        return o.reshape(Bn, BLOCK, HQ * d).astype(v.dtype)

    out = lax.map(one_block, jnp.arange(S // BLOCK))
    return out.transpose(1, 0, 2, 3).reshape(Bn, S, HQ * d)


def dense_block_attention(q, k, v):
    Bn, S, HQ, d = q.shape
    HKV = k.shape[2]
    G = HQ // HKV
    qg = q.reshape(Bn, S, HKV, G, d)
    scale = d ** -0.5

    def one_block(i):
        qb = lax.dynamic_slice_in_dim(qg, i * BLOCK, BLOCK, axis=1)
        s = jnp.einsum('bqkgd,bjkd->bkgqj', qb, k,
                       preferred_element_type=jnp.float32) * scale
        p = jax.nn.softmax(s, axis=-1)
        o = jnp.einsum('bkgqj,bjkd->bqkgd', p, v.astype(jnp.float32))
        return o.reshape(Bn, BLOCK, HQ * d).astype(v.dtype)

    out = lax.map(one_block, jnp.arange(S // BLOCK))
    return out.transpose(1, 0, 2, 3).reshape(Bn, S, HQ * d)


def memory_cross_attention(h, mem_n, w_cq, w_ckv, cqn, ckn, w_co):
    Bn, S, _ = h.shape
    q = (h @ w_cq).reshape(Bn, S, X_HEADS, X_HEAD_DIM)
    k, v = jnp.split(mem_n @ w_ckv, 2, axis=-1)
    k = k.reshape(Bn, N_MEM, X_HEADS, X_HEAD_DIM)
    v = v.reshape(Bn, N_MEM, X_HEADS, X_HEAD_DIM)
    q = rms_norm(q, cqn)
    k = rms_norm(k, ckn)
    s = jnp.einsum('bqhd,bmhd->bhqm', q, k,
                   preferred_element_type=jnp.float32) * (X_HEAD_DIM ** -0.5)
    p = jax.nn.softmax(s, axis=-1)
    o = jnp.einsum('bhqm,bmhd->bqhd', p, v.astype(jnp.float32)).astype(h.dtype)
    return o.reshape(Bn, S, X_WIDTH) @ w_co


def peer_ffn(h, w_pq, pk1, pk2, peer_u, peer_v):
    Bn, S, D = h.shape
    T = Bn * S
    half = PEER_QDIM // 2
    hc = h.reshape(T // PEER_CHUNK, PEER_CHUNK, D)

    def one_chunk(xc):
        q = (xc @ w_pq).reshape(PEER_CHUNK, PEER_HEADS, PEER_QDIM)
        s1 = jnp.einsum('thd,hnd->thn', q[..., :half], pk1,
                        preferred_element_type=jnp.float32)
        s2 = jnp.einsum('thd,hnd->thn', q[..., half:], pk2,
                        preferred_element_type=jnp.float32)
        v1, i1 = lax.top_k(s1, PEER_TOPK)
        v2, i2 = lax.top_k(s2, PEER_TOPK)
        cand = (v1[..., :, None] + v2[..., None, :]).reshape(
            PEER_CHUNK, PEER_HEADS, PEER_TOPK * PEER_TOPK)
        sc, ic = lax.top_k(cand, PEER_TOPK)
        e = (jnp.take_along_axis(i1, ic // PEER_TOPK, axis=-1) * PEER_NKEYS
             + jnp.take_along_axis(i2, ic % PEER_TOPK, axis=-1))
        g = jax.nn.softmax(sc, axis=-1)
        u = peer_u[e]
        a = jax.nn.gelu(jnp.einsum('td,thkd->thk', xc, u,
                                   preferred_element_type=jnp.float32), approximate=False)
        w = (g * a).astype(xc.dtype)
        return jnp.einsum('thk,thkd->td', w, peer_v[e])

    return lax.map(one_chunk, hc).reshape(Bn, S, D)


def encoder_layer(x, mem, ln_mix, w_in, qn_a, kn_a, sink_a, qn_b, kn_b, go_a, go_b, w_out,
                  ln_x, ln_mem, w_cq, w_ckv, cqn, ckn, w_co,
                  ln_ff, w_pq, pk1, pk2, peer_u, peer_v):
    Bn, S, _ = x.shape
    h = rms_norm(x, ln_mix)
    z = h @ w_in
    qa, ka, va, qb, kb, vb = jnp.split(z, IN_SPLITS, axis=-1)
    qa = qa.reshape(Bn, S, A_Q_HEADS, HEAD_DIM)
    ka = ka.reshape(Bn, S, A_KV_HEADS, HEAD_DIM)
    va = va.reshape(Bn, S, A_KV_HEADS, HEAD_DIM)
    qb = qb.reshape(Bn, S, B_Q_HEADS, HEAD_DIM)
    kb = kb.reshape(Bn, S, B_KV_HEADS, HEAD_DIM)
    vb = vb.reshape(Bn, S, B_KV_HEADS, HEAD_DIM)
    t = jnp.arange(S, dtype=jnp.int32)
    qa = rope(rms_norm(qa, qn_a), t)
    ka = rope(rms_norm(ka, kn_a), t)
    qb = axial_rope(rms_norm(qb, qn_b))
    kb = axial_rope(rms_norm(kb, kn_b))
    o_a = rms_norm(window_sink_attention(qa, ka, va, sink_a), go_a)
    o_b = rms_norm(dense_block_attention(qb, kb, vb), go_b)
    x = x + jnp.concatenate([o_a, o_b], axis=-1) @ w_out
    x = x + memory_cross_attention(rms_norm(x, ln_x), rms_norm(mem, ln_mem),
                                   w_cq, w_ckv, cqn, ckn, w_co)
    x = x + peer_ffn(rms_norm(x, ln_ff), w_pq, pk1, pk2, peer_u, peer_v)
    return x


def setup_inputs(seed: int = 0) -> dict:
    key = jax.random.key(seed)
    ks = jax.random.split(key, 32)
    f32 = jnp.float32

    def nrm(k, shape, scale):
        return jax.random.normal(k, shape, f32) * scale

    def gain(k, shape):
        return 1.0 + 0.02 * jax.random.normal(k, shape, f32)

    L = DEPTH
    return {
        "x_prompt": nrm(ks[0], (BATCH, SEQ, D_MODEL), 1.0),
        "x_sample": nrm(ks[1], (DEC_BATCH, DEC_SEQ, D_MODEL), 1.0),
        "mem_prompt": nrm(ks[2], (BATCH, N_MEM, D_MODEL), 1.0),
        "mem_sample": nrm(ks[3], (DEC_BATCH, N_MEM, D_MODEL), 1.0),
        "ln_mix": gain(ks[4], (L, D_MODEL)),
        "w_in": nrm(ks[5], (L, D_MODEL, IN_COLS), D_MODEL ** -0.5),
        "qn_a": gain(ks[6], (L, HEAD_DIM)),
        "kn_a": gain(ks[7], (L, HEAD_DIM)),
        "sink_a": nrm(ks[8], (L, A_Q_HEADS), 0.5),
        "qn_b": gain(ks[9], (L, HEAD_DIM)),
        "kn_b": gain(ks[10], (L, HEAD_DIM)),
        "go_a": gain(ks[11], (L, A_WIDTH)),
        "go_b": gain(ks[12], (L, B_WIDTH)),
        "w_out": nrm(ks[13], (L, MIX_WIDTH, D_MODEL), MIX_WIDTH ** -0.5),
        "ln_x": gain(ks[14], (L, D_MODEL)),
        "ln_mem": gain(ks[15], (L, D_MODEL)),
        "w_cq": nrm(ks[16], (L, D_MODEL, X_WIDTH), D_MODEL ** -0.5),
        "w_ckv": nrm(ks[17], (L, D_MODEL, 2 * X_WIDTH), D_MODEL ** -0.5),
        "cqn": gain(ks[18], (L, X_HEAD_DIM)),
        "ckn": gain(ks[19], (L, X_HEAD_DIM)),
        "w_co": nrm(ks[20], (L, X_WIDTH, D_MODEL), X_WIDTH ** -0.5),
        "ln_ff": gain(ks[21], (L, D_MODEL)),
        "w_pq": nrm(ks[22], (L, D_MODEL, PEER_HEADS * PEER_QDIM), D_MODEL ** -0.5),
        "pk1": nrm(ks[23], (L, PEER_HEADS, PEER_NKEYS, PEER_QDIM // 2), (PEER_QDIM // 2) ** -0.5),
        "pk2": nrm(ks[24], (L, PEER_HEADS, PEER_NKEYS, PEER_QDIM // 2), (PEER_QDIM // 2) ** -0.5),
        "peer_u": nrm(ks[25], (L, PEER_EXPERTS, D_MODEL), D_MODEL ** -0.5),
        "peer_v": nrm(ks[26], (L, PEER_EXPERTS, D_MODEL), (PEER_HEADS * PEER_TOPK) ** -0.5),
    }


def reference(x_prompt, x_sample, mem_prompt, mem_sample,
              ln_mix, w_in, qn_a, kn_a, sink_a, qn_b, kn_b, go_a, go_b, w_out,
              ln_x, ln_mem, w_cq, w_ckv, cqn, ckn, w_co,
              ln_ff, w_pq, pk1, pk2, peer_u, peer_v):
    params = (ln_mix, w_in, qn_a, kn_a, sink_a, qn_b, kn_b, go_a, go_b, w_out,
              ln_x, ln_mem, w_cq, w_ckv, cqn, ckn, w_co,
              ln_ff, w_pq, pk1, pk2, peer_u, peer_v)
    y_prompt = x_prompt
    y_sample = x_sample
    for l in range(DEPTH):
        layer_params = [p[l] for p in params]
        y_prompt = encoder_layer(y_prompt, mem_prompt, *layer_params)
        y_sample = encoder_layer(y_sample, mem_sample, *layer_params)
    return (y_prompt, y_sample)
```

```python
import contextlib
import os
import numpy as np
import concourse.bass as bass
import concourse.mybir as mybir
from concourse.bass_utils import run_bass_kernel_spmd

F32 = mybir.dt.float32
BF16 = mybir.dt.bfloat16
U32 = mybir.dt.uint32
I32 = mybir.dt.int32
ALU = mybir.AluOpType
AF = mybir.ActivationFunctionType
AX = mybir.AxisListType

ENGS = ("pe", "act", "dve", "pool", "sp")
D = 1024
EPS = 1e-6


class _Op:
    __slots__ = ("fn", "deps", "dma_slot", "dma_val", "signal", "sigval")

    def __init__(self, fn, deps, dma_slot=None, dma_val=0):
        self.fn = fn
        self.deps = deps
        self.dma_slot = dma_slot
        self.dma_val = dma_val
        self.signal = False
        self.sigval = 0


class Prog:
    def __init__(self, nc):
        self.nc = nc
        self.q = {e: [] for e in ENGS}
        self.lastw = {}
        self.readers = {}
        self.slot_cnt = {}
        self.slot_last = {}
        self.locklast = {}
        self.nops = 0
        self.cut = int(os.environ.get("KCUT", "0"))
        self.marks = []

    def mark(self, name):
        self.marks.append((name, self.nops))

    def op(self, eng, fn, r=(), w=(), slot=None, extra=(), force=False):
        self.nops += 1
        if self.cut and self.nops > self.cut and not force:
            return None
        q = self.q[eng]
        idx = len(q)
        me = (eng, idx)
        key = eng if slot is None else ("dma", slot)
        deps = set(extra)
        for t in r:
            for x in self.lastw.get(t, {}).values():
                deps.add(x)
        for t in w:
            for x in self.lastw.get(t, {}).values():
                deps.add(x)
            for x in self.readers.get(t, {}).values():
                deps.add(x)
        for t in list(r) + list(w):
            if t.startswith("ps"):
                prev = self.locklast.get(t)
                if prev is not None and prev[0] != eng:
                    deps.add(prev)
                self.locklast[t] = me
        dma_val = 0
        if slot is not None:
            pl = self.slot_last.get(slot)
            if pl is not None:
                deps.add(pl)
            c = self.slot_cnt.get(slot, 0) + 1
            self.slot_cnt[slot] = c
            dma_val = 16 * c
            self.slot_last[slot] = me
        for t in w:
            if slot is None:
                self.lastw[t] = {key: me}
            else:
                d = {k: v for k, v in self.lastw.get(t, {}).items() if not isinstance(k, str)}
                d[key] = me
                self.lastw[t] = d
            self.readers[t] = {}
        for t in r:
            self.readers.setdefault(t, {})[key] = me
        deps.discard(me)
        q.append(_Op(fn, deps, slot, dma_val))
        return me

    def barrier(self):
        last = [(e, len(self.q[e]) - 1) for e in ENGS if self.q[e]]
        last += list(self.slot_last.values())
        for e in ENGS:
            self.op(e, lambda h: h.nop(), extra=last)
        self.lastw = {}
        self.readers = {}
        self.locklast = {}

    def emit(self):
        nc = self.nc
        for e in ENGS:
            for i, o in enumerate(self.q[e]):
                for (e2, i2) in o.deps:
                    o2 = self.q[e2][i2]
                    if o2.dma_slot is not None:
                        continue
                    if e2 == e and e in ("pe", "sp"):
                        continue
                    o2.signal = True
        for e in ENGS:
            c = 0
            for o in self.q[e]:
                if o.signal:
                    c += 1
                    o.sigval = c
        slots = sorted(self.slot_cnt.keys(), key=str)
        with contextlib.ExitStack() as st:
            esem = {e: st.enter_context(nc.semaphore("s_" + e)) for e in ENGS}
            ssem = {s: st.enter_context(nc.semaphore("d%d" % k)) for k, s in enumerate(slots)}
            block = st.enter_context(nc.Block())
            prog = self

            def run(e, h):
                seen = {}
                for i, o in enumerate(prog.q[e]):
                    need = {}
                    for (e2, i2) in o.deps:
                        o2 = prog.q[e2][i2]
                        if o2.dma_slot is not None:
                            sem = ssem[o2.dma_slot]
                            val = o2.dma_val
                        else:
                            if e2 == e and e in ("pe", "sp"):
                                continue
                            sem = esem[e2]
                            val = o2.sigval
                        k = id(sem)
                        if seen.get(k, 0) >= val:
                            continue
                        if need.get(k, (None, 0))[1] < val:
                            need[k] = (sem, val)
                    for k, (sem, val) in need.items():
                        if os.environ.get("KSKIP") and e == "act" and i == len(prog.q[e]) - 1 and sem.name in os.environ["KSKIP"].split(","):
                            print("SKIPPING WAIT", sem.name, val)
                            continue
                        h.wait_ge(sem, val)
                        seen[k] = val
                        if os.environ.get("KDBG") and (i >= len(prog.q[e]) - 3 or os.environ.get("KDBG") == e):
                            print("WAIT", e, i, sem.name if hasattr(sem, "name") else sem, val)
                    ins = o.fn(h)
                    if o.dma_slot is not None:
                        ins.then_inc(ssem[o.dma_slot], 16)
                    elif o.signal:
                        ins.then_inc(esem[e], 1)

            @block.sync
            def _(h):
                run("sp", h)

            @block.scalar
            def _(h):
                run("act", h)

            @block.vector
            def _(h):
                run("dve", h)

            @block.gpsimd
            def _(h):
                run("pool", h)

            @block.tensor
            def _(h):
                run("pe", h)


WNAMES = ["ln_mix", "w_in", "qn_a", "kn_a", "sink_a", "qn_b", "kn_b", "go_a", "go_b", "w_out",
          "ln_x", "ln_mem", "w_cq", "w_ckv", "cqn", "ckn", "w_co", "ln_ff", "w_pq", "pk1", "pk2",
          "peer_u", "peer_v"]
WSHAPES = {"ln_mix": [D], "w_in": [D, 1536], "qn_a": [64], "kn_a": [64], "sink_a": [8], "qn_b": [64],
           "kn_b": [64], "go_a": [512], "go_b": [512], "w_out": [D, D], "ln_x": [D], "ln_mem": [D],
           "w_cq": [D, 512], "w_ckv": [D, D], "cqn": [128], "ckn": [128], "w_co": [512, D],
           "ln_ff": [D], "w_pq": [D, 2048], "pk1": [8, 128, 128], "pk2": [8, 128, 128],
           "peer_u": [16384, D], "peer_v": [16384, D]}


def build(NP, SP, SS, SQ, KBLK, LVL=4):
    nc = bass.Bass("TRN2", target_bir_lowering=False)
    NTOK = NP * SP + SS
    NT_ALL = NTOK // 128

    def din(name, shape, dt=F32):
        return nc.dram_tensor(name, list(shape), dt, kind="ExternalInput").ap()

    xp = din("xp", [NP, SP, D])
    xs = din("xs", [SS, D])
    memp = din("memp", [NP, 256, D])
    mems = din("mems", [256, D])
    tabs = din("tabs", [NTOK, 4, 64])
    cmask = din("cmask", [4, 128, 512])
    ident_d = din("ident", [128, 128])
    iota_d = din("iota", [128, 256])
    W = {n: din(n, WSHAPES[n]) for n in WNAMES}
    yp = nc.dram_tensor("yp", [NP, SP, D], F32, kind="ExternalOutput").ap()
    ys = nc.dram_tensor("ys", [SQ, D], F32, kind="ExternalOutput").ap()
    KTA_d = nc.dram_tensor("KTA_d", [128, NTOK], BF16, kind="Internal").ap()
    KTB_d = nc.dram_tensor("KTB_d", [128, NTOK], BF16, kind="Internal").ap()
    VA_d = nc.dram_tensor("VA_d", [NT_ALL, 128, 130], BF16, kind="Internal").ap()
    VB_d = nc.dram_tensor("VB_d", [NT_ALL, 128, 130], BF16, kind="Internal").ap()
    UV_d = nc.dram_tensor("UV_d", [16384, 2048], BF16, kind="Internal").ap()

    seqs = []
    for i in range(NP):
        seqs.append((xp[i], memp[i], SP, i * SP, SP // 128, yp[i], False))
    seqs.append((xs, mems, SS, NP * SP, SQ // 128, ys, True))
    NSEQ = len(seqs)

    P = Prog(nc)
    with contextlib.ExitStack() as st:
        st.enter_context(nc.allow_non_contiguous_dma(reason="tiny gain-vector / V-tile layout loads"))

        def sb(name, shape, dt):
            return st.enter_context(nc.sbuf_tensor("sb_" + name, shape, dt))

        def ps(name, shape, dt):
            return st.enter_context(nc.psum_tensor("ps_" + name, shape, dt))

        def dve(fn, r=(), w=()):
            return P.op("dve", fn, r, w)

        def act(fn, r=(), w=()):
            return P.op("act", fn, r, w)

        def pe(fn, r=(), w=()):
            return P.op("pe", fn, r, w)

        def pool(fn, r=(), w=()):
            return P.op("pool", fn, r, w)

        def dma(out, in_, r=(), w=(), slot=None, eng="sp"):
            return P.op(eng, lambda h: h.dma_start(out=out, in_=in_), r, w, slot=slot)

        w_in = sb("w_in", [128, 8, 1536], BF16)
        w_out = sb("w_out", [128, 8, 1024], BF16)
        w_cq = sb("w_cq", [128, 8, 512], BF16)
        w_co = sb("w_co", [128, 4, 1024], BF16)
        w_pq = sb("w_pq", [128, 8, 2048], BF16)
        pkT = sb("pkT", [128, 16, 128], BF16)
        gb = sb("gb", [128, 4, 1024], F32)
        w_ckv = gb[:, :, :].rearrange("p a b -> p (a b)").bitcast(BF16).rearrange("p (c n) -> p c n", c=8)
        gbb = gb[:, :, :].rearrange("p a b -> p (a b)").bitcast(BF16).rearrange("p (s n) -> p s n", s=4)
        ga = sb("ga", [128, 128], F32)
        ww = sb("ww", [128, 128], F32)
        dg = sb("dg", [128, 4, 128], BF16)
        X = sb("X", [128, 1024], F32)
        f = [sb("f%d" % i, [128, 1024], F32) for i in range(6)]
        b = [sb("b%d" % i, [128, 1024], BF16) for i in range(6)]
        cand = sb("cand", [128, 4, 256], F32)
        candw = sb("candw", [128, 4, 256], F32)
        candix = sb("candix", [128, 4, 256], F32)
        lnff = sb("lnff", [128, 1024], F32)
        KB = [sb("KB%d" % i, [128, KBLK], BF16) for i in range(2)]
        VB = [sb("VB%d" % i, [128, KBLK // 128, 130], BF16) for i in range(2)]
        KA = sb("KA", [128, 3, 128], BF16)
        VA = sb("VA", [128, 3, 130], BF16)
        memK = sb("memK", [128, 4, 256], BF16)
        memV = sb("memV", [128, 2, 4, 129], BF16)
        idb = sb("idb", [128, 128], BF16)
        idf = sb("idf", [128, 128], F32)
        masks = sb("masks", [128, 4, 512], BF16)
        gT = sb("gT", [128, 6, 8], F32)
        GQ = sb("GQ", [128, 2, 64], F32)
        GK = sb("GK", [128, 2, 64], F32)
        GCQ = sb("GCQ", [128, 1, 128], F32)
        GCK = sb("GCK", [128, 1, 128], F32)
        sinkb = sb("sinkb", [128, 8], F32)
        sinkexp = sb("sinkexp", [128, 8], F32)
        negc = sb("negc", [128, 4], F32)
        tab = sb("tab", [128, 4, 64], F32)
        sm = sb("sm", [128, 64], F32)
        sm2 = sb("sm2", [128, 64], F32)
        v1 = sb("v1", [128, 8, 16], F32)
        v2 = sb("v2", [128, 8, 16], F32)
        i1 = sb("i1", [128, 8, 16], U32)
        i2 = sb("i2", [128, 8, 16], U32)
        i1f = sb("i1f", [128, 8, 16], F32)
        i2f = sb("i2f", [128, 8, 16], F32)
        sc = sb("sc", [128, 8, 16], F32)
        ef = sb("ef", [128, 128], F32)
        icu = sb("icu", [128, 8, 16], U32)
        icf = sb("icf", [128, 8, 16], F32)
        iota = sb("iota", [128, 256], F32)
        eu = sb("eu", [128, 128], U32)
        gw = sb("gw", [128, 128], F32)
        aa = sb("aa", [128, 128], F32)
        psA = ps("psA", [128, 2, 512], F32)
        psT = ps("psT", [128, 8, 128], BF16)
        psS = [ps("psS%d" % i, [128, 2, 512], F32) for i in range(2)]
        psO = ps("psO", [128, 512], F32)

        fn_ = ["f%d" % i for i in range(6)]
        bn_ = ["b%d" % i for i in range(6)]

        dma(idf[:], ident_d, w=["idf"], slot="c0")
        dma(iota[:], iota_d, w=["iota"], slot="c1")
        dve(lambda h: h.tensor_copy(out=idb[:], in_=idf[:]), r=["idf"], w=["idb"])
        for k in range(4):
            dma(f[k][:, 0:512], cmask[k], w=[fn_[k]], slot="c%d" % (1 + k))
            dve(lambda h, k=k: h.tensor_copy(out=masks[:, k, :], in_=f[k][:, 0:512]), r=[fn_[k]], w=["masks"])
        for k, nm in enumerate(["ln_mix", "go_a", "ln_x", "ln_mem", "ln_ff"]):
            if nm == "go_a":
                dma(gT[:, 1, 0:4].unsqueeze(2), W["go_a"].rearrange("(c p o) -> p c o", p=128, o=1), w=["gT"], slot="c5")
                dma(gT[:, 1, 4:8].unsqueeze(2), W["go_b"].rearrange("(c p o) -> p c o", p=128, o=1), w=["gT"], slot="c6")
            else:
                dma(gT[:, k, :].unsqueeze(2), W[nm].rearrange("(c p o) -> p c o", p=128, o=1), w=["gT"], slot="c5")
        dma(lnff[:], W["ln_ff"].partition_broadcast(128), w=["lnff"], slot="c6")
        dma(GQ[:, 0, :], W["qn_a"].partition_broadcast(128), w=["GQ"], slot="c5")
        dma(GQ[:, 1, :], W["qn_b"].partition_broadcast(128), w=["GQ"], slot="c6")
        dma(GK[:, 0, :], W["kn_a"].partition_broadcast(128), w=["GK"], slot="c5")
        dma(GK[:, 1, :], W["kn_b"].partition_broadcast(128), w=["GK"], slot="c6")
        dma(GCQ[:, 0, :], W["cqn"].partition_broadcast(128), w=["GCQ"], slot="c5")
        dma(GCK[:, 0, :], W["ckn"].partition_broadcast(128), w=["GCK"], slot="c6")
        dma(sinkb[:], W["sink_a"].partition_broadcast(128), w=["sinkb"], slot="c5")

        def absmax(dst, src, r):
            dve(lambda h: h.tensor_reduce(out=dst, in_=src, axis=AX.X, op=ALU.max, apply_absolute_value=True),
                r=r, w=["sm"])
        absmax(sm[:, 0:1], GQ[:, 0, :], ["GQ"])
        absmax(sm[:, 1:2], GK[:, 0, :], ["GK"])
        absmax(sm[:, 2:3], GQ[:, 1, :], ["GQ"])
        absmax(sm[:, 3:4], GK[:, 1, :], ["GK"])
        absmax(sm[:, 4:5], GCQ[:, 0, :], ["GCQ"])
        absmax(sm[:, 5:6], GCK[:, 0, :], ["GCK"])
        dve(lambda h: h.tensor_reduce(out=sm[:, 6:7], in_=sinkb[:], axis=AX.X, op=ALU.max), r=["sinkb"], w=["sm"])
        dve(lambda h: h.tensor_tensor(out=sm[:, 8:9], in0=sm[:, 0:1], in1=sm[:, 1:2], op=ALU.mult), r=["sm"], w=["sm"])
        dve(lambda h: h.tensor_tensor(out=sm[:, 9:10], in0=sm[:, 2:3], in1=sm[:, 3:4], op=ALU.mult), r=["sm"], w=["sm"])
        dve(lambda h: h.tensor_tensor(out=sm[:, 10:11], in0=sm[:, 4:5], in1=sm[:, 5:6], op=ALU.mult), r=["sm"], w=["sm"])
        dve(lambda h: h.tensor_scalar(out=sm[:, 8:10], in0=sm[:, 8:10], scalar1=8.0, scalar2=None, op0=ALU.mult), r=["sm"], w=["sm"])
        dve(lambda h: h.tensor_scalar(out=sm[:, 10:11], in0=sm[:, 10:11], scalar1=float(np.sqrt(128.0)), scalar2=None, op0=ALU.mult), r=["sm"], w=["sm"])
        dve(lambda h: h.tensor_tensor(out=sm[:, 8:9], in0=sm[:, 8:9], in1=sm[:, 6:7], op=ALU.max), r=["sm"], w=["sm"])
        dve(lambda h: h.tensor_scalar(out=negc[:, 0:3], in0=sm[:, 8:11], scalar1=-1.0, scalar2=None, op0=ALU.mult), r=["sm"], w=["negc"])
        act(lambda h: h.activation(out=sinkexp[:], in_=sinkb[:], func=AF.Exp, bias=negc[:, 0:1], scale=1.0),
            r=["sinkb", "negc"], w=["sinkexp"])

        P.mark("consts")
        if LVL >= 4:
            for rblk in range(128):
                k2 = rblk % 2
                r0 = rblk * 128
                dma(f[2 * k2][:], W["peer_u"][r0:r0 + 128, :], w=[fn_[2 * k2]], slot="uvl%d" % (2 * k2))
                dma(f[2 * k2 + 1][:], W["peer_v"][r0:r0 + 128, :], w=[fn_[2 * k2 + 1]], slot="uvl%d" % (2 * k2 + 1))
                dve(lambda h, k2=k2: h.tensor_copy(out=gbb[:, k2, 0:1024], in_=f[2 * k2][:]), r=[fn_[2 * k2]], w=["gb%d" % k2])
                act(lambda h, k2=k2: h.activation(out=gbb[:, k2, 1024:2048], in_=f[2 * k2 + 1][:], func=AF.Copy), r=[fn_[2 * k2 + 1]], w=["gb%d" % k2])
                dma(UV_d[r0:r0 + 128, :], gbb[:, k2, :], r=["gb%d" % k2], slot="uvs%d" % k2)
            P.barrier()
        stg_i = [0]

        def load_w(dst, src, K, N, grow):
            for c in range(K // 128):
                for n0 in range(0, N, 1024):
                    nn = min(1024, N - n0)
                    k = stg_i[0] % 2
                    stg_i[0] += 1
                    dma(f[k][:, 0:nn], src[c * 128:(c + 1) * 128, n0:n0 + nn], w=[fn_[k]], slot="stg%d" % k)
                    if grow is None:
                        dve(lambda h, k=k, c=c, n0=n0, nn=nn: h.tensor_copy(out=dst[:, c, n0:n0 + nn], in_=f[k][:, 0:nn]),
                            r=[fn_[k]], w=["W"])
                    else:
                        dve(lambda h, k=k, c=c, n0=n0, nn=nn: h.tensor_scalar(
                            out=dst[:, c, n0:n0 + nn], in0=f[k][:, 0:nn], scalar1=gT[:, grow, c:c + 1], scalar2=None, op0=ALU.mult),
                            r=[fn_[k], "gT"], w=["W"])
        load_w(w_in, W["w_in"], 1024, 1536, 0)
        load_w(w_out, W["w_out"], 1024, 1024, 1)
        load_w(w_cq, W["w_cq"], 1024, 512, 2)
        load_w(w_ckv, W["w_ckv"], 1024, 1024, 3)
        load_w(w_co, W["w_co"], 512, 1024, None)
        load_w(w_pq, W["w_pq"], 1024, 2048, 4)
        for side, nm in enumerate(["pk1", "pk2"]):
            dma(f[2 + side][:, :].rearrange("p (h d) -> p h d", h=8), W[nm].rearrange("h n d -> n h d"),
                w=[fn_[2 + side]], slot="c%d" % (1 + side))
            dve(lambda h, side=side: h.tensor_copy(out=b[side][:, :], in_=f[2 + side][:, :]), r=[fn_[2 + side]], w=[bn_[side]])
            for hh in range(8):
                pe(lambda h, side=side, hh=hh: h.transpose(out=psT[:, hh, :], in_=b[side][:, hh * 128:(hh + 1) * 128], identity=idb[:]),
                   r=[bn_[side], "idb"], w=["psT"])
            act(lambda h, side=side: h.activation(out=pkT[:, :, :].rearrange("p (h s) k -> p s h k", s=2)[:, side, :, :],
                                                   in_=psT[:, :, :], func=AF.Copy), r=["psT"], w=["W"])

        P.mark("weights")
        def rstd_from_ss(dst, src, n, rtag, wtag):
            act(lambda h: h.activation(out=dst, in_=src, func=AF.Ln, scale=1.0 / n, bias=EPS), r=[rtag], w=[wtag])
            act(lambda h: h.activation(out=dst, in_=dst, func=AF.Ln if False else AF.Exp, scale=-0.5), r=[wtag], w=[wtag])

        def norm_to_hT(src, srctag, extra_f32=None):
            act(lambda h: h.activation(out=f[0][:], in_=src, func=AF.Square, accum_out=sm[:, 16:17]), r=[srctag], w=["f0", "sm"])
            rstd_from_ss(sm[:, 17:18], sm[:, 16:17], 1024.0, "sm", "sm")
            dve(lambda h: h.tensor_scalar(out=b[0][:], in0=src, scalar1=sm[:, 17:18], scalar2=None, op0=ALU.mult),
                r=[srctag, "sm"], w=["b0"])
            if extra_f32 is not None:
                dve(lambda h: h.scalar_tensor_tensor(out=extra_f32, in0=src, scalar=sm[:, 17:18], in1=lnff[:],
                                                     op0=ALU.mult, op1=ALU.mult), r=[srctag, "sm", "lnff"], w=["f1"])
            for c in range(8):
                pe(lambda h, c=c: h.transpose(out=psT[:, c, :], in_=b[0][:, c * 128:(c + 1) * 128], identity=idb[:]),
                   r=["b0", "idb"], w=["psT"])
            act(lambda h: h.activation(out=b[1][:, :], in_=psT[:, :, :].rearrange("p a b -> p (a b)"), func=AF.Copy), r=["psT"], w=["b1"])

        hT = b[1][:, :].rearrange("p (c t) -> p c t", c=8)

        def proj(dst_ps, wt, nK, n0, nn, dtag):
            for c in range(nK):
                pe(lambda h, c=c: h.matmul(dst_ps, lhsT=hT[:, c, :], rhs=wt[:, c, n0:n0 + nn], start=(c == 0), stop=(c == nK - 1)),
                   r=["b1", "W"], w=[dtag])

        def headnorm_rope(zsrc, ztag, NH, G, Gtag, tsel, dst, dtag, perm):
            n = NH * 64
            half = 32 if tsel == 0 else 16
            nb = 64 // (2 * half)
            act(lambda h: h.activation(out=f[0][:, 0:n], in_=zsrc, func=AF.Square), r=[ztag], w=["f0"])
            dve(lambda h: h.tensor_reduce(out=sm[:, 20:20 + NH], in_=f[0][:, 0:n].rearrange("p (h d) -> p h d", d=64), axis=AX.X, op=ALU.add),
                r=["f0"], w=["sm"])
            rstd_from_ss(sm[:, 20:20 + NH], sm[:, 20:20 + NH], 64.0, "sm", "sm")
            dve(lambda h: h.tensor_tensor(out=f[1][:, 0:n].rearrange("p (h d) -> p h d", d=64), in0=zsrc.rearrange("p (h d) -> p h d", d=64),
                                          in1=sm[:, 20:20 + NH].unsqueeze(2).to_broadcast([128, NH, 64]), op=ALU.mult),
                r=[ztag, "sm"], w=["f1"])
            dve(lambda h: h.tensor_tensor(out=f[1][:, 0:n].rearrange("p (h d) -> p h d", d=64), in0=f[1][:, 0:n].rearrange("p (h d) -> p h d", d=64),
                                          in1=G.unsqueeze(1).to_broadcast([128, NH, 64]), op=ALU.mult), r=["f1", Gtag], w=["f1"])
            y4 = f[1][:, 0:n].rearrange("p (h b t f) -> p h b t f", h=NH, b=nb, t=2)
            t24 = f[3][:, 0:n].rearrange("p (h b t f) -> p h b t f", h=NH, b=nb, t=2)
            C3 = tab[:, tsel, :].unsqueeze(1).to_broadcast([128, NH, 64])
            S4 = tab[:, tsel + 1, :].rearrange("p (b t f) -> p b t f", b=nb, t=2)
            dve(lambda h: h.tensor_tensor(out=f[2][:, 0:n].rearrange("p (h d) -> p h d", d=64), in0=f[1][:, 0:n].rearrange("p (h d) -> p h d", d=64),
                                          in1=C3, op=ALU.mult), r=["f1", "tab"], w=["f2"])
            for t in range(2):
                dve(lambda h, t=t: h.tensor_tensor(out=t24[:, :, :, t, :], in0=y4[:, :, :, 1 - t, :],
                                                   in1=S4[:, :, t, :].unsqueeze(1).to_broadcast([128, NH, nb, half]), op=ALU.mult),
                    r=["f1", "tab"], w=["f3"])
            if perm:
                o3 = dst.rearrange("p (g k d) -> p k g d", g=4, k=2)
                a3 = f[2][:, 0:n].rearrange("p (k g d) -> p k g d", k=2, g=4)
                c3 = f[3][:, 0:n].rearrange("p (k g d) -> p k g d", k=2, g=4)
                for kv in range(2):
                    dve(lambda h, kv=kv: h.tensor_tensor(out=o3[:, kv], in0=a3[:, kv], in1=c3[:, kv], op=ALU.add), r=["f2", "f3"], w=[dtag])
            else:
                dve(lambda h: h.tensor_tensor(out=dst, in0=f[2][:, 0:n], in1=f[3][:, 0:n], op=ALU.add), r=["f2", "f3"], w=[dtag])

        P.barrier()
        for si, (xap, memap, S, tok0, nq, oap, is_s) in enumerate(seqs):
            for ti in range(S // 128):
                t0 = tok0 + ti * 128
                gt = t0 // 128
                P.mark("pro_tile_%d_%d" % (si, ti))
                dma(X[:], xap[ti * 128:(ti + 1) * 128, :], w=["X"], slot="x")
                dma(tab[:], tabs[t0:t0 + 128], w=["tab"], slot="tab")
                norm_to_hT(X[:], "X")
                P.mark("pro_normT")
                proj(psA[:, 0, 0:256], w_in, 8, 512, 256, "psA")
                proj(psA[:, 1, 0:256], w_in, 8, 1280, 256, "psA")
                for ty in range(2):
                    z = psA[:, ty, 0:128]
                    headnorm_rope(z, "psA", 2, GK[:, ty, :], "GK", 2 * ty,
                                  b[2][:, ty * 128:(ty + 1) * 128], "b2", False)
                    pe(lambda h, ty=ty: h.transpose(out=psT[:, ty, :], in_=b[2][:, ty * 128:(ty + 1) * 128], identity=idb[:]),
                       r=["b2", "idb"], w=["psT"])
                    vdst = b[3][:, ty * 130:(ty + 1) * 130].rearrange("p (k c) -> p k c", k=2)
                    act(lambda h, ty=ty, vdst=vdst: h.activation(out=vdst[:, :, 0:64], in_=psA[:, ty, 128:256].rearrange("p (k c) -> p k c", k=2),
                                                                  func=AF.Copy), r=["psA"], w=["b3"])
                    pool(lambda h, vdst=vdst: h.memset(vdst[:, :, 64:65], 1.0), w=["b3"])
                act(lambda h: h.activation(out=b[4][:, 0:256], in_=psT[:, 0:2, :].rearrange("p a b -> p (a b)"), func=AF.Copy), r=["psT"], w=["b4"])
                dma(KTA_d[:, t0:t0 + 128], b[4][:, 0:128], r=["b4"], slot="st0")
                dma(KTB_d[:, t0:t0 + 128], b[4][:, 128:256], r=["b4"], slot="st1")
                dma(VA_d[gt], b[3][:, 0:130], r=["b3"], slot="st2")
                dma(VB_d[gt], b[3][:, 130:260], r=["b3"], slot="st3")
        P.barrier()

        def mem_kv(memap):
            for mt in range(2):
                dma(X[:], memap[mt * 128:(mt + 1) * 128, :], w=["X"], slot="x")
                norm_to_hT(X[:], "X")
                proj(psA[:, 0, :], w_ckv, 8, 0, 512, "psA")
                proj(psA[:, 1, :], w_ckv, 8, 512, 512, "psA")
                act(lambda h: h.activation(out=f[0][:, 0:512], in_=psA[:, 0, :], func=AF.Square), r=["psA"], w=["f0"])
                dve(lambda h: h.tensor_reduce(out=sm[:, 20:24], in_=f[0][:, 0:512].rearrange("p (h d) -> p h d", d=128), axis=AX.X, op=ALU.add),
                    r=["f0"], w=["sm"])
                rstd_from_ss(sm[:, 20:24], sm[:, 20:24], 128.0, "sm", "sm")
                dve(lambda h: h.tensor_tensor(out=f[1][:, 0:512].rearrange("p (h d) -> p h d", d=128), in0=psA[:, 0, :].rearrange("p (h d) -> p h d", d=128),
                                              in1=sm[:, 20:24].unsqueeze(2).to_broadcast([128, 4, 128]), op=ALU.mult), r=["psA", "sm"], w=["f1"])
                dve(lambda h: h.tensor_tensor(out=b[2][:, 0:512].rearrange("p (h d) -> p h d", d=128), in0=f[1][:, 0:512].rearrange("p (h d) -> p h d", d=128),
                                              in1=GCK[:, :, :].to_broadcast([128, 4, 128]), op=ALU.mult),
                    r=["f1", "GCK"], w=["b2"])
                for hh in range(4):
                    pe(lambda h, hh=hh: h.transpose(out=psT[:, hh, :], in_=b[2][:, hh * 128:(hh + 1) * 128], identity=idb[:]),
                       r=["b2", "idb"], w=["psT"])
                act(lambda h, mt=mt: h.activation(out=memK[:, :, mt * 128:(mt + 1) * 128], in_=psT[:, 0:4, :], func=AF.Copy), r=["psT"], w=["memK"])
                act(lambda h, mt=mt: h.activation(out=memV[:, mt, :, 0:128], in_=psA[:, 1, :].rearrange("p (h d) -> p h d", d=128), func=AF.Copy),
                    r=["psA"], w=["memV"])
                pool(lambda h, mt=mt: h.memset(memV[:, mt, :, 128:129], 1.0), w=["memV"])

        st_par = [0]
        for si, (xap, memap, S, tok0, nq, oap, is_s) in enumerate(seqs):
            NT = S // 128
            KT = KBLK // 128
            nblk = S // KBLK
            P.mark("main_seq_%d" % si)
            P.barrier()
            load_w(w_ckv, W["w_ckv"], 1024, 1024, 3) if si > 0 else None
            mem_kv(memap)
            P.barrier()
            for qi in range(nq):
                t0 = tok0 + qi * 128
                dma(X[:], xap[qi * 128:(qi + 1) * 128, :], w=["X"], slot="x")
                dma(tab[:], tabs[t0:t0 + 128], w=["tab"], slot="tab")
                P.mark("main_tile_%d_%d" % (si, qi))
                norm_to_hT(X[:], "X")
                proj(psA[:, 0, :], w_in, 8, 0, 512, "psA")
                proj(psA[:, 1, :], w_in, 8, 768, 512, "psA")
                for ty in range(2):
                    headnorm_rope(psA[:, ty, :], "psA", 8, GQ[:, ty, :], "GQ", 2 * ty,
                                  b[2][:, ty * 512:(ty + 1) * 512], "b2", True)
                for c in range(8):
                    pe(lambda h, c=c: h.transpose(out=psT[:, c, :], in_=b[2][:, c * 128:(c + 1) * 128], identity=idb[:]),
                       r=["b2", "idb"], w=["psT"])
                act(lambda h: h.activation(out=b[3][:, :], in_=psT[:, :, :].rearrange("p a b -> p (a b)"), func=AF.Copy), r=["psT"], w=["b3"])
                QT = b[3]
                P.mark("QT")

                def attn_units(units, accs, tsel):
                    cnt = len(units)
                    started = {}
                    def scores(u, bi):
                        kv, tiles, pre = units[u]
                        for jj, (kfn, vfn, msel, ktag, vtag) in enumerate(tiles):
                            pe(lambda h, kv=kv, kfn=kfn, jj=jj, bi=bi: h.matmul(
                                psS[bi][:, jj, :], lhsT=kfn(kv), rhs=QT[kv * 64:(kv + 1) * 64, tsel * 512:(tsel + 1) * 512], start=True, stop=True),
                               r=["b3", ktag], w=["psS%d" % bi])
                        nt = len(tiles)
                        act(lambda h, bi=bi, nt=nt: h.activation(out=b[4 + bi][:, 0:nt * 512], in_=psS[bi][:, 0:nt, :].rearrange("p a b -> p (a b)"),
                                                                  func=AF.Exp, bias=negc[:, tsel:tsel + 1], scale=0.125),
                            r=["psS%d" % bi, "negc"], w=[bn_[4 + bi]])
                        for jj, (kfn, vfn, msel, ktag, vtag) in enumerate(tiles):
                            if msel is not None:
                                pool(lambda h, bi=bi, jj=jj, msel=msel: h.tensor_tensor(out=b[4 + bi][:, jj * 512:(jj + 1) * 512], in0=b[4 + bi][:, jj * 512:(jj + 1) * 512],
                                                                                        in1=masks[:, msel, :], op=ALU.mult),
                                     r=[bn_[4 + bi], "masks"], w=[bn_[4 + bi]])
                    def pv(u, bi):
                        kv, tiles, pre = units[u]
                        acc, atag = accs[kv]
                        for jj, (kfn, vfn, msel, ktag, vtag) in enumerate(tiles):
                            first = kv not in started
                            started[kv] = True
                            last = all(units[u2][0] != kv for u2 in range(u + 1, cnt)) and jj == len(tiles) - 1
                            pe(lambda h, kv=kv, vfn=vfn, jj=jj, bi=bi, first=first, last=last, acc=acc: h.matmul(
                                acc, lhsT=vfn(kv), rhs=b[4 + bi][:, jj * 512:(jj + 1) * 512], start=first, stop=last),
                               r=[bn_[4 + bi], vtag], w=[atag])
                    for u in range(cnt):
                        scores(u, u % 2)
                        if u > 0:
                            pv(u - 1, (u - 1) % 2)
                        if units[u][2] is not None:
                            units[u][2]()
                    pv(cnt - 1, (cnt - 1) % 2)

                def finish_attn(accs, ocol, with_sink):
                    for kv in range(2):
                        acc, atag = accs[kv]
                        act(lambda h, kv=kv, acc=acc: h.activation(out=f[4][0:65, kv * 512:(kv + 1) * 512], in_=acc, func=AF.Copy), r=[atag], w=["f4"])
                    for kv in range(2):
                        for g in range(4):
                            pe(lambda h, kv=kv, g=g: h.transpose(out=psA[:, kv, g * 65:(g + 1) * 65], in_=f[4][0:65, kv * 512 + g * 128:kv * 512 + (g + 1) * 128],
                                                                  identity=idf[0:65, 0:65]), r=["f4", "idf"], w=["psA"])
                    pv4 = psA[:, :, 0:260].rearrange("p k (g c) -> p k g c", g=4)
                    dve(lambda h: h.tensor_copy(out=sm2[:, 0:8].rearrange("p (k g) -> p k g", k=2), in_=pv4[:, :, :, 64]), r=["psA"], w=["sm2"])
                    if with_sink:
                        dve(lambda h: h.tensor_tensor(out=sm2[:, 0:8], in0=sm2[:, 0:8], in1=sinkexp[:], op=ALU.add), r=["sm2", "sinkexp"], w=["sm2"])
                    dve(lambda h: h.reciprocal(out=sm2[:, 8:16], in_=sm2[:, 0:8]), r=["sm2"], w=["sm2"])
                    for kv in range(2):
                        dve(lambda h, kv=kv: h.tensor_tensor(out=f[5][:, ocol + kv * 256:ocol + (kv + 1) * 256].rearrange("p (g d) -> p g d", g=4),
                                                             in0=pv4[:, kv, :, 0:64],
                                                             in1=sm2[:, 8 + kv * 4:12 + kv * 4].unsqueeze(2).to_broadcast([128, 4, 64]), op=ALU.mult),
                            r=["psA", "sm2"], w=["f5"])

                if is_s:
                    jl = [((qi - 1) % NT, 2 if qi == 0 else 0), (qi, None), ((qi + 1) % NT, 3 if qi == nq - 1 else 1)]
                else:
                    jl = []
                    if qi > 0:
                        jl.append((qi - 1, 0))
                    jl.append((qi, None))
                    if qi < NT - 1:
                        jl.append((qi + 1, 1))
                for jj, (kt, msel) in enumerate(jl):
                    kt0 = tok0 + kt * 128
                    dma(KA[:, jj, :], KTA_d[:, kt0:kt0 + 128], w=["KA%d" % jj], slot="ka%d" % jj)
                    dma(VA[:, jj, :], VA_d[kt0 // 128], w=["VA%d" % jj], slot="va%d" % jj)
                unitsA = []
                for kv in range(2):
                    tl = [((lambda kv, jj=jj: KA[kv * 64:(kv + 1) * 64, jj, :]), (lambda kv, jj=jj: VA[:, jj, kv * 65:(kv + 1) * 65]), msel,
                           "KA%d" % jj, "VA%d" % jj) for jj, (kt, msel) in enumerate(jl)]
                    unitsA.append((kv, tl[0:2], None))
                    if len(tl) > 2:
                        unitsA.append((kv, tl[2:3], None))
                accs = {0: (psO[0:65, :], "psO"), 1: (psA[0:65, 0, :], "psA")}
                attn_units(unitsA, accs, 0)
                P.mark("attnA_units")
                finish_attn(accs, 0, True)
                P.mark("attnA_fin")
                def load_blk(blk):
                    bb = blk % 2
                    k0 = tok0 + blk * KBLK
                    dma(KB[bb][:, :], KTB_d[:, k0:k0 + KBLK], w=["KB%d" % bb], slot="kb%d" % bb)
                    dma(VB[bb][:, :, :], VB_d[k0 // 128:k0 // 128 + KT].rearrange("j p c -> p j c"), w=["VB%d" % bb], slot="vb%d" % bb)
                load_blk(0)
                unitsB = []
                for blk in range(nblk):
                    bb = blk % 2
                    firstu = True
                    for kv in range(2):
                        for j0 in range(0, KT, 2):
                            tl = [((lambda kv, bb=bb, j=j: KB[bb][kv * 64:(kv + 1) * 64, j * 128:(j + 1) * 128]),
                                   (lambda kv, bb=bb, j=j: VB[bb][:, j, kv * 65:(kv + 1) * 65]), None, "KB%d" % bb, "VB%d" % bb)
                                  for j in range(j0, min(j0 + 2, KT))]
                            pre = None
                            if firstu and blk + 1 < nblk:
                                pre = (lambda blk=blk: load_blk(blk + 1))
                            firstu = False
                            unitsB.append((kv, tl, pre))
                attn_units(unitsB, accs, 1)
                finish_attn(accs, 512, False)
                P.mark("attnB_fin")

                act(lambda h: h.activation(out=f[0][:, 0:512], in_=f[5][:, 0:512], func=AF.Square, accum_out=sm[:, 30:31]), r=["f5"], w=["f0", "sm"])
                act(lambda h: h.activation(out=f[0][:, 512:1024], in_=f[5][:, 512:1024], func=AF.Square, accum_out=sm[:, 31:32]), r=["f5"], w=["f0", "sm"])
                rstd_from_ss(sm[:, 30:32], sm[:, 30:32], 512.0, "sm", "sm")
                dve(lambda h: h.tensor_tensor(out=b[0][:, :].rearrange("p (a d) -> p a d", a=2), in0=f[5][:, :].rearrange("p (a d) -> p a d", a=2),
                                              in1=sm[:, 30:32].unsqueeze(2).to_broadcast([128, 2, 512]), op=ALU.mult), r=["f5", "sm"], w=["b0"])
                for c in range(8):
                    pe(lambda h, c=c: h.transpose(out=psT[:, c, :], in_=b[0][:, c * 128:(c + 1) * 128], identity=idb[:]),
                       r=["b0", "idb"], w=["psT"])
                act(lambda h: h.activation(out=b[1][:, :], in_=psT[:, :, :].rearrange("p a b -> p (a b)"), func=AF.Copy), r=["psT"], w=["b1"])
                proj(psA[:, 0, :], w_out, 8, 0, 512, "psA")
                proj(psA[:, 1, :], w_out, 8, 512, 512, "psA")
                dve(lambda h: h.tensor_tensor(out=X[:], in0=X[:], in1=psA[:, :, :].rearrange("p a b -> p (a b)"), op=ALU.add), r=["X", "psA"], w=["X"])

                for _lvl2 in ([0] if LVL >= 2 else []):
                    norm_to_hT(X[:], "X")
                    proj(psA[:, 0, :], w_cq, 8, 0, 512, "psA")
                    act(lambda h: h.activation(out=f[0][:, 0:512], in_=psA[:, 0, :], func=AF.Square), r=["psA"], w=["f0"])
                    dve(lambda h: h.tensor_reduce(out=sm[:, 20:24], in_=f[0][:, 0:512].rearrange("p (h d) -> p h d", d=128), axis=AX.X, op=ALU.add),
                        r=["f0"], w=["sm"])
                    rstd_from_ss(sm[:, 20:24], sm[:, 20:24], 128.0, "sm", "sm")
                    dve(lambda h: h.tensor_tensor(out=f[1][:, 0:512].rearrange("p (h d) -> p h d", d=128), in0=psA[:, 0, :].rearrange("p (h d) -> p h d", d=128),
                                                  in1=sm[:, 20:24].unsqueeze(2).to_broadcast([128, 4, 128]), op=ALU.mult), r=["psA", "sm"], w=["f1"])
                    dve(lambda h: h.tensor_tensor(out=b[2][:, 0:512].rearrange("p (h d) -> p h d", d=128), in0=f[1][:, 0:512].rearrange("p (h d) -> p h d", d=128),
                                                  in1=GCQ[:, :, :].to_broadcast([128, 4, 128]), op=ALU.mult),
                        r=["f1", "GCQ"], w=["b2"])
                    for hh in range(4):
                        pe(lambda h, hh=hh: h.transpose(out=psT[:, hh, :], in_=b[2][:, hh * 128:(hh + 1) * 128], identity=idb[:]),
                           r=["b2", "idb"], w=["psT"])
                    act(lambda h: h.activation(out=b[3][:, 0:512], in_=psT[:, 0:4, :].rearrange("p a b -> p (a b)"), func=AF.Copy), r=["psT"], w=["b3"])
                    for mt in range(2):
                        for hh in range(4):
                            pe(lambda h, mt=mt, hh=hh: h.matmul(psS[0][:, mt, hh * 128:(hh + 1) * 128], lhsT=memK[:, hh, mt * 128:(mt + 1) * 128],
                                                                rhs=b[3][:, hh * 128:(hh + 1) * 128], start=True, stop=True),
                               r=["memK", "b3"], w=["psS0"])
                    act(lambda h: h.activation(out=b[4][:, :], in_=psS[0][:, :, :].rearrange("p a b -> p (a b)"), func=AF.Exp,
                                               bias=negc[:, 2:3], scale=float(128.0 ** -0.5)), r=["psS0", "negc"], w=["b4"])
                    for hh in range(4):
                        for mt in range(2):
                            pe(lambda h, mt=mt, hh=hh: h.matmul(psA[:, hh // 2, (hh % 2) * 129:(hh % 2 + 1) * 129],
                                                                lhsT=b[4][:, mt * 512 + hh * 128:mt * 512 + (hh + 1) * 128],
                                                                rhs=memV[:, mt, hh, :], start=(mt == 0), stop=(mt == 1)),
                               r=["b4", "memV"], w=["psA"])
                    px = psA[:, :, 0:258].rearrange("p a (h c) -> p a h c", h=2)
                    dve(lambda h: h.reciprocal(out=sm2[:, 16:20].rearrange("p (a h) -> p a h", a=2), in_=px[:, :, :, 128]), r=["psA"], w=["sm2"])
                    for a in range(2):
                        dve(lambda h, a=a: h.tensor_tensor(out=b[0][:, a * 256:(a + 1) * 256].rearrange("p (h d) -> p h d", h=2), in0=px[:, a, :, 0:128],
                                                           in1=sm2[:, 16 + 2 * a:18 + 2 * a].unsqueeze(2).to_broadcast([128, 2, 128]), op=ALU.mult),
                            r=["psA", "sm2"], w=["b0"])
                    for c in range(4):
                        pe(lambda h, c=c: h.transpose(out=psT[:, c, :], in_=b[0][:, c * 128:(c + 1) * 128], identity=idb[:]),
                           r=["b0", "idb"], w=["psT"])
                    act(lambda h: h.activation(out=b[1][:, 0:512], in_=psT[:, 0:4, :].rearrange("p a b -> p (a b)"), func=AF.Copy), r=["psT"], w=["b1"])
                    proj(psA[:, 0, :], w_co, 4, 0, 512, "psA")
                    proj(psA[:, 1, :], w_co, 4, 512, 512, "psA")
                    dve(lambda h: h.tensor_tensor(out=X[:], in0=X[:], in1=psA[:, :, :].rearrange("p a b -> p (a b)"), op=ALU.add), r=["X", "psA"], w=["X"])

                for _lvl3 in ([0] if LVL >= 3 else []):
                    norm_to_hT(X[:], "X", extra_f32=f[1][:])
                    for half in range(2):
                        for nb in range(8):
                            for c in range(8):
                                pe(lambda h, half=half, nb=nb, c=c: h.matmul(psA[:, nb // 4, (nb % 4) * 128:(nb % 4 + 1) * 128],
                                                                              lhsT=w_pq[:, c, (half * 8 + nb) * 128:(half * 8 + nb + 1) * 128],
                                                                              rhs=hT[:, c, :], start=(c == 0), stop=(c == 7)),
                                   r=["W", "b1"], w=["psA"])
                        act(lambda h, half=half: h.activation(out=b[2 + half][:, :], in_=psA[:, :, :].rearrange("p a b -> p (a b)"), func=AF.Copy),
                            r=["psA"], w=[bn_[2 + half]])
                    for hh in range(8):
                        for side in range(2):
                            blk = 2 * hh + side
                            src = b[2 + blk // 8][:, (blk % 8) * 128:(blk % 8 + 1) * 128]
                            pe(lambda h, hh=hh, side=side, src=src: h.matmul(psS[side][:, hh // 4, (hh % 4) * 128:(hh % 4 + 1) * 128], lhsT=src,
                                                                             rhs=pkT[:, 2 * hh + side, :], start=True, stop=True),
                               r=["b2", "b3", "W"], w=["psS%d" % side])
                    for side in range(2):
                        act(lambda h, side=side: h.activation(out=f[2 + side][:, :], in_=psS[side][:, :, :].rearrange("p a b -> p (a b)"), func=AF.Copy),
                            r=["psS%d" % side], w=[fn_[2 + side]])
                    for side in range(2):
                        S_ = f[2 + side]
                        Wk = f[4 + side]
                        vv = v1 if side == 0 else v2
                        ii = i1 if side == 0 else i2
                        vt = "v%d" % side
                        for hh in range(8):
                            sl = slice(hh * 128, (hh + 1) * 128)
                            dve(lambda h, S_=S_, vv=vv, hh=hh, sl=sl: h.max(out=vv[:, hh, 0:8], in_=S_[:, sl]), r=[fn_[2 + side]], w=[vt])
                            dve(lambda h, S_=S_, Wk=Wk, vv=vv, hh=hh, sl=sl: h.match_replace(out=Wk[:, sl], in_to_replace=vv[:, hh, 0:8], in_values=S_[:, sl], imm_value=-1e30),
                                r=[fn_[2 + side], vt], w=[fn_[4 + side]])
                            dve(lambda h, Wk=Wk, vv=vv, hh=hh, sl=sl: h.max(out=vv[:, hh, 8:16], in_=Wk[:, sl]), r=[fn_[4 + side]], w=[vt])
                            dve(lambda h, S_=S_, vv=vv, ii=ii, hh=hh, sl=sl: h.max_index(out=ii[:, hh, 0:8], in_max=vv[:, hh, 0:8], in_values=S_[:, sl]),
                                r=[fn_[2 + side], vt], w=["i%d" % side])
                            dve(lambda h, Wk=Wk, vv=vv, ii=ii, hh=hh, sl=sl: h.max_index(out=ii[:, hh, 8:16], in_max=vv[:, hh, 8:16], in_values=Wk[:, sl]),
                                r=[fn_[4 + side], vt], w=["i%d" % side])
                    dve(lambda h: h.tensor_scalar(out=i1f[:], in0=i1[:], scalar1=128.0, scalar2=None, op0=ALU.mult), r=["i0"], w=["i1f"])
                    dve(lambda h: h.tensor_copy(out=i2f[:], in_=i2[:]), r=["i1"], w=["i2f"])
                    for hg in range(2):
                        hs = slice(hg * 4, hg * 4 + 4)
                        dve(lambda h, hs=hs: h.tensor_tensor(out=cand[:, :, :].rearrange("p h (a b) -> p h a b", a=16),
                                                             in0=v1[:, hs, :].unsqueeze(3).to_broadcast([128, 4, 16, 16]),
                                                             in1=v2[:, hs, :].unsqueeze(2).to_broadcast([128, 4, 16, 16]), op=ALU.add),
                            r=["v0", "v1"], w=["cand"])
                        dve(lambda h, hs=hs: h.tensor_tensor(out=candix[:, :, :].rearrange("p h (a b) -> p h a b", a=16),
                                                             in0=i1f[:, hs, :].unsqueeze(3).to_broadcast([128, 4, 16, 16]),
                                                             in1=i2f[:, hs, :].unsqueeze(2).to_broadcast([128, 4, 16, 16]), op=ALU.add),
                            r=["i1f", "i2f"], w=["candix"])
                        for h4 in range(4):
                            hh = hg * 4 + h4
                            dve(lambda h, h4=h4, hh=hh: h.max(out=sc[:, hh, 0:8], in_=cand[:, h4, :]), r=["cand"], w=["sc"])
                            dve(lambda h, h4=h4, hh=hh: h.match_replace(out=candw[:, h4, :], in_to_replace=sc[:, hh, 0:8], in_values=cand[:, h4, :], imm_value=-1e30),
                                r=["cand", "sc"], w=["candw"])
                            dve(lambda h, h4=h4, hh=hh: h.max(out=sc[:, hh, 8:16], in_=candw[:, h4, :]), r=["candw"], w=["sc"])
                            dve(lambda h, h4=h4, hh=hh: h.max_index(out=icu[:, hh, 0:8], in_max=sc[:, hh, 0:8], in_values=cand[:, h4, :]), r=["cand", "sc"], w=["icu"])
                            dve(lambda h, h4=h4, hh=hh: h.max_index(out=icu[:, hh, 8:16], in_max=sc[:, hh, 8:16], in_values=candw[:, h4, :]), r=["candw", "sc"], w=["icu"])
                            dve(lambda h, hh=hh: h.tensor_copy(out=icf[:, hh, :], in_=icu[:, hh, :]), r=["icu"], w=["icf"])
                            for k in range(16):
                                dve(lambda h, h4=h4, hh=hh, k=k: h.scalar_tensor_tensor(out=f[0][:, 0:256], in0=iota[:, :], scalar=icf[:, hh, k:k + 1],
                                                                                        in1=candix[:, h4, :], op0=ALU.is_equal, op1=ALU.mult,
                                                                                        accum_out=ef[:, hh * 16 + k:hh * 16 + k + 1]),
                                    r=["iota", "icf", "candix"], w=["f0", "ef"])
                    dve(lambda h: h.tensor_copy(out=eu[:], in_=ef[:]), r=["ef"], w=["eu"])
                    dve(lambda h: h.tensor_tensor(out=gw[:, :].rearrange("p (h k) -> p h k", h=8), in0=sc[:, :, :],
                                                  in1=sc[:, :, 0:1].to_broadcast([128, 8, 16]), op=ALU.subtract), r=["sc"], w=["gw"])
                    act(lambda h: h.activation(out=gw[:], in_=gw[:], func=AF.Exp), r=["gw"], w=["gw"])
                    dve(lambda h: h.tensor_reduce(out=sm2[:, 24:32], in_=gw[:, :].rearrange("p (h k) -> p h k", h=8), axis=AX.X, op=ALU.add), r=["gw"], w=["sm2"])
                    dve(lambda h: h.reciprocal(out=sm2[:, 32:40], in_=sm2[:, 24:32]), r=["sm2"], w=["sm2"])
                    dve(lambda h: h.tensor_tensor(out=gw[:, :].rearrange("p (h k) -> p h k", h=8), in0=gw[:, :].rearrange("p (h k) -> p h k", h=8),
                                                  in1=sm2[:, 32:40].unsqueeze(2).to_broadcast([128, 8, 16]), op=ALU.mult), r=["gw", "sm2"], w=["gw"])
                    NJ = 128 if LVL >= 4 else 0
                    for j in range(NJ):
                        s4 = j % 4
                        P.op("pool", lambda h, j=j, s4=s4: h.indirect_dma_start(out=gbb[:, s4, :], out_offset=None, in_=UV_d,
                                                                              in_offset=bass.IndirectOffsetOnAxis(ap=eu[:, j:j + 1], axis=0)),
                             r=["eu"], w=["gb%d" % s4], slot="gu%d" % s4)
                        dve(lambda h, j=j, s4=s4: h.scalar_tensor_tensor(out=f[0][:], in0=gbb[:, s4, 0:1024], scalar=1.0, in1=f[1][:], op0=ALU.mult, op1=ALU.mult,
                                                                       accum_out=aa[:, j:j + 1]), r=["gb%d" % s4, "f1"], w=["f0", "aa%d" % s4])
                        act(lambda h, j=j: h.activation(out=ga[:, j:j + 1], in_=aa[:, j:j + 1], func=AF.Gelu), r=["aa%d" % s4], w=["ga%d" % s4])
                        act(lambda h, j=j: h.activation(out=ww[:, j:j + 1], in_=ga[:, j:j + 1], func=AF.Copy, scale=gw[:, j:j + 1]),
                            r=["ga%d" % s4, "gw"], w=["ww%d" % s4])
                        act(lambda h, j=j, s4=s4: h.activation(out=dg[:, s4, :], in_=idb[:], func=AF.Copy, scale=ww[:, j:j + 1]),
                            r=["ww%d" % s4, "idb"], w=["dg%d" % s4])
                        for hf in range(2):
                            pe(lambda h, j=j, s4=s4, hf=hf: h.matmul(psA[:, hf, :], lhsT=dg[:, s4, :], rhs=gbb[:, s4, 1024 + hf * 512:1024 + (hf + 1) * 512],
                                                                     start=(j == 0), stop=(j == NJ - 1)),
                               r=["dg%d" % s4, "gb%d" % s4], w=["psA"])
                    if NJ:
                        dve(lambda h: h.tensor_tensor(out=X[:], in0=X[:], in1=psA[:, :, :].rearrange("p a b -> p (a b)"), op=ALU.add), r=["X", "psA"], w=["X"])
                P.mark("pre_store")
                pr = st_par[0] % 2
                st_par[0] += 1
                dve(lambda h, pr=pr: h.tensor_copy(out=f[4 + pr][:], in_=X[:]), r=["X"], w=[fn_[4 + pr]])
                dma(oap[qi * 128:(qi + 1) * 128, :], f[4 + pr][:], r=[fn_[4 + pr]], slot="out%d" % pr)

        P.op("sp", lambda h: h.nop(), extra=list(P.slot_last.values()), force=True)
        if os.environ.get("KMARKS"):
            print("MARKS", P.marks, "TOTAL", P.nops)
        P.emit()
    return nc


def _rope_tabs(pos1d, posr, posc):
    T = pos1d.shape[0]
    out = np.zeros((T, 4, 64), np.float32)
    invA = (np.float32(10000.0) ** (-np.arange(0, 64, 2, dtype=np.float32) / np.float32(64))).astype(np.float32)
    invB = (np.float32(10000.0) ** (-np.arange(0, 32, 2, dtype=np.float32) / np.float32(32))).astype(np.float32)
    angA = pos1d.astype(np.float32)[:, None] * invA[None, :]
    cA, sA = np.cos(angA).astype(np.float32), np.sin(angA).astype(np.float32)
    out[:, 0, :32] = cA
    out[:, 0, 32:] = cA
    out[:, 1, :32] = -sA
    out[:, 1, 32:] = sA
    angR = posr.astype(np.float32)[:, None] * invB[None, :]
    angC = posc.astype(np.float32)[:, None] * invB[None, :]
    cR, sR = np.cos(angR).astype(np.float32), np.sin(angR).astype(np.float32)
    cC, sC = np.cos(angC).astype(np.float32), np.sin(angC).astype(np.float32)
    out[:, 2, 0:16] = cR
    out[:, 2, 16:32] = cR
    out[:, 2, 32:48] = cC
    out[:, 2, 48:64] = cC
    out[:, 3, 0:16] = -sR
    out[:, 3, 16:32] = sR
    out[:, 3, 32:48] = -sC
    out[:, 3, 48:64] = sC
    return out


def _consts(qoff, SS, SQ):
    kk = np.arange(128)[:, None]
    qq = np.arange(128)[None, :]
    prev = (qq <= kk).astype(np.float32)
    nxt = (kk <= qq).astype(np.float32)
    m = np.zeros((4, 128, 512), np.float32)
    m[0] = np.tile(prev, (1, 4))
    m[1] = np.tile(nxt, (1, 4))
    m[2] = m[0] if qoff > 0 else 0.0
    m[3] = m[1] if qoff + SQ < SS else 0.0
    return m


def make_in_maps(inputs, NP, SP, SS, SQ, n_cores, cores_per_sample):
    xp, xs, mp, ms = inputs["x_prompt"], inputs["x_sample"], inputs["mem_prompt"], inputs["mem_sample"]
    wts = {n: np.ascontiguousarray(np.asarray(inputs[n], np.float32).reshape(WSHAPES[n])) for n in WNAMES}
    ident = np.eye(128, dtype=np.float32)
    tp = np.arange(SP)
    tab_p = _rope_tabs(tp, tp // 64, tp % 64)
    maps = []
    for c in range(n_cores):
        sb_, part = c // cores_per_sample, c % cores_per_sample
        qoff = part * SQ
        order = (np.arange(SS) + qoff) % SS
        tab_s = _rope_tabs(order, order // 64, order % 64)
        m = dict(wts)
        m["xp"] = np.ascontiguousarray(xp[c * NP:(c + 1) * NP])
        m["memp"] = np.ascontiguousarray(mp[c * NP:(c + 1) * NP])
        m["xs"] = np.ascontiguousarray(xs[sb_][order])
        m["mems"] = np.ascontiguousarray(ms[sb_])
        m["tabs"] = np.ascontiguousarray(np.concatenate([tab_p] * NP + [tab_s], axis=0))
        m["cmask"] = _consts(qoff, SS, SQ)
        m["ident"] = ident
        m["iota"] = np.ascontiguousarray(np.tile(np.arange(256, dtype=np.float32)[None, :], (128, 1)))
        maps.append(m)
    return maps


_NC_CACHE = {}


def kernel(**inputs):
    NP, SP, SS, SQ, KBLK = 4, 2048, 16384, 4096, 2048
    n = 8
    key = (NP, SP, SS, SQ, KBLK)
    if key not in _NC_CACHE:
        _NC_CACHE[key] = build(*key)
    nc = _NC_CACHE[key]
    maps = make_in_maps(inputs, NP, SP, SS, SQ, n, 4)
    res = run_bass_kernel_spmd(nc, maps, core_ids=list(range(n)))
    yp = np.concatenate([r["yp"] for r in res.results], axis=0).astype(np.float32)
    ys = np.stack([np.concatenate([res.results[s * 4 + p]["ys"] for p in range(4)], axis=0) for s in range(2)], axis=0).astype(np.float32)
    return (yp, ys)
```

```python
import contextlib
import os
import numpy as np
import concourse.bass as bass
import concourse.mybir as mybir
from concourse.bass_utils import run_bass_kernel_spmd

F32 = mybir.dt.float32
BF16 = mybir.dt.bfloat16
U32 = mybir.dt.uint32
I32 = mybir.dt.int32
ALU = mybir.AluOpType
AF = mybir.ActivationFunctionType
AX = mybir.AxisListType

ENGS = ("pe", "act", "dve", "pool", "sp")
D = 1024
EPS = 1e-6


class _Op:
    __slots__ = ("fn", "deps", "dma_slot", "dma_val", "signal", "sigval")

    def __init__(self, fn, deps, dma_slot=None, dma_val=0):
        self.fn = fn
        self.deps = deps
        self.dma_slot = dma_slot
        self.dma_val = dma_val
        self.signal = False
        self.sigval = 0


class Prog:
    def __init__(self, nc):
        self.nc = nc
        self.q = {e: [] for e in ENGS}
        self.lastw = {}
        self.readers = {}
        self.slot_cnt = {}
        self.slot_last = {}
        self.locklast = {}
        self.nops = 0
        self.cut = int(os.environ.get("KCUT", "0"))
        self.marks = []

    def mark(self, name):
        self.marks.append((name, self.nops))

    def op(self, eng, fn, r=(), w=(), slot=None, extra=(), force=False):
        self.nops += 1
        if self.cut and self.nops > self.cut and not force:
            return None
        q = self.q[eng]
        idx = len(q)
        me = (eng, idx)
        key = eng if slot is None else ("dma", slot)
        deps = set(extra)
        for t in r:
            for x in self.lastw.get(t, {}).values():
                deps.add(x)
        for t in w:
            for x in self.lastw.get(t, {}).values():
                deps.add(x)
            for x in self.readers.get(t, {}).values():
                deps.add(x)
        for t in list(r) + list(w):
            if t.startswith("ps"):
                prev = self.locklast.get(t)
                if prev is not None and prev[0] != eng:
                    deps.add(prev)
                self.locklast[t] = me
        dma_val = 0
        if slot is not None:
            pl = self.slot_last.get(slot)
            if pl is not None:
                deps.add(pl)
            c = self.slot_cnt.get(slot, 0) + 1
            self.slot_cnt[slot] = c
            dma_val = 16 * c
            self.slot_last[slot] = me
        for t in w:
            if slot is None:
                self.lastw[t] = {key: me}
            else:
                d = {k: v for k, v in self.lastw.get(t, {}).items() if not isinstance(k, str)}
                d[key] = me
                self.lastw[t] = d
            self.readers[t] = {}
        for t in r:
            self.readers.setdefault(t, {})[key] = me
        deps.discard(me)
        q.append(_Op(fn, deps, slot, dma_val))
        return me

    def barrier(self):
        last = [(e, len(self.q[e]) - 1) for e in ENGS if self.q[e]]
        last += list(self.slot_last.values())
        for e in ENGS:
            self.op(e, lambda h: h.nop(), extra=last)
        self.lastw = {}
        self.readers = {}
        self.locklast = {}

    def emit(self):
        nc = self.nc
        for e in ENGS:
            for i, o in enumerate(self.q[e]):
                for (e2, i2) in o.deps:
                    o2 = self.q[e2][i2]
                    if o2.dma_slot is not None:
                        continue
                    if e2 == e and e in ("pe", "sp"):
                        continue
                    o2.signal = True
        for e in ENGS:
            c = 0
            for o in self.q[e]:
                if o.signal:
                    c += 1
                    o.sigval = c
        slots = sorted(self.slot_cnt.keys(), key=str)
        with contextlib.ExitStack() as st:
            esem = {e: st.enter_context(nc.semaphore("s_" + e)) for e in ENGS}
            ssem = {s: st.enter_context(nc.semaphore("d%d" % k)) for k, s in enumerate(slots)}
            block = st.enter_context(nc.Block())
            prog = self

            def run(e, h):
                seen = {}
                for i, o in enumerate(prog.q[e]):
                    need = {}
                    for (e2, i2) in o.deps:
                        o2 = prog.q[e2][i2]
                        if o2.dma_slot is not None:
                            sem = ssem[o2.dma_slot]
                            val = o2.dma_val
                        else:
                            if e2 == e and e in ("pe", "sp"):
                                continue
                            sem = esem[e2]
                            val = o2.sigval
                        k = id(sem)
                        if seen.get(k, 0) >= val:
                            continue
                        if need.get(k, (None, 0))[1] < val:
                            need[k] = (sem, val)
                    for k, (sem, val) in need.items():
                        if os.environ.get("KSKIP") and e == "act" and i == len(prog.q[e]) - 1 and sem.name in os.environ["KSKIP"].split(","):
                            print("SKIPPING WAIT", sem.name, val)
                            continue
                        h.wait_ge(sem, val)
                        seen[k] = val
                        if os.environ.get("KDBG") and (i >= len(prog.q[e]) - 3 or os.environ.get("KDBG") == e):
                            print("WAIT", e, i, sem.name if hasattr(sem, "name") else sem, val)
                    ins = o.fn(h)
                    if o.dma_slot is not None:
                        ins.then_inc(ssem[o.dma_slot], 16)
                    elif o.signal:
                        ins.then_inc(esem[e], 1)

            @block.sync
            def _(h):
                run("sp", h)

            @block.scalar
            def _(h):
                run("act", h)

            @block.vector
            def _(h):
                run("dve", h)

            @block.gpsimd
            def _(h):
                run("pool", h)

            @block.tensor
            def _(h):
                run("pe", h)


WNAMES = ["ln_mix", "w_in", "qn_a", "kn_a", "sink_a", "qn_b", "kn_b", "go_a", "go_b", "w_out",
          "ln_x", "ln_mem", "w_cq", "w_ckv", "cqn", "ckn", "w_co", "ln_ff", "w_pq", "pk1", "pk2",
          "peer_u", "peer_v"]
WSHAPES = {"ln_mix": [D], "w_in": [D, 1536], "qn_a": [64], "kn_a": [64], "sink_a": [8], "qn_b": [64],
           "kn_b": [64], "go_a": [512], "go_b": [512], "w_out": [D, D], "ln_x": [D], "ln_mem": [D],
           "w_cq": [D, 512], "w_ckv": [D, D], "cqn": [128], "ckn": [128], "w_co": [512, D],
           "ln_ff": [D], "w_pq": [D, 2048], "pk1": [8, 128, 128], "pk2": [8, 128, 128],
           "peer_u": [16384, D], "peer_v": [16384, D]}


def build(NP, SP, SS, SQ, KBLK, LVL=4):
    nc = bass.Bass("TRN2", target_bir_lowering=False)
    NTOK = NP * SP + SS
    NT_ALL = NTOK // 128

    def din(name, shape, dt=F32):
        return nc.dram_tensor(name, list(shape), dt, kind="ExternalInput").ap()

    xp = din("xp", [NP, SP, D])
    xs = din("xs", [SS, D])
    memp = din("memp", [NP, 256, D])
    mems = din("mems", [256, D])
    tabs = din("tabs", [NTOK, 4, 64])
    cmask = din("cmask", [4, 128, 512])
    ident_d = din("ident", [128, 128])
    iota_d = din("iota", [128, 256])
    W = {n: din(n, WSHAPES[n]) for n in WNAMES}
    yp = nc.dram_tensor("yp", [NP, SP, D], F32, kind="ExternalOutput").ap()
    ys = nc.dram_tensor("ys", [SQ, D], F32, kind="ExternalOutput").ap()
    KTA_d = nc.dram_tensor("KTA_d", [128, NTOK], BF16, kind="Internal").ap()
    KTB_d = nc.dram_tensor("KTB_d", [128, NTOK], BF16, kind="Internal").ap()
    VA_d = nc.dram_tensor("VA_d", [NT_ALL, 128, 130], BF16, kind="Internal").ap()
    VB_d = nc.dram_tensor("VB_d", [NT_ALL, 128, 130], BF16, kind="Internal").ap()
    UV_d = nc.dram_tensor("UV_d", [16384, 2048], BF16, kind="Internal").ap()

    seqs = []
    for i in range(NP):
        seqs.append((xp[i], memp[i], SP, i * SP, SP // 128, yp[i], False))
    seqs.append((xs, mems, SS, NP * SP, SQ // 128, ys, True))
    NSEQ = len(seqs)

    P = Prog(nc)
    with contextlib.ExitStack() as st:
        st.enter_context(nc.allow_non_contiguous_dma(reason="tiny gain-vector / V-tile layout loads"))

        def sb(name, shape, dt):
            return st.enter_context(nc.sbuf_tensor("sb_" + name, shape, dt))

        def ps(name, shape, dt):
            return st.enter_context(nc.psum_tensor("ps_" + name, shape, dt))

        def dve(fn, r=(), w=()):
            return P.op("dve", fn, r, w)

        def act(fn, r=(), w=()):
            return P.op("act", fn, r, w)

        def pe(fn, r=(), w=()):
            return P.op("pe", fn, r, w)

        def pool(fn, r=(), w=()):
            return P.op("pool", fn, r, w)

        def dma(out, in_, r=(), w=(), slot=None, eng="sp"):
            return P.op(eng, lambda h: h.dma_start(out=out, in_=in_), r, w, slot=slot)

        w_in = sb("w_in", [128, 8, 1536], BF16)
        w_out = sb("w_out", [128, 8, 1024], BF16)
        w_cq = sb("w_cq", [128, 8, 512], BF16)
        w_co = sb("w_co", [128, 4, 1024], BF16)
        w_pq = sb("w_pq", [128, 8, 2048], BF16)
        pkT = sb("pkT", [128, 16, 128], BF16)
        gb = sb("gb", [128, 4, 1024], F32)
        w_ckv = gb[:, :, :].rearrange("p a b -> p (a b)").bitcast(BF16).rearrange("p (c n) -> p c n", c=8)
        gbb = gb[:, :, :].rearrange("p a b -> p (a b)").bitcast(BF16).rearrange("p (s n) -> p s n", s=4)
        dmy = sb("dmy", [128, 8], F32)
        af = sb("af", [128, 128], F32)
        bf_ = sb("bf", [128, 128], F32)
        eb = sb("eb", [128, 128], F32)
        ga = sb("ga", [128, 128], F32)
        ww = sb("ww", [128, 128], F32)
        dg = sb("dg", [128, 4, 128], BF16)
        X = sb("X", [128, 1024], F32)
        f = [sb("f%d" % i, [128, 1024], F32) for i in range(6)]
        b = [sb("b%d" % i, [128, 1024], BF16) for i in range(6)]
        cand = sb("cand", [128, 4, 256], F32)
        candw = sb("candw", [128, 4, 256], F32)
        candix = sb("candix", [128, 4, 256], F32)
        lnff = sb("lnff", [128, 1024], F32)
        KB = [sb("KB%d" % i, [128, KBLK], BF16) for i in range(2)]
        VB = [sb("VB%d" % i, [128, KBLK // 128, 130], BF16) for i in range(2)]
        KA = sb("KA", [128, 3, 128], BF16)
        VA = sb("VA", [128, 3, 130], BF16)
        memK = sb("memK", [128, 4, 256], BF16)
        memV = sb("memV", [128, 2, 4, 129], BF16)
        idb = sb("idb", [128, 128], BF16)
        idf = sb("idf", [128, 128], F32)
        masks = sb("masks", [128, 4, 512], BF16)
        gT = sb("gT", [128, 6, 8], F32)
        GQ = sb("GQ", [128, 2, 64], F32)
        GK = sb("GK", [128, 2, 64], F32)
        GCQ = sb("GCQ", [128, 1, 128], F32)
        GCK = sb("GCK", [128, 1, 128], F32)
        sinkb = sb("sinkb", [128, 8], F32)
        sinkexp = sb("sinkexp", [128, 8], F32)
        negc = sb("negc", [128, 4], F32)
        tab = sb("tab", [128, 4, 64], F32)
        sm = sb("sm", [128, 64], F32)
        sm2 = sb("sm2", [128, 64], F32)
        v1 = sb("v1", [128, 8, 16], F32)
        v2 = sb("v2", [128, 8, 16], F32)
        i1 = sb("i1", [128, 8, 16], U32)
        i2 = sb("i2", [128, 8, 16], U32)
        i1f = sb("i1f", [128, 8, 16], F32)
        i2f = sb("i2f", [128, 8, 16], F32)
        sc = sb("sc", [128, 8, 16], F32)
        ef = sb("ef", [128, 128], F32)
        icu = sb("icu", [128, 8, 16], U32)
        icf = sb("icf", [128, 8, 16], F32)
        iota = sb("iota", [128, 256], F32)
        eu = sb("eu", [128, 128], U32)
        gw = sb("gw", [128, 128], F32)
        aa = sb("aa", [128, 128], F32)
        psA = ps("psA", [128, 2, 512], F32)
        psT = ps("psT", [128, 8, 128], BF16)
        psS = [ps("psS%d" % i, [128, 2, 512], F32) for i in range(2)]
        psO = ps("psO", [128, 512], F32)

        fn_ = ["f%d" % i for i in range(6)]
        bn_ = ["b%d" % i for i in range(6)]

        dma(idf[:], ident_d, w=["idf"], slot="c0")
        dma(iota[:], iota_d, w=["iota"], slot="c1")
        dve(lambda h: h.tensor_copy(out=idb[:], in_=idf[:]), r=["idf"], w=["idb"])
        for k in range(4):
            dma(f[k][:, 0:512], cmask[k], w=[fn_[k]], slot="c%d" % (1 + k))
            dve(lambda h, k=k: h.tensor_copy(out=masks[:, k, :], in_=f[k][:, 0:512]), r=[fn_[k]], w=["masks"])
        for k, nm in enumerate(["ln_mix", "go_a", "ln_x", "ln_mem", "ln_ff"]):
            if nm == "go_a":
                dma(gT[:, 1, 0:4].unsqueeze(2), W["go_a"].rearrange("(c p o) -> p c o", p=128, o=1), w=["gT"], slot="c5")
                dma(gT[:, 1, 4:8].unsqueeze(2), W["go_b"].rearrange("(c p o) -> p c o", p=128, o=1), w=["gT"], slot="c6")
            else:
                dma(gT[:, k, :].unsqueeze(2), W[nm].rearrange("(c p o) -> p c o", p=128, o=1), w=["gT"], slot="c5")
        dma(lnff[:], W["ln_ff"].partition_broadcast(128), w=["lnff"], slot="c6")
        dma(GQ[:, 0, :], W["qn_a"].partition_broadcast(128), w=["GQ"], slot="c5")
        dma(GQ[:, 1, :], W["qn_b"].partition_broadcast(128), w=["GQ"], slot="c6")
        dma(GK[:, 0, :], W["kn_a"].partition_broadcast(128), w=["GK"], slot="c5")
        dma(GK[:, 1, :], W["kn_b"].partition_broadcast(128), w=["GK"], slot="c6")
        dma(GCQ[:, 0, :], W["cqn"].partition_broadcast(128), w=["GCQ"], slot="c5")
        dma(GCK[:, 0, :], W["ckn"].partition_broadcast(128), w=["GCK"], slot="c6")
        dma(sinkb[:], W["sink_a"].partition_broadcast(128), w=["sinkb"], slot="c5")

        def absmax(dst, src, r):
            dve(lambda h: h.tensor_reduce(out=dst, in_=src, axis=AX.X, op=ALU.max, apply_absolute_value=True),
                r=r, w=["sm"])
        absmax(sm[:, 0:1], GQ[:, 0, :], ["GQ"])
        absmax(sm[:, 1:2], GK[:, 0, :], ["GK"])
        absmax(sm[:, 2:3], GQ[:, 1, :], ["GQ"])
        absmax(sm[:, 3:4], GK[:, 1, :], ["GK"])
        absmax(sm[:, 4:5], GCQ[:, 0, :], ["GCQ"])
        absmax(sm[:, 5:6], GCK[:, 0, :], ["GCK"])
        dve(lambda h: h.tensor_reduce(out=sm[:, 6:7], in_=sinkb[:], axis=AX.X, op=ALU.max), r=["sinkb"], w=["sm"])
        dve(lambda h: h.tensor_tensor(out=sm[:, 8:9], in0=sm[:, 0:1], in1=sm[:, 1:2], op=ALU.mult), r=["sm"], w=["sm"])
        dve(lambda h: h.tensor_tensor(out=sm[:, 9:10], in0=sm[:, 2:3], in1=sm[:, 3:4], op=ALU.mult), r=["sm"], w=["sm"])
        dve(lambda h: h.tensor_tensor(out=sm[:, 10:11], in0=sm[:, 4:5], in1=sm[:, 5:6], op=ALU.mult), r=["sm"], w=["sm"])
        dve(lambda h: h.tensor_scalar(out=sm[:, 8:10], in0=sm[:, 8:10], scalar1=8.0, scalar2=None, op0=ALU.mult), r=["sm"], w=["sm"])
        dve(lambda h: h.tensor_scalar(out=sm[:, 10:11], in0=sm[:, 10:11], scalar1=float(np.sqrt(128.0)), scalar2=None, op0=ALU.mult), r=["sm"], w=["sm"])
        dve(lambda h: h.tensor_tensor(out=sm[:, 8:9], in0=sm[:, 8:9], in1=sm[:, 6:7], op=ALU.max), r=["sm"], w=["sm"])
        dve(lambda h: h.tensor_scalar(out=negc[:, 0:3], in0=sm[:, 8:11], scalar1=-1.0, scalar2=None, op0=ALU.mult), r=["sm"], w=["negc"])
        act(lambda h: h.activation(out=sinkexp[:], in_=sinkb[:], func=AF.Exp, bias=negc[:, 0:1], scale=1.0),
            r=["sinkb", "negc"], w=["sinkexp"])

        P.mark("consts")
        if LVL >= 4 and not os.environ.get("KNOUV"):
            for rblk in range(128):
                k2 = rblk % 2
                r0 = rblk * 128
                dma(f[2 * k2][:], W["peer_u"][r0:r0 + 128, :], w=[fn_[2 * k2]], slot="uvl%d" % (2 * k2))
                dma(f[2 * k2 + 1][:], W["peer_v"][r0:r0 + 128, :], w=[fn_[2 * k2 + 1]], slot="uvl%d" % (2 * k2 + 1))
                dve(lambda h, k2=k2: h.tensor_copy(out=gbb[:, k2, 0:1024], in_=f[2 * k2][:]), r=[fn_[2 * k2]], w=["gb%d" % k2])
                act(lambda h, k2=k2: h.activation(out=gbb[:, k2, 1024:2048], in_=f[2 * k2 + 1][:], func=AF.Copy), r=[fn_[2 * k2 + 1]], w=["gb%d" % k2])
                dma(UV_d[r0:r0 + 128, :], gbb[:, k2, :], r=["gb%d" % k2], slot="uvs%d" % k2)
            P.barrier()
        stg_i = [0]

        def load_w(dst, src, K, N, grow):
            for c in range(K // 128):
                for n0 in range(0, N, 1024):
                    nn = min(1024, N - n0)
                    k = stg_i[0] % 2
                    stg_i[0] += 1
                    dma(f[k][:, 0:nn], src[c * 128:(c + 1) * 128, n0:n0 + nn], w=[fn_[k]], slot="stg%d" % k)
                    if grow is None:
                        dve(lambda h, k=k, c=c, n0=n0, nn=nn: h.tensor_copy(out=dst[:, c, n0:n0 + nn], in_=f[k][:, 0:nn]),
                            r=[fn_[k]], w=["W"])
                    else:
                        dve(lambda h, k=k, c=c, n0=n0, nn=nn: h.tensor_scalar(
                            out=dst[:, c, n0:n0 + nn], in0=f[k][:, 0:nn], scalar1=gT[:, grow, c:c + 1], scalar2=None, op0=ALU.mult),
                            r=[fn_[k], "gT"], w=["W"])
        load_w(w_in, W["w_in"], 1024, 1536, 0)
        load_w(w_out, W["w_out"], 1024, 1024, 1)
        load_w(w_cq, W["w_cq"], 1024, 512, 2)
        load_w(w_ckv, W["w_ckv"], 1024, 1024, 3)
        load_w(w_co, W["w_co"], 512, 1024, None)
        load_w(w_pq, W["w_pq"], 1024, 2048, 4)
        for side, nm in enumerate(["pk1", "pk2"]):
            dma(f[2 + side][:, :].rearrange("p (h d) -> p h d", h=8), W[nm].rearrange("h n d -> n h d"),
                w=[fn_[2 + side]], slot="c%d" % (1 + side))
            dve(lambda h, side=side: h.tensor_copy(out=b[side][:, :], in_=f[2 + side][:, :]), r=[fn_[2 + side]], w=[bn_[side]])
            for hh in range(8):
                pe(lambda h, side=side, hh=hh: h.transpose(out=psT[:, hh, :], in_=b[side][:, hh * 128:(hh + 1) * 128], identity=idb[:]),
                   r=[bn_[side], "idb"], w=["psT"])
            act(lambda h, side=side: h.activation(out=pkT[:, :, :].rearrange("p (h s) k -> p s h k", s=2)[:, side, :, :],
                                                   in_=psT[:, :, :], func=AF.Copy), r=["psT"], w=["W"])

        P.mark("weights")
        def rstd_from_ss(dst, src, n, rtag, wtag):
            act(lambda h: h.activation(out=dst, in_=src, func=AF.Ln, scale=1.0 / n, bias=EPS), r=[rtag], w=[wtag])
            act(lambda h: h.activation(out=dst, in_=dst, func=AF.Ln if False else AF.Exp, scale=-0.5), r=[wtag], w=[wtag])

        def norm_to_hT(src, srctag, extra_f32=None):
            act(lambda h: h.activation(out=f[0][:], in_=src, func=AF.Square, accum_out=sm[:, 16:17]), r=[srctag], w=["f0", "sm"])
            rstd_from_ss(sm[:, 17:18], sm[:, 16:17], 1024.0, "sm", "sm")
            dve(lambda h: h.tensor_scalar(out=b[0][:], in0=src, scalar1=sm[:, 17:18], scalar2=None, op0=ALU.mult),
                r=[srctag, "sm"], w=["b0"])
            if extra_f32 is not None:
                dve(lambda h: h.scalar_tensor_tensor(out=extra_f32, in0=src, scalar=sm[:, 17:18], in1=lnff[:],
                                                     op0=ALU.mult, op1=ALU.mult), r=[srctag, "sm", "lnff"], w=["f1"])
            for c in range(8):
                pe(lambda h, c=c: h.transpose(out=psT[:, c, :], in_=b[0][:, c * 128:(c + 1) * 128], identity=idb[:]),
                   r=["b0", "idb"], w=["psT"])
            act(lambda h: h.activation(out=b[1][:, :], in_=psT[:, :, :].rearrange("p a b -> p (a b)"), func=AF.Copy), r=["psT"], w=["b1"])

        hT = b[1][:, :].rearrange("p (c t) -> p c t", c=8)

        def proj(dst_ps, wt, nK, n0, nn, dtag):
            for c in range(nK):
                pe(lambda h, c=c: h.matmul(dst_ps, lhsT=hT[:, c, :], rhs=wt[:, c, n0:n0 + nn], start=(c == 0), stop=(c == nK - 1)),
                   r=["b1", "W"], w=[dtag])

        def headnorm_rope(zsrc, ztag, NH, G, Gtag, tsel, dst, dtag, perm):
            n = NH * 64
            half = 32 if tsel == 0 else 16
            nb = 64 // (2 * half)
            act(lambda h: h.activation(out=f[0][:, 0:n], in_=zsrc, func=AF.Square), r=[ztag], w=["f0"])
            dve(lambda h: h.tensor_reduce(out=sm[:, 20:20 + NH], in_=f[0][:, 0:n].rearrange("p (h d) -> p h d", d=64), axis=AX.X, op=ALU.add),
                r=["f0"], w=["sm"])
            rstd_from_ss(sm[:, 20:20 + NH], sm[:, 20:20 + NH], 64.0, "sm", "sm")
            dve(lambda h: h.tensor_tensor(out=f[1][:, 0:n].rearrange("p (h d) -> p h d", d=64), in0=zsrc.rearrange("p (h d) -> p h d", d=64),
                                          in1=sm[:, 20:20 + NH].unsqueeze(2).to_broadcast([128, NH, 64]), op=ALU.mult),
                r=[ztag, "sm"], w=["f1"])
            dve(lambda h: h.tensor_tensor(out=f[1][:, 0:n].rearrange("p (h d) -> p h d", d=64), in0=f[1][:, 0:n].rearrange("p (h d) -> p h d", d=64),
                                          in1=G.unsqueeze(1).to_broadcast([128, NH, 64]), op=ALU.mult), r=["f1", Gtag], w=["f1"])
            y4 = f[1][:, 0:n].rearrange("p (h b t f) -> p h b t f", h=NH, b=nb, t=2)
            t24 = f[3][:, 0:n].rearrange("p (h b t f) -> p h b t f", h=NH, b=nb, t=2)
            C3 = tab[:, tsel, :].unsqueeze(1).to_broadcast([128, NH, 64])
            S4 = tab[:, tsel + 1, :].rearrange("p (b t f) -> p b t f", b=nb, t=2)
            dve(lambda h: h.tensor_tensor(out=f[2][:, 0:n].rearrange("p (h d) -> p h d", d=64), in0=f[1][:, 0:n].rearrange("p (h d) -> p h d", d=64),
                                          in1=C3, op=ALU.mult), r=["f1", "tab"], w=["f2"])
            for t in range(2):
                dve(lambda h, t=t: h.tensor_tensor(out=t24[:, :, :, t, :], in0=y4[:, :, :, 1 - t, :],
                                                   in1=S4[:, :, t, :].unsqueeze(1).to_broadcast([128, NH, nb, half]), op=ALU.mult),
                    r=["f1", "tab"], w=["f3"])
            if perm:
                o3 = dst.rearrange("p (g k d) -> p k g d", g=4, k=2)
                a3 = f[2][:, 0:n].rearrange("p (k g d) -> p k g d", k=2, g=4)
                c3 = f[3][:, 0:n].rearrange("p (k g d) -> p k g d", k=2, g=4)
                for kv in range(2):
                    dve(lambda h, kv=kv: h.tensor_tensor(out=o3[:, kv], in0=a3[:, kv], in1=c3[:, kv], op=ALU.add), r=["f2", "f3"], w=[dtag])
            else:
                dve(lambda h: h.tensor_tensor(out=dst, in0=f[2][:, 0:n], in1=f[3][:, 0:n], op=ALU.add), r=["f2", "f3"], w=[dtag])

        P.barrier()
        for si, (xap, memap, S, tok0, nq, oap, is_s) in enumerate(seqs):
            for ti in range(S // 128):
                t0 = tok0 + ti * 128
                gt = t0 // 128
                P.mark("pro_tile_%d_%d" % (si, ti))
                dma(X[:], xap[ti * 128:(ti + 1) * 128, :], w=["X"], slot="x")
                dma(tab[:], tabs[t0:t0 + 128], w=["tab"], slot="tab")
                norm_to_hT(X[:], "X")
                P.mark("pro_normT")
                proj(psA[:, 0, 0:256], w_in, 8, 512, 256, "psA")
                proj(psA[:, 1, 0:256], w_in, 8, 1280, 256, "psA")
                for ty in range(2):
                    z = psA[:, ty, 0:128]
                    headnorm_rope(z, "psA", 2, GK[:, ty, :], "GK", 2 * ty,
                                  b[2][:, ty * 128:(ty + 1) * 128], "b2", False)
                    pe(lambda h, ty=ty: h.transpose(out=psT[:, ty, :], in_=b[2][:, ty * 128:(ty + 1) * 128], identity=idb[:]),
                       r=["b2", "idb"], w=["psT"])
                    vdst = b[3][:, ty * 130:(ty + 1) * 130].rearrange("p (k c) -> p k c", k=2)
                    act(lambda h, ty=ty, vdst=vdst: h.activation(out=vdst[:, :, 0:64], in_=psA[:, ty, 128:256].rearrange("p (k c) -> p k c", k=2),
                                                                  func=AF.Copy), r=["psA"], w=["b3"])
                    pool(lambda h, vdst=vdst: h.memset(vdst[:, :, 64:65], 1.0), w=["b3"])
                act(lambda h: h.activation(out=b[4][:, 0:256], in_=psT[:, 0:2, :].rearrange("p a b -> p (a b)"), func=AF.Copy), r=["psT"], w=["b4"])
                dma(KTA_d[:, t0:t0 + 128], b[4][:, 0:128], r=["b4"], slot="st0")
                dma(KTB_d[:, t0:t0 + 128], b[4][:, 128:256], r=["b4"], slot="st1")
                dma(VA_d[gt], b[3][:, 0:130], r=["b3"], slot="st2")
                dma(VB_d[gt], b[3][:, 130:260], r=["b3"], slot="st3")
        P.barrier()

        def mem_kv(memap):
            for mt in range(2):
                dma(X[:], memap[mt * 128:(mt + 1) * 128, :], w=["X"], slot="x")
                norm_to_hT(X[:], "X")
                proj(psA[:, 0, :], w_ckv, 8, 0, 512, "psA")
                proj(psA[:, 1, :], w_ckv, 8, 512, 512, "psA")
                act(lambda h: h.activation(out=f[0][:, 0:512], in_=psA[:, 0, :], func=AF.Square), r=["psA"], w=["f0"])
                dve(lambda h: h.tensor_reduce(out=sm[:, 20:24], in_=f[0][:, 0:512].rearrange("p (h d) -> p h d", d=128), axis=AX.X, op=ALU.add),
                    r=["f0"], w=["sm"])
                rstd_from_ss(sm[:, 20:24], sm[:, 20:24], 128.0, "sm", "sm")
                dve(lambda h: h.tensor_tensor(out=f[1][:, 0:512].rearrange("p (h d) -> p h d", d=128), in0=psA[:, 0, :].rearrange("p (h d) -> p h d", d=128),
                                              in1=sm[:, 20:24].unsqueeze(2).to_broadcast([128, 4, 128]), op=ALU.mult), r=["psA", "sm"], w=["f1"])
                dve(lambda h: h.tensor_tensor(out=b[2][:, 0:512].rearrange("p (h d) -> p h d", d=128), in0=f[1][:, 0:512].rearrange("p (h d) -> p h d", d=128),
                                              in1=GCK[:, :, :].to_broadcast([128, 4, 128]), op=ALU.mult),
                    r=["f1", "GCK"], w=["b2"])
                for hh in range(4):
                    pe(lambda h, hh=hh: h.transpose(out=psT[:, hh, :], in_=b[2][:, hh * 128:(hh + 1) * 128], identity=idb[:]),
                       r=["b2", "idb"], w=["psT"])
                act(lambda h, mt=mt: h.activation(out=memK[:, :, mt * 128:(mt + 1) * 128], in_=psT[:, 0:4, :], func=AF.Copy), r=["psT"], w=["memK"])
                act(lambda h, mt=mt: h.activation(out=memV[:, mt, :, 0:128], in_=psA[:, 1, :].rearrange("p (h d) -> p h d", d=128), func=AF.Copy),
                    r=["psA"], w=["memV"])
                pool(lambda h, mt=mt: h.memset(memV[:, mt, :, 128:129], 1.0), w=["memV"])

        st_par = [0]
        for si, (xap, memap, S, tok0, nq, oap, is_s) in enumerate(seqs):
            NT = S // 128
            KT = KBLK // 128
            nblk = S // KBLK
            P.mark("main_seq_%d" % si)
            P.barrier()
            load_w(w_ckv, W["w_ckv"], 1024, 1024, 3) if si > 0 else None
            mem_kv(memap)
            P.barrier()
            for qi in range(nq):
                t0 = tok0 + qi * 128
                dma(X[:], xap[qi * 128:(qi + 1) * 128, :], w=["X"], slot="x")
                dma(tab[:], tabs[t0:t0 + 128], w=["tab"], slot="tab")
                P.mark("main_tile_%d_%d" % (si, qi))
                norm_to_hT(X[:], "X")
                proj(psA[:, 0, :], w_in, 8, 0, 512, "psA")
                proj(psA[:, 1, :], w_in, 8, 768, 512, "psA")
                for ty in range(2):
                    headnorm_rope(psA[:, ty, :], "psA", 8, GQ[:, ty, :], "GQ", 2 * ty,
                                  b[2][:, ty * 512:(ty + 1) * 512], "b2", True)
                for c in range(8):
                    pe(lambda h, c=c: h.transpose(out=psT[:, c, :], in_=b[2][:, c * 128:(c + 1) * 128], identity=idb[:]),
                       r=["b2", "idb"], w=["psT"])
                act(lambda h: h.activation(out=b[3][:, :], in_=psT[:, :, :].rearrange("p a b -> p (a b)"), func=AF.Copy), r=["psT"], w=["b3"])
                QT = b[3]
                P.mark("QT")

                def attn_units(units, accs, tsel):
                    cnt = len(units)
                    started = {}
                    def scores(u, bi):
                        kv, tiles, pre = units[u]
                        for jj, (kfn, vfn, msel, ktag, vtag) in enumerate(tiles):
                            pe(lambda h, kv=kv, kfn=kfn, jj=jj, bi=bi: h.matmul(
                                psS[bi][:, jj, :], lhsT=kfn(kv), rhs=QT[kv * 64:(kv + 1) * 64, tsel * 512:(tsel + 1) * 512], start=True, stop=True),
                               r=["b3", ktag], w=["psS%d" % bi])
                        nt = len(tiles)
                        act(lambda h, bi=bi, nt=nt: h.activation(out=b[4 + bi][:, 0:nt * 512], in_=psS[bi][:, 0:nt, :].rearrange("p a b -> p (a b)"),
                                                                  func=AF.Exp, bias=negc[:, tsel:tsel + 1], scale=0.125),
                            r=["psS%d" % bi, "negc"], w=[bn_[4 + bi]])
                        for jj, (kfn, vfn, msel, ktag, vtag) in enumerate(tiles):
                            if msel is not None:
                                pool(lambda h, bi=bi, jj=jj, msel=msel: h.tensor_tensor(out=b[4 + bi][:, jj * 512:(jj + 1) * 512], in0=b[4 + bi][:, jj * 512:(jj + 1) * 512],
                                                                                        in1=masks[:, msel, :], op=ALU.mult),
                                     r=[bn_[4 + bi], "masks"], w=[bn_[4 + bi]])
                    def pv(u, bi):
                        kv, tiles, pre = units[u]
                        acc, atag = accs[kv]
                        for jj, (kfn, vfn, msel, ktag, vtag) in enumerate(tiles):
                            first = kv not in started
                            started[kv] = True
                            last = all(units[u2][0] != kv for u2 in range(u + 1, cnt)) and jj == len(tiles) - 1
                            pe(lambda h, kv=kv, vfn=vfn, jj=jj, bi=bi, first=first, last=last, acc=acc: h.matmul(
                                acc, lhsT=vfn(kv), rhs=b[4 + bi][:, jj * 512:(jj + 1) * 512], start=first, stop=last),
                               r=[bn_[4 + bi], vtag], w=[atag])
                    for u in range(cnt):
                        scores(u, u % 2)
                        if u > 0:
                            pv(u - 1, (u - 1) % 2)
                        if units[u][2] is not None:
                            units[u][2]()
                    pv(cnt - 1, (cnt - 1) % 2)

                def finish_attn(accs, ocol, with_sink):
                    for kv in range(2):
                        acc, atag = accs[kv]
                        act(lambda h, kv=kv, acc=acc: h.activation(out=f[4][0:65, kv * 512:(kv + 1) * 512], in_=acc, func=AF.Copy), r=[atag], w=["f4"])
                    for kv in range(2):
                        for g in range(4):
                            pe(lambda h, kv=kv, g=g: h.transpose(out=psA[:, kv, g * 65:(g + 1) * 65], in_=f[4][0:65, kv * 512 + g * 128:kv * 512 + (g + 1) * 128],
                                                                  identity=idf[0:65, 0:65]), r=["f4", "idf"], w=["psA"])
                    pv4 = psA[:, :, 0:260].rearrange("p k (g c) -> p k g c", g=4)
                    dve(lambda h: h.tensor_copy(out=sm2[:, 0:8].rearrange("p (k g) -> p k g", k=2), in_=pv4[:, :, :, 64]), r=["psA"], w=["sm2"])
                    if with_sink:
                        dve(lambda h: h.tensor_tensor(out=sm2[:, 0:8], in0=sm2[:, 0:8], in1=sinkexp[:], op=ALU.add), r=["sm2", "sinkexp"], w=["sm2"])
                    dve(lambda h: h.reciprocal(out=sm2[:, 8:16], in_=sm2[:, 0:8]), r=["sm2"], w=["sm2"])
                    for kv in range(2):
                        dve(lambda h, kv=kv: h.tensor_tensor(out=f[5][:, ocol + kv * 256:ocol + (kv + 1) * 256].rearrange("p (g d) -> p g d", g=4),
                                                             in0=pv4[:, kv, :, 0:64],
                                                             in1=sm2[:, 8 + kv * 4:12 + kv * 4].unsqueeze(2).to_broadcast([128, 4, 64]), op=ALU.mult),
                            r=["psA", "sm2"], w=["f5"])

                if is_s:
                    jl = [((qi - 1) % NT, 2 if qi == 0 else 0), (qi, None), ((qi + 1) % NT, 3 if qi == nq - 1 else 1)]
                else:
                    jl = []
                    if qi > 0:
                        jl.append((qi - 1, 0))
                    jl.append((qi, None))
                    if qi < NT - 1:
                        jl.append((qi + 1, 1))
                for jj, (kt, msel) in enumerate(jl):
                    kt0 = tok0 + kt * 128
                    dma(KA[:, jj, :], KTA_d[:, kt0:kt0 + 128], w=["KA%d" % jj], slot="ka%d" % jj)
                    dma(VA[:, jj, :], VA_d[kt0 // 128], w=["VA%d" % jj], slot="va%d" % jj)
                unitsA = []
                for kv in range(2):
                    tl = [((lambda kv, jj=jj: KA[kv * 64:(kv + 1) * 64, jj, :]), (lambda kv, jj=jj: VA[:, jj, kv * 65:(kv + 1) * 65]), msel,
                           "KA%d" % jj, "VA%d" % jj) for jj, (kt, msel) in enumerate(jl)]
                    unitsA.append((kv, tl[0:2], None))
                    if len(tl) > 2:
                        unitsA.append((kv, tl[2:3], None))
                accs = {0: (psO[0:65, :], "psO"), 1: (psA[0:65, 0, :], "psA")}
                attn_units(unitsA, accs, 0)
                P.mark("attnA_units")
                finish_attn(accs, 0, True)
                P.mark("attnA_fin")
                def load_blk(blk):
                    bb = blk % 2
                    k0 = tok0 + blk * KBLK
                    dma(KB[bb][:, :], KTB_d[:, k0:k0 + KBLK], w=["KB%d" % bb], slot="kb%d" % bb)
                    dma(VB[bb][:, :, :], VB_d[k0 // 128:k0 // 128 + KT].rearrange("j p c -> p j c"), w=["VB%d" % bb], slot="vb%d" % bb)
                load_blk(0)
                unitsB = []
                for blk in range(nblk):
                    bb = blk % 2
                    firstu = True
                    for kv in range(2):
                        for j0 in range(0, KT, 2):
                            tl = [((lambda kv, bb=bb, j=j: KB[bb][kv * 64:(kv + 1) * 64, j * 128:(j + 1) * 128]),
                                   (lambda kv, bb=bb, j=j: VB[bb][:, j, kv * 65:(kv + 1) * 65]), None, "KB%d" % bb, "VB%d" % bb)
                                  for j in range(j0, min(j0 + 2, KT))]
                            pre = None
                            if firstu and blk + 1 < nblk:
                                pre = (lambda blk=blk: load_blk(blk + 1))
                            firstu = False
                            unitsB.append((kv, tl, pre))
                attn_units(unitsB, accs, 1)
                finish_attn(accs, 512, False)
                P.mark("attnB_fin")

                act(lambda h: h.activation(out=f[0][:, 0:512], in_=f[5][:, 0:512], func=AF.Square, accum_out=sm[:, 30:31]), r=["f5"], w=["f0", "sm"])
                act(lambda h: h.activation(out=f[0][:, 512:1024], in_=f[5][:, 512:1024], func=AF.Square, accum_out=sm[:, 31:32]), r=["f5"], w=["f0", "sm"])
                rstd_from_ss(sm[:, 30:32], sm[:, 30:32], 512.0, "sm", "sm")
                dve(lambda h: h.tensor_tensor(out=b[0][:, :].rearrange("p (a d) -> p a d", a=2), in0=f[5][:, :].rearrange("p (a d) -> p a d", a=2),
                                              in1=sm[:, 30:32].unsqueeze(2).to_broadcast([128, 2, 512]), op=ALU.mult), r=["f5", "sm"], w=["b0"])
                for c in range(8):
                    pe(lambda h, c=c: h.transpose(out=psT[:, c, :], in_=b[0][:, c * 128:(c + 1) * 128], identity=idb[:]),
                       r=["b0", "idb"], w=["psT"])
                act(lambda h: h.activation(out=b[1][:, :], in_=psT[:, :, :].rearrange("p a b -> p (a b)"), func=AF.Copy), r=["psT"], w=["b1"])
                proj(psA[:, 0, :], w_out, 8, 0, 512, "psA")
                proj(psA[:, 1, :], w_out, 8, 512, 512, "psA")
                dve(lambda h: h.tensor_tensor(out=X[:], in0=X[:], in1=psA[:, :, :].rearrange("p a b -> p (a b)"), op=ALU.add), r=["X", "psA"], w=["X"])

                for _lvl2 in ([0] if LVL >= 2 else []):
                    norm_to_hT(X[:], "X")
                    proj(psA[:, 0, :], w_cq, 8, 0, 512, "psA")
                    act(lambda h: h.activation(out=f[0][:, 0:512], in_=psA[:, 0, :], func=AF.Square), r=["psA"], w=["f0"])
                    dve(lambda h: h.tensor_reduce(out=sm[:, 20:24], in_=f[0][:, 0:512].rearrange("p (h d) -> p h d", d=128), axis=AX.X, op=ALU.add),
                        r=["f0"], w=["sm"])
                    rstd_from_ss(sm[:, 20:24], sm[:, 20:24], 128.0, "sm", "sm")
                    dve(lambda h: h.tensor_tensor(out=f[1][:, 0:512].rearrange("p (h d) -> p h d", d=128), in0=psA[:, 0, :].rearrange("p (h d) -> p h d", d=128),
                                                  in1=sm[:, 20:24].unsqueeze(2).to_broadcast([128, 4, 128]), op=ALU.mult), r=["psA", "sm"], w=["f1"])
                    dve(lambda h: h.tensor_tensor(out=b[2][:, 0:512].rearrange("p (h d) -> p h d", d=128), in0=f[1][:, 0:512].rearrange("p (h d) -> p h d", d=128),
                                                  in1=GCQ[:, :, :].to_broadcast([128, 4, 128]), op=ALU.mult),
                        r=["f1", "GCQ"], w=["b2"])
                    for hh in range(4):
                        pe(lambda h, hh=hh: h.transpose(out=psT[:, hh, :], in_=b[2][:, hh * 128:(hh + 1) * 128], identity=idb[:]),
                           r=["b2", "idb"], w=["psT"])
                    act(lambda h: h.activation(out=b[3][:, 0:512], in_=psT[:, 0:4, :].rearrange("p a b -> p (a b)"), func=AF.Copy), r=["psT"], w=["b3"])
                    for mt in range(2):
                        for hh in range(4):
                            pe(lambda h, mt=mt, hh=hh: h.matmul(psS[0][:, mt, hh * 128:(hh + 1) * 128], lhsT=memK[:, hh, mt * 128:(mt + 1) * 128],
                                                                rhs=b[3][:, hh * 128:(hh + 1) * 128], start=True, stop=True),
                               r=["memK", "b3"], w=["psS0"])
                    act(lambda h: h.activation(out=b[4][:, :], in_=psS[0][:, :, :].rearrange("p a b -> p (a b)"), func=AF.Exp,
                                               bias=negc[:, 2:3], scale=float(128.0 ** -0.5)), r=["psS0", "negc"], w=["b4"])
                    for hh in range(4):
                        for mt in range(2):
                            pe(lambda h, mt=mt, hh=hh: h.matmul(psA[:, hh // 2, (hh % 2) * 129:(hh % 2 + 1) * 129],
                                                                lhsT=b[4][:, mt * 512 + hh * 128:mt * 512 + (hh + 1) * 128],
                                                                rhs=memV[:, mt, hh, :], start=(mt == 0), stop=(mt == 1)),
                               r=["b4", "memV"], w=["psA"])
                    px = psA[:, :, 0:258].rearrange("p a (h c) -> p a h c", h=2)
                    dve(lambda h: h.reciprocal(out=sm2[:, 16:20].rearrange("p (a h) -> p a h", a=2), in_=px[:, :, :, 128]), r=["psA"], w=["sm2"])
                    for a in range(2):
                        dve(lambda h, a=a: h.tensor_tensor(out=b[0][:, a * 256:(a + 1) * 256].rearrange("p (h d) -> p h d", h=2), in0=px[:, a, :, 0:128],
                                                           in1=sm2[:, 16 + 2 * a:18 + 2 * a].unsqueeze(2).to_broadcast([128, 2, 128]), op=ALU.mult),
                            r=["psA", "sm2"], w=["b0"])
                    for c in range(4):
                        pe(lambda h, c=c: h.transpose(out=psT[:, c, :], in_=b[0][:, c * 128:(c + 1) * 128], identity=idb[:]),
                           r=["b0", "idb"], w=["psT"])
                    act(lambda h: h.activation(out=b[1][:, 0:512], in_=psT[:, 0:4, :].rearrange("p a b -> p (a b)"), func=AF.Copy), r=["psT"], w=["b1"])
                    proj(psA[:, 0, :], w_co, 4, 0, 512, "psA")
                    proj(psA[:, 1, :], w_co, 4, 512, 512, "psA")
                    dve(lambda h: h.tensor_tensor(out=X[:], in0=X[:], in1=psA[:, :, :].rearrange("p a b -> p (a b)"), op=ALU.add), r=["X", "psA"], w=["X"])

                for _lvl3 in ([0] if LVL >= 3 else []):
                    norm_to_hT(X[:], "X", extra_f32=f[1][:])
                    for half in range(2):
                        for nb in range(8):
                            for c in range(8):
                                pe(lambda h, half=half, nb=nb, c=c: h.matmul(psA[:, nb // 4, (nb % 4) * 128:(nb % 4 + 1) * 128],
                                                                              lhsT=w_pq[:, c, (half * 8 + nb) * 128:(half * 8 + nb + 1) * 128],
                                                                              rhs=hT[:, c, :], start=(c == 0), stop=(c == 7)),
                                   r=["W", "b1"], w=["psA"])
                        act(lambda h, half=half: h.activation(out=b[2 + half][:, :], in_=psA[:, :, :].rearrange("p a b -> p (a b)"), func=AF.Copy),
                            r=["psA"], w=[bn_[2 + half]])
                    for hh in range(8):
                        for side in range(2):
                            blk = 2 * hh + side
                            src = b[2 + blk // 8][:, (blk % 8) * 128:(blk % 8 + 1) * 128]
                            pe(lambda h, hh=hh, side=side, src=src: h.matmul(psS[side][:, hh // 4, (hh % 4) * 128:(hh % 4 + 1) * 128], lhsT=src,
                                                                             rhs=pkT[:, 2 * hh + side, :], start=True, stop=True),
                               r=["b2", "b3", "W"], w=["psS%d" % side])
                    for side in range(2):
                        act(lambda h, side=side: h.activation(out=f[2 + side][:, :], in_=psS[side][:, :, :].rearrange("p a b -> p (a b)"), func=AF.Copy),
                            r=["psS%d" % side], w=[fn_[2 + side]])
                    def ct(nm, side, hh):
                        return "%s%d_%d" % (nm, side, hh)
                    chains = [(side, hh) for hh in range(8) for side in range(2)]
                    wktags = [ct("wk", side, hh) for (side, hh) in chains]
                    dve(lambda h: h.memset(dmy[:, 0:1], 0.0), w=["f4", "f5"] + wktags)
                    for step in range(5):
                        for (side, hh) in chains:
                            S_ = f[2 + side]
                            Wk = f[4 + side]
                            vv = v1 if side == 0 else v2
                            ii = i1 if side == 0 else i2
                            sl = slice(hh * 128, (hh + 1) * 128)
                            if step == 0:
                                dve(lambda h, S_=S_, vv=vv, hh=hh, sl=sl: h.max(out=vv[:, hh, 0:8], in_=S_[:, sl]), r=[fn_[2 + side]], w=[ct("va", side, hh)])
                            elif step == 1:
                                dve(lambda h, S_=S_, Wk=Wk, vv=vv, hh=hh, sl=sl: h.match_replace(out=Wk[:, sl], in_to_replace=vv[:, hh, 0:8], in_values=S_[:, sl], imm_value=-1e30),
                                    r=[fn_[2 + side], ct("va", side, hh)], w=[ct("wk", side, hh)])
                            elif step == 2:
                                dve(lambda h, Wk=Wk, vv=vv, hh=hh, sl=sl: h.max(out=vv[:, hh, 8:16], in_=Wk[:, sl]), r=[ct("wk", side, hh)], w=[ct("vb", side, hh)])
                            elif step == 3:
                                dve(lambda h, S_=S_, vv=vv, ii=ii, hh=hh, sl=sl: h.max_index(out=ii[:, hh, 0:8], in_max=vv[:, hh, 0:8], in_values=S_[:, sl]),
                                    r=[fn_[2 + side], ct("va", side, hh)], w=[ct("ia", side, hh)])
                            else:
                                dve(lambda h, Wk=Wk, vv=vv, ii=ii, hh=hh, sl=sl: h.max_index(out=ii[:, hh, 8:16], in_max=vv[:, hh, 8:16], in_values=Wk[:, sl]),
                                    r=[ct("wk", side, hh), ct("vb", side, hh)], w=[ct("ib", side, hh)])
                    dve(lambda h: h.memset(dmy[:, 1:2], 0.0), w=["f4", "f5"] + wktags)
                    vtags = [ct(nm, side, hh) for nm in ("va", "vb") for side in range(2) for hh in range(8)]
                    cwtags = ["cw%d" % k for k in range(4)]
                    dve(lambda h: h.memset(dmy[:, 2:3], 0.0), w=["candw"] + cwtags)
                    dve(lambda h: h.tensor_scalar(out=i1f[:], in0=i1[:], scalar1=128.0, scalar2=None, op0=ALU.mult),
                        r=[ct(nm, 0, hh) for nm in ("ia", "ib") for hh in range(8)], w=["i1f"])
                    dve(lambda h: h.tensor_copy(out=i2f[:], in_=i2[:]), r=[ct(nm, 1, hh) for nm in ("ia", "ib") for hh in range(8)], w=["i2f"])
                    for hg in range(2):
                        hs = slice(hg * 4, hg * 4 + 4)
                        dve(lambda h, hs=hs: h.tensor_tensor(out=cand[:, :, :].rearrange("p h (a b) -> p h a b", a=16),
                                                             in0=v1[:, hs, :].unsqueeze(3).to_broadcast([128, 4, 16, 16]),
                                                             in1=v2[:, hs, :].unsqueeze(2).to_broadcast([128, 4, 16, 16]), op=ALU.add),
                            r=vtags, w=["cand"])
                        for step in range(5):
                            for h4 in range(4):
                                hh = hg * 4 + h4
                                if step == 0:
                                    dve(lambda h, h4=h4, hh=hh: h.max(out=sc[:, hh, 0:8], in_=cand[:, h4, :]), r=["cand"], w=["sca%d" % hh])
                                elif step == 1:
                                    dve(lambda h, h4=h4, hh=hh: h.match_replace(out=candw[:, h4, :], in_to_replace=sc[:, hh, 0:8], in_values=cand[:, h4, :], imm_value=-1e30),
                                        r=["cand", "sca%d" % hh], w=["cw%d" % h4])
                                elif step == 2:
                                    dve(lambda h, h4=h4, hh=hh: h.max(out=sc[:, hh, 8:16], in_=candw[:, h4, :]), r=["cw%d" % h4], w=["scb%d" % hh])
                                elif step == 3:
                                    dve(lambda h, h4=h4, hh=hh: h.max_index(out=icu[:, hh, 0:8], in_max=sc[:, hh, 0:8], in_values=cand[:, h4, :]),
                                        r=["cand", "sca%d" % hh], w=["ica%d" % hh])
                                else:
                                    dve(lambda h, h4=h4, hh=hh: h.max_index(out=icu[:, hh, 8:16], in_max=sc[:, hh, 8:16], in_values=candw[:, h4, :]),
                                        r=["cw%d" % h4, "scb%d" % hh], w=["icb%d" % hh])
                    dve(lambda h: h.memset(dmy[:, 3:4], 0.0), w=["candw"] + cwtags)
                    sctags = ["sca%d" % hh for hh in range(8)] + ["scb%d" % hh for hh in range(8)]
                    dve(lambda h: h.tensor_copy(out=icf[:], in_=icu[:]), r=["ica%d" % hh for hh in range(8)] + ["icb%d" % hh for hh in range(8)], w=["icf"])
                    icf2 = icf[:, :, :].rearrange("p h k -> p (h k)")
                    thr = iota[:, :].rearrange("p (m s) -> p m s", s=16)[:, 1:16, 0]
                    for hv in range(2):
                        js = slice(hv * 64, (hv + 1) * 64)
                        tb, tbt = (f[0], "f0") if hv == 0 else (f[2], "f2")
                        t3 = tb[:, 0:960].rearrange("p (j m) -> p j m", m=15)
                        dve(lambda h, t3=t3, js=js: h.tensor_tensor(out=t3, in0=icf2[:, js].unsqueeze(2).to_broadcast([128, 64, 15]),
                                                                    in1=thr.unsqueeze(1).to_broadcast([128, 64, 15]), op=ALU.is_ge), r=["icf", "iota"], w=[tbt])
                        dve(lambda h, t3=t3, js=js: h.tensor_reduce(out=af[:, js], in_=t3, axis=AX.X, op=ALU.add), r=[tbt], w=["af%d" % hv])
                        dve(lambda h, js=js: h.scalar_tensor_tensor(out=bf_[:, js], in0=af[:, js], scalar=-16.0, in1=icf2[:, js], op0=ALU.mult, op1=ALU.add),
                            r=["af%d" % hv, "icf"], w=["bf%d" % hv])
                    k_ = 0
                    for hv in range(2):
                        js = slice(hv * 64, (hv + 1) * 64)
                        hsl = slice(hv * 4, hv * 4 + 4)
                        for (src, srct, tabf, tabt, dst, dstt) in ((af, "af%d" % hv, i1f, "i1f", ef, "ef"), (bf_, "bf%d" % hv, i2f, "i2f", eb, "eb")):
                            buf, buft = ((f[0], "f0"), (f[2], "f2"), (f[3], "f3"))[k_ % 3]
                            k_ += 1
                            oh3 = buf[:, 0:1024].rearrange("p (j a) -> p j a", a=16)
                            oh4 = buf[:, 0:1024].rearrange("p (h k a) -> p h k a", h=4, k=16)
                            dve(lambda h, oh3=oh3, src=src, js=js: h.tensor_tensor(out=oh3, in0=iota[:, 0:16].unsqueeze(1).to_broadcast([128, 64, 16]),
                                                                                   in1=src[:, js].unsqueeze(2).to_broadcast([128, 64, 16]), op=ALU.is_equal),
                                r=["iota", srct], w=[buft])
                            dve(lambda h, oh4=oh4, tabf=tabf, hsl=hsl: h.tensor_tensor(out=oh4, in0=oh4, in1=tabf[:, hsl, :].unsqueeze(2).to_broadcast([128, 4, 16, 16]), op=ALU.mult),
                                r=[buft, tabt], w=[buft])
                            dve(lambda h, oh3=oh3, dst=dst, js=js: h.tensor_reduce(out=dst[:, js], in_=oh3, axis=AX.X, op=ALU.add), r=[buft], w=[dstt])
                    dve(lambda h: h.tensor_tensor(out=ef[:], in0=ef[:], in1=eb[:], op=ALU.add), r=["ef", "eb"], w=["ef"])
                    dve(lambda h: h.tensor_copy(out=eu[:], in_=ef[:]), r=["ef"], w=["eu"])
                    dve(lambda h: h.tensor_tensor(out=gw[:, :].rearrange("p (h k) -> p h k", h=8), in0=sc[:, :, :],
                                                  in1=sc[:, :, 0:1].to_broadcast([128, 8, 16]), op=ALU.subtract), r=sctags, w=["gw"])
                    act(lambda h: h.activation(out=gw[:], in_=gw[:], func=AF.Exp), r=["gw"], w=["gw"])
                    dve(lambda h: h.tensor_reduce(out=sm2[:, 24:32], in_=gw[:, :].rearrange("p (h k) -> p h k", h=8), axis=AX.X, op=ALU.add), r=["gw"], w=["sm2"])
                    dve(lambda h: h.reciprocal(out=sm2[:, 32:40], in_=sm2[:, 24:32]), r=["sm2"], w=["sm2"])
                    dve(lambda h: h.tensor_tensor(out=gw[:, :].rearrange("p (h k) -> p h k", h=8), in0=gw[:, :].rearrange("p (h k) -> p h k", h=8),
                                                  in1=sm2[:, 32:40].unsqueeze(2).to_broadcast([128, 8, 16]), op=ALU.mult), r=["gw", "sm2"], w=["gw"])
                    NJ = 128 if LVL >= 4 else 0
                    gbufs = [(gbb[:, k, :], "gb%d" % k) for k in range(4)]
                    for tns, tg in ((cand, "cand"), (candw, "candw"), (candix, "candix")):
                        gbufs.append((tns[:, :, :].rearrange("p a b -> p (a b)").bitcast(BF16), tg))
                    for j in range(NJ):
                        s4 = j % 4
                        gbuf, gtag = gbufs[j % len(gbufs)]
                        P.op("pool", lambda h, j=j, gbuf=gbuf: h.indirect_dma_start(out=gbuf, out_offset=None, in_=UV_d,
                                                                                in_offset=bass.IndirectOffsetOnAxis(ap=eu[:, j:j + 1], axis=0)),
                             r=["eu"], w=[gtag], slot="gu%d" % (j % len(gbufs)))
                        dve(lambda h, j=j, gbuf=gbuf: h.scalar_tensor_tensor(out=f[0][:], in0=gbuf[:, 0:1024], scalar=1.0, in1=f[1][:], op0=ALU.mult, op1=ALU.mult,
                                                                         accum_out=aa[:, j:j + 1]), r=[gtag, "f1"], w=["f0", "aa%d" % s4])
                        act(lambda h, j=j: h.activation(out=ga[:, j:j + 1], in_=aa[:, j:j + 1], func=AF.Gelu), r=["aa%d" % s4], w=["ga%d" % s4])
                        act(lambda h, j=j: h.activation(out=ww[:, j:j + 1], in_=ga[:, j:j + 1], func=AF.Copy, scale=gw[:, j:j + 1]),
                            r=["ga%d" % s4, "gw"], w=["ww%d" % s4])
                        act(lambda h, j=j, s4=s4: h.activation(out=dg[:, s4, :], in_=idb[:], func=AF.Copy, scale=ww[:, j:j + 1]),
                            r=["ww%d" % s4, "idb"], w=["dg%d" % s4])
                        for hf in range(2):
                            pe(lambda h, j=j, s4=s4, hf=hf, gbuf=gbuf: h.matmul(psA[:, hf, :], lhsT=dg[:, s4, :], rhs=gbuf[:, 1024 + hf * 512:1024 + (hf + 1) * 512],
                                                                                start=(j == 0), stop=(j == NJ - 1)),
                               r=["dg%d" % s4, gtag], w=["psA"])
                    if NJ:
                        dve(lambda h: h.tensor_tensor(out=X[:], in0=X[:], in1=psA[:, :, :].rearrange("p a b -> p (a b)"), op=ALU.add), r=["X", "psA"], w=["X"])
                P.mark("pre_store")
                pr = st_par[0] % 2
                st_par[0] += 1
                dve(lambda h, pr=pr: h.tensor_copy(out=f[4 + pr][:], in_=X[:]), r=["X"], w=[fn_[4 + pr]])
                dma(oap[qi * 128:(qi + 1) * 128, :], f[4 + pr][:], r=[fn_[4 + pr]], slot="out%d" % pr)

        P.op("sp", lambda h: h.nop(), extra=list(P.slot_last.values()), force=True)
        if os.environ.get("KMARKS"):
            print("MARKS", P.marks, "TOTAL", P.nops)
        P.emit()
    return nc


def _rope_tabs(pos1d, posr, posc):
    T = pos1d.shape[0]
    out = np.zeros((T, 4, 64), np.float32)
    invA = (np.float32(10000.0) ** (-np.arange(0, 64, 2, dtype=np.float32) / np.float32(64))).astype(np.float32)
    invB = (np.float32(10000.0) ** (-np.arange(0, 32, 2, dtype=np.float32) / np.float32(32))).astype(np.float32)
    angA = pos1d.astype(np.float32)[:, None] * invA[None, :]
    cA, sA = np.cos(angA).astype(np.float32), np.sin(angA).astype(np.float32)
    out[:, 0, :32] = cA
    out[:, 0, 32:] = cA
    out[:, 1, :32] = -sA
    out[:, 1, 32:] = sA
    angR = posr.astype(np.float32)[:, None] * invB[None, :]
    angC = posc.astype(np.float32)[:, None] * invB[None, :]
    cR, sR = np.cos(angR).astype(np.float32), np.sin(angR).astype(np.float32)
    cC, sC = np.cos(angC).astype(np.float32), np.sin(angC).astype(np.float32)
    out[:, 2, 0:16] = cR
    out[:, 2, 16:32] = cR
    out[:, 2, 32:48] = cC
    out[:, 2, 48:64] = cC
    out[:, 3, 0:16] = -sR
    out[:, 3, 16:32] = sR
    out[:, 3, 32:48] = -sC
    out[:, 3, 48:64] = sC
    return out


def _consts(qoff, SS, SQ):
    kk = np.arange(128)[:, None]
    qq = np.arange(128)[None, :]
    prev = (qq <= kk).astype(np.float32)
    nxt = (kk <= qq).astype(np.float32)
    m = np.zeros((4, 128, 512), np.float32)
    m[0] = np.tile(prev, (1, 4))
    m[1] = np.tile(nxt, (1, 4))
    m[2] = m[0] if qoff > 0 else 0.0
    m[3] = m[1] if qoff + SQ < SS else 0.0
    return m


def make_in_maps(inputs, NP, SP, SS, SQ, n_cores, cores_per_sample):
    xp, xs, mp, ms = inputs["x_prompt"], inputs["x_sample"], inputs["mem_prompt"], inputs["mem_sample"]
    wts = {n: np.ascontiguousarray(np.asarray(inputs[n], np.float32).reshape(WSHAPES[n])) for n in WNAMES}
    ident = np.eye(128, dtype=np.float32)
    tp = np.arange(SP)
    tab_p = _rope_tabs(tp, tp // 64, tp % 64)
    maps = []
    for c in range(n_cores):
        sb_, part = c // cores_per_sample, c % cores_per_sample
        qoff = part * SQ
        order = (np.arange(SS) + qoff) % SS
        tab_s = _rope_tabs(order, order // 64, order % 64)
        m = dict(wts)
        m["xp"] = np.ascontiguousarray(xp[c * NP:(c + 1) * NP])
        m["memp"] = np.ascontiguousarray(mp[c * NP:(c + 1) * NP])
        m["xs"] = np.ascontiguousarray(xs[sb_][order])
        m["mems"] = np.ascontiguousarray(ms[sb_])
        m["tabs"] = np.ascontiguousarray(np.concatenate([tab_p] * NP + [tab_s], axis=0))
        m["cmask"] = _consts(qoff, SS, SQ)
        m["ident"] = ident
        m["iota"] = np.ascontiguousarray(np.tile(np.arange(256, dtype=np.float32)[None, :], (128, 1)))
        maps.append(m)
    return maps


_NC_CACHE = {}


def kernel(**inputs):
    NP, SP, SS, SQ, KBLK = 4, 2048, 16384, 4096, 2048
    n = 8
    key = (NP, SP, SS, SQ, KBLK)
    if key not in _NC_CACHE:
        _NC_CACHE[key] = build(*key)
    nc = _NC_CACHE[key]
    maps = make_in_maps(inputs, NP, SP, SS, SQ, n, 4)
    res = run_bass_kernel_spmd(nc, maps, core_ids=list(range(n)))
    yp = np.concatenate([r["yp"] for r in res.results], axis=0).astype(np.float32)
    ys = np.stack([np.concatenate([res.results[s * 4 + p]["ys"] for p in range(4)], axis=0) for s in range(2)], axis=0).astype(np.float32)
    return (yp, ys)
```

```python
import contextlib
import os
import numpy as np
import concourse.bass as bass
import concourse.mybir as mybir
from concourse.bass_utils import run_bass_kernel_spmd

F32 = mybir.dt.float32
BF16 = mybir.dt.bfloat16
U32 = mybir.dt.uint32
I32 = mybir.dt.int32
ALU = mybir.AluOpType
AF = mybir.ActivationFunctionType
AX = mybir.AxisListType

ENGS = ("pe", "act", "dve", "pool", "sp")
D = 1024
EPS = 1e-6


class _Op:
    __slots__ = ("fn", "deps", "dma_slot", "dma_val", "signal", "sigval")

    def __init__(self, fn, deps, dma_slot=None, dma_val=0):
        self.fn = fn
        self.deps = deps
        self.dma_slot = dma_slot
        self.dma_val = dma_val
        self.signal = False
        self.sigval = 0


class Prog:
    def __init__(self, nc):
        self.nc = nc
        self.q = {e: [] for e in ENGS}
        self.lastw = {}
        self.readers = {}
        self.slot_cnt = {}
        self.slot_last = {}
        self.locklast = {}
        self.nops = 0
        self.cut = int(os.environ.get("KCUT", "0"))
        self.marks = []

    def mark(self, name):
        self.marks.append((name, self.nops))

    def op(self, eng, fn, r=(), w=(), slot=None, extra=(), force=False):
        self.nops += 1
        if self.cut and self.nops > self.cut and not force:
            return None
        q = self.q[eng]
        idx = len(q)
        me = (eng, idx)
        key = eng if slot is None else ("dma", slot)
        deps = set(extra)
        for t in r:
            for x in self.lastw.get(t, {}).values():
                deps.add(x)
        for t in w:
            for x in self.lastw.get(t, {}).values():
                deps.add(x)
            for x in self.readers.get(t, {}).values():
                deps.add(x)
        for t in list(r) + list(w):
            if t.startswith("ps"):
                prev = self.locklast.get(t)
                if prev is not None and prev[0] != eng:
                    deps.add(prev)
                self.locklast[t] = me
        dma_val = 0
        if slot is not None:
            pl = self.slot_last.get(slot)
            if pl is not None:
                deps.add(pl)
            c = self.slot_cnt.get(slot, 0) + 1
            self.slot_cnt[slot] = c
            dma_val = 16 * c
            self.slot_last[slot] = me
        for t in w:
            if slot is None:
                self.lastw[t] = {key: me}
            else:
                d = {k: v for k, v in self.lastw.get(t, {}).items() if not isinstance(k, str)}
                d[key] = me
                self.lastw[t] = d
            self.readers[t] = {}
        for t in r:
            self.readers.setdefault(t, {})[key] = me
        deps.discard(me)
        q.append(_Op(fn, deps, slot, dma_val))
        return me

    def barrier(self):
        last = [(e, len(self.q[e]) - 1) for e in ENGS if self.q[e]]
        last += list(self.slot_last.values())
        for e in ENGS:
            self.op(e, lambda h: h.nop(), extra=last)
        self.lastw = {}
        self.readers = {}
        self.locklast = {}

    def emit(self):
        nc = self.nc
        for e in ENGS:
            for i, o in enumerate(self.q[e]):
                for (e2, i2) in o.deps:
                    o2 = self.q[e2][i2]
                    if o2.dma_slot is not None:
                        continue
                    if e2 == e and e in ("pe", "sp"):
                        continue
                    o2.signal = True
        for e in ENGS:
            c = 0
            for o in self.q[e]:
                if o.signal:
                    c += 1
                    o.sigval = c
        slots = sorted(self.slot_cnt.keys(), key=str)
        with contextlib.ExitStack() as st:
            esem = {e: st.enter_context(nc.semaphore("s_" + e)) for e in ENGS}
            ssem = {s: st.enter_context(nc.semaphore("d%d" % k)) for k, s in enumerate(slots)}
            block = st.enter_context(nc.Block())
            prog = self

            def run(e, h):
                seen = {}
                for i, o in enumerate(prog.q[e]):
                    need = {}
                    for (e2, i2) in o.deps:
                        o2 = prog.q[e2][i2]
                        if o2.dma_slot is not None:
                            sem = ssem[o2.dma_slot]
                            val = o2.dma_val
                        else:
                            if e2 == e and e in ("pe", "sp"):
                                continue
                            sem = esem[e2]
                            val = o2.sigval
                        k = id(sem)
                        if seen.get(k, 0) >= val:
                            continue
                        if need.get(k, (None, 0))[1] < val:
                            need[k] = (sem, val)
                    for k, (sem, val) in need.items():
                        if os.environ.get("KSKIP") and e == "act" and i == len(prog.q[e]) - 1 and sem.name in os.environ["KSKIP"].split(","):
                            print("SKIPPING WAIT", sem.name, val)
                            continue
                        h.wait_ge(sem, val)
                        seen[k] = val
                        if os.environ.get("KDBG") and (i >= len(prog.q[e]) - 3 or os.environ.get("KDBG") == e):
                            print("WAIT", e, i, sem.name if hasattr(sem, "name") else sem, val)
                    ins = o.fn(h)
                    if o.dma_slot is not None:
                        ins.then_inc(ssem[o.dma_slot], 16)
                    elif o.signal:
                        ins.then_inc(esem[e], 1)

            @block.sync
            def _(h):
                run("sp", h)

            @block.scalar
            def _(h):
                run("act", h)

            @block.vector
            def _(h):
                run("dve", h)

            @block.gpsimd
            def _(h):
                run("pool", h)

            @block.tensor
            def _(h):
                run("pe", h)


WNAMES = ["ln_mix", "w_in", "qn_a", "kn_a", "sink_a", "qn_b", "kn_b", "go_a", "go_b", "w_out",
          "ln_x", "ln_mem", "w_cq", "w_ckv", "cqn", "ckn", "w_co", "ln_ff", "w_pq", "pk1", "pk2",
          "peer_u", "peer_v"]
WSHAPES = {"ln_mix": [D], "w_in": [D, 1536], "qn_a": [64], "kn_a": [64], "sink_a": [8], "qn_b": [64],
           "kn_b": [64], "go_a": [512], "go_b": [512], "w_out": [D, D], "ln_x": [D], "ln_mem": [D],
           "w_cq": [D, 512], "w_ckv": [D, D], "cqn": [128], "ckn": [128], "w_co": [512, D],
           "ln_ff": [D], "w_pq": [D, 2048], "pk1": [8, 128, 128], "pk2": [8, 128, 128],
           "peer_u": [16384, D], "peer_v": [16384, D]}


def build(NP, SP, SS, SQ, KBLK, LVL=4):
    nc = bass.Bass("TRN2", target_bir_lowering=False)
    NTOK = NP * SP + SS
    NT_ALL = NTOK // 128

    def din(name, shape, dt=F32):
        return nc.dram_tensor(name, list(shape), dt, kind="ExternalInput").ap()

    xp = din("xp", [NP, SP, D])
    xs = din("xs", [SS, D])
    memp = din("memp", [NP, 256, D])
    mems = din("mems", [256, D])
    tabs = din("tabs", [NTOK, 4, 64])
    cmask = din("cmask", [4, 128, 512])
    ident_d = din("ident", [128, 128])
    iota_d = din("iota", [128, 256])
    W = {n: din(n, WSHAPES[n]) for n in WNAMES}
    yp = nc.dram_tensor("yp", [NP, SP, D], F32, kind="ExternalOutput").ap()
    ys = nc.dram_tensor("ys", [SQ, D], F32, kind="ExternalOutput").ap()
    KTA_d = nc.dram_tensor("KTA_d", [128, NTOK], BF16, kind="Internal").ap()
    KTB_d = nc.dram_tensor("KTB_d", [128, NTOK], BF16, kind="Internal").ap()
    VA_d = nc.dram_tensor("VA_d", [NT_ALL, 128, 130], BF16, kind="Internal").ap()
    VB_d = nc.dram_tensor("VB_d", [NT_ALL, 128, 130], BF16, kind="Internal").ap()
    UV_d = nc.dram_tensor("UV_d", [16384, 2048], BF16, kind="Internal").ap()

    seqs = []
    for i in range(NP):
        seqs.append((xp[i], memp[i], SP, i * SP, SP // 128, yp[i], False))
    seqs.append((xs, mems, SS, NP * SP, SQ // 128, ys, True))
    NSEQ = len(seqs)

    P = Prog(nc)
    with contextlib.ExitStack() as st:
        st.enter_context(nc.allow_non_contiguous_dma(reason="tiny gain-vector / V-tile layout loads"))

        def sb(name, shape, dt):
            return st.enter_context(nc.sbuf_tensor("sb_" + name, shape, dt))

        def ps(name, shape, dt):
            return st.enter_context(nc.psum_tensor("ps_" + name, shape, dt))

        def dve(fn, r=(), w=()):
            return P.op("dve", fn, r, w)

        def act(fn, r=(), w=()):
            return P.op("act", fn, r, w)

        def pe(fn, r=(), w=()):
            return P.op("pe", fn, r, w)

        def pool(fn, r=(), w=()):
            return P.op("pool", fn, r, w)

        def dma(out, in_, r=(), w=(), slot=None, eng="sp"):
            return P.op(eng, lambda h: h.dma_start(out=out, in_=in_), r, w, slot=slot)

        w_in = sb("w_in", [128, 8, 1536], BF16)
        w_out = sb("w_out", [128, 8, 1024], BF16)
        w_cq = sb("w_cq", [128, 8, 512], BF16)
        w_co = sb("w_co", [128, 4, 1024], BF16)
        w_pq = sb("w_pq", [128, 8, 2048], BF16)
        pkT = sb("pkT", [128, 16, 128], BF16)
        gb = sb("gb", [128, 4, 1024], F32)
        w_ckv = gb[:, :, :].rearrange("p a b -> p (a b)").bitcast(BF16).rearrange("p (c n) -> p c n", c=8)
        gbb = gb[:, :, :].rearrange("p a b -> p (a b)").bitcast(BF16).rearrange("p (s n) -> p s n", s=4)
        dmy = sb("dmy", [128, 8], F32)
        af = sb("af", [128, 128], F32)
        bf_ = sb("bf", [128, 128], F32)
        eb = sb("eb", [128, 128], F32)
        ga = sb("ga", [128, 128], F32)
        ww = sb("ww", [128, 128], F32)
        dg = sb("dg", [128, 4, 128], BF16)
        X = sb("X", [128, 1024], F32)
        f = [sb("f%d" % i, [128, 1024], F32) for i in range(6)]
        b = [sb("b%d" % i, [128, 1024], BF16) for i in range(6)]
        cand = sb("cand", [128, 4, 256], F32)
        candw = sb("candw", [128, 4, 256], F32)
        candix = sb("candix", [128, 4, 256], F32)
        lnff = sb("lnff", [128, 1024], F32)
        KB = [sb("KB%d" % i, [128, KBLK], BF16) for i in range(2)]
        VB = [sb("VB%d" % i, [128, KBLK // 128, 130], BF16) for i in range(2)]
        KA = sb("KA", [128, 3, 128], BF16)
        VA = sb("VA", [128, 3, 130], BF16)
        memK = sb("memK", [128, 4, 256], BF16)
        memV = sb("memV", [128, 2, 4, 129], BF16)
        idb = sb("idb", [128, 128], BF16)
        idf = sb("idf", [128, 128], F32)
        masks = sb("masks", [128, 4, 512], BF16)
        gT = sb("gT", [128, 6, 8], F32)
        GQ = sb("GQ", [128, 2, 64], F32)
        GK = sb("GK", [128, 2, 64], F32)
        GCQ = sb("GCQ", [128, 1, 128], F32)
        GCK = sb("GCK", [128, 1, 128], F32)
        sinkb = sb("sinkb", [128, 8], F32)
        sinkexp = sb("sinkexp", [128, 8], F32)
        negc = sb("negc", [128, 4], F32)
        tab = sb("tab", [128, 4, 64], F32)
        sm = sb("sm", [128, 64], F32)
        sm2 = sb("sm2", [128, 64], F32)
        v1 = sb("v1", [128, 8, 16], F32)
        v2 = sb("v2", [128, 8, 16], F32)
        i1 = sb("i1", [128, 8, 16], U32)
        i2 = sb("i2", [128, 8, 16], U32)
        i1f = sb("i1f", [128, 8, 16], F32)
        i2f = sb("i2f", [128, 8, 16], F32)
        sc = sb("sc", [128, 8, 16], F32)
        ef = sb("ef", [128, 128], F32)
        icu = sb("icu", [128, 8, 16], U32)
        icf = sb("icf", [128, 8, 16], F32)
        iota = sb("iota", [128, 256], F32)
        eu = sb("eu", [128, 128], U32)
        gw = sb("gw", [128, 128], F32)
        aa = sb("aa", [128, 128], F32)
        psA = ps("psA", [128, 2, 512], F32)
        psT = ps("psT", [128, 8, 128], BF16)
        psS = [ps("psS%d" % i, [128, 2, 512], F32) for i in range(2)]
        psO = ps("psO", [128, 512], F32)

        fn_ = ["f%d" % i for i in range(6)]
        bn_ = ["b%d" % i for i in range(6)]

        dma(idf[:], ident_d, w=["idf"], slot="c0")
        dma(iota[:], iota_d, w=["iota"], slot="c1")
        dve(lambda h: h.tensor_copy(out=idb[:], in_=idf[:]), r=["idf"], w=["idb"])
        for k in range(4):
            dma(f[k][:, 0:512], cmask[k], w=[fn_[k]], slot="c%d" % (1 + k))
            dve(lambda h, k=k: h.tensor_copy(out=masks[:, k, :], in_=f[k][:, 0:512]), r=[fn_[k]], w=["masks"])
        for k, nm in enumerate(["ln_mix", "go_a", "ln_x", "ln_mem", "ln_ff"]):
            if nm == "go_a":
                dma(gT[:, 1, 0:4].unsqueeze(2), W["go_a"].rearrange("(c p o) -> p c o", p=128, o=1), w=["gT"], slot="c5")
                dma(gT[:, 1, 4:8].unsqueeze(2), W["go_b"].rearrange("(c p o) -> p c o", p=128, o=1), w=["gT"], slot="c6")
            else:
                dma(gT[:, k, :].unsqueeze(2), W[nm].rearrange("(c p o) -> p c o", p=128, o=1), w=["gT"], slot="c5")
        dma(lnff[:], W["ln_ff"].partition_broadcast(128), w=["lnff"], slot="c6")
        dma(GQ[:, 0, :], W["qn_a"].partition_broadcast(128), w=["GQ"], slot="c5")
        dma(GQ[:, 1, :], W["qn_b"].partition_broadcast(128), w=["GQ"], slot="c6")
        dma(GK[:, 0, :], W["kn_a"].partition_broadcast(128), w=["GK"], slot="c5")
        dma(GK[:, 1, :], W["kn_b"].partition_broadcast(128), w=["GK"], slot="c6")
        dma(GCQ[:, 0, :], W["cqn"].partition_broadcast(128), w=["GCQ"], slot="c5")
        dma(GCK[:, 0, :], W["ckn"].partition_broadcast(128), w=["GCK"], slot="c6")
        dma(sinkb[:], W["sink_a"].partition_broadcast(128), w=["sinkb"], slot="c5")

        def absmax(dst, src, r):
            dve(lambda h: h.tensor_reduce(out=dst, in_=src, axis=AX.X, op=ALU.max, apply_absolute_value=True),
                r=r, w=["sm"])
        absmax(sm[:, 0:1], GQ[:, 0, :], ["GQ"])
        absmax(sm[:, 1:2], GK[:, 0, :], ["GK"])
        absmax(sm[:, 2:3], GQ[:, 1, :], ["GQ"])
        absmax(sm[:, 3:4], GK[:, 1, :], ["GK"])
        absmax(sm[:, 4:5], GCQ[:, 0, :], ["GCQ"])
        absmax(sm[:, 5:6], GCK[:, 0, :], ["GCK"])
        dve(lambda h: h.tensor_reduce(out=sm[:, 6:7], in_=sinkb[:], axis=AX.X, op=ALU.max), r=["sinkb"], w=["sm"])
        dve(lambda h: h.tensor_tensor(out=sm[:, 8:9], in0=sm[:, 0:1], in1=sm[:, 1:2], op=ALU.mult), r=["sm"], w=["sm"])
        dve(lambda h: h.tensor_tensor(out=sm[:, 9:10], in0=sm[:, 2:3], in1=sm[:, 3:4], op=ALU.mult), r=["sm"], w=["sm"])
        dve(lambda h: h.tensor_tensor(out=sm[:, 10:11], in0=sm[:, 4:5], in1=sm[:, 5:6], op=ALU.mult), r=["sm"], w=["sm"])
        dve(lambda h: h.tensor_scalar(out=sm[:, 8:10], in0=sm[:, 8:10], scalar1=8.0, scalar2=None, op0=ALU.mult), r=["sm"], w=["sm"])
        dve(lambda h: h.tensor_scalar(out=sm[:, 10:11], in0=sm[:, 10:11], scalar1=float(np.sqrt(128.0)), scalar2=None, op0=ALU.mult), r=["sm"], w=["sm"])
        dve(lambda h: h.tensor_tensor(out=sm[:, 8:9], in0=sm[:, 8:9], in1=sm[:, 6:7], op=ALU.max), r=["sm"], w=["sm"])
        dve(lambda h: h.tensor_scalar(out=negc[:, 0:3], in0=sm[:, 8:11], scalar1=-1.0, scalar2=None, op0=ALU.mult), r=["sm"], w=["negc"])
        act(lambda h: h.activation(out=sinkexp[:], in_=sinkb[:], func=AF.Exp, bias=negc[:, 0:1], scale=1.0),
            r=["sinkb", "negc"], w=["sinkexp"])

        P.mark("consts")
        uv_next = [0]

        def uv_iter():
            rblk = uv_next[0]
            if rblk >= 128 or LVL < 4 or os.environ.get("KNOUV"):
                return
            uv_next[0] += 1
            k2 = rblk % 2
            r0 = rblk * 128
            ub, ubt = (cand, "cand") if k2 == 0 else (candix, "candix")
            ubf = ub[:, :, :].rearrange("p a b -> p (a b)")
            vbf, vbt = (candw[:, :, :].rearrange("p a b -> p (a b)"), "candw") if k2 == 0 else (f[5][:, :], "f5")
            dma(ubf, W["peer_u"][r0:r0 + 128, :], w=[ubt], slot="uvl%d" % (2 * k2), eng="pool")
            dma(vbf, W["peer_v"][r0:r0 + 128, :], w=[vbt], slot="uvl%d" % (2 * k2 + 1), eng="pool")
            pool(lambda h: h.tensor_copy(out=gbb[:, k2, 0:1024], in_=ubf), r=[ubt], w=["gb%d" % k2])
            pool(lambda h: h.tensor_copy(out=gbb[:, k2, 1024:2048], in_=vbf), r=[vbt], w=["gb%d" % k2])
            dma(UV_d[r0:r0 + 128, :], gbb[:, k2, :], r=["gb%d" % k2], slot="uvs%d" % k2, eng="pool")
        stg_i = [0]

        def load_w(dst, src, K, N, grow):
            for c in range(K // 128):
                for n0 in range(0, N, 1024):
                    nn = min(1024, N - n0)
                    k = stg_i[0] % 2
                    stg_i[0] += 1
                    dma(f[k][:, 0:nn], src[c * 128:(c + 1) * 128, n0:n0 + nn], w=[fn_[k]], slot="stg%d" % k)
                    if grow is None:
                        dve(lambda h, k=k, c=c, n0=n0, nn=nn: h.tensor_copy(out=dst[:, c, n0:n0 + nn], in_=f[k][:, 0:nn]),
                            r=[fn_[k]], w=["W"])
                    else:
                        dve(lambda h, k=k, c=c, n0=n0, nn=nn: h.tensor_scalar(
                            out=dst[:, c, n0:n0 + nn], in0=f[k][:, 0:nn], scalar1=gT[:, grow, c:c + 1], scalar2=None, op0=ALU.mult),
                            r=[fn_[k], "gT"], w=["W"])
        load_w(w_in, W["w_in"], 1024, 1536, 0)
        load_w(w_out, W["w_out"], 1024, 1024, 1)
        load_w(w_cq, W["w_cq"], 1024, 512, 2)
        load_w(w_co, W["w_co"], 512, 1024, None)
        load_w(w_pq, W["w_pq"], 1024, 2048, 4)
        for side, nm in enumerate(["pk1", "pk2"]):
            dma(f[2 + side][:, :].rearrange("p (h d) -> p h d", h=8), W[nm].rearrange("h n d -> n h d"),
                w=[fn_[2 + side]], slot="c%d" % (1 + side))
            dve(lambda h, side=side: h.tensor_copy(out=b[side][:, :], in_=f[2 + side][:, :]), r=[fn_[2 + side]], w=[bn_[side]])
            for hh in range(8):
                pe(lambda h, side=side, hh=hh: h.transpose(out=psT[:, hh, :], in_=b[side][:, hh * 128:(hh + 1) * 128], identity=idb[:]),
                   r=[bn_[side], "idb"], w=["psT"])
            act(lambda h, side=side: h.activation(out=pkT[:, :, :].rearrange("p (h s) k -> p s h k", s=2)[:, side, :, :],
                                                   in_=psT[:, :, :], func=AF.Copy), r=["psT"], w=["W"])

        P.mark("weights")
        def rstd_from_ss(dst, src, n, rtag, wtag):
            act(lambda h: h.activation(out=dst, in_=src, func=AF.Ln, scale=1.0 / n, bias=EPS), r=[rtag], w=[wtag])
            act(lambda h: h.activation(out=dst, in_=dst, func=AF.Ln if False else AF.Exp, scale=-0.5), r=[wtag], w=[wtag])

        def norm_to_hT(src, srctag, extra_f32=None):
            act(lambda h: h.activation(out=f[0][:], in_=src, func=AF.Square, accum_out=sm[:, 16:17]), r=[srctag], w=["f0", "sm"])
            rstd_from_ss(sm[:, 17:18], sm[:, 16:17], 1024.0, "sm", "sm")
            dve(lambda h: h.tensor_scalar(out=b[0][:], in0=src, scalar1=sm[:, 17:18], scalar2=None, op0=ALU.mult),
                r=[srctag, "sm"], w=["b0"])
            if extra_f32 is not None:
                dve(lambda h: h.scalar_tensor_tensor(out=extra_f32, in0=src, scalar=sm[:, 17:18], in1=lnff[:],
                                                     op0=ALU.mult, op1=ALU.mult), r=[srctag, "sm", "lnff"], w=["f1"])
            for c in range(8):
                pe(lambda h, c=c: h.transpose(out=psT[:, c, :], in_=b[0][:, c * 128:(c + 1) * 128], identity=idb[:]),
                   r=["b0", "idb"], w=["psT"])
            act(lambda h: h.activation(out=b[1][:, :], in_=psT[:, :, :].rearrange("p a b -> p (a b)"), func=AF.Copy), r=["psT"], w=["b1"])

        hT = b[1][:, :].rearrange("p (c t) -> p c t", c=8)

        def proj(dst_ps, wt, nK, n0, nn, dtag):
            for c in range(nK):
                pe(lambda h, c=c: h.matmul(dst_ps, lhsT=hT[:, c, :], rhs=wt[:, c, n0:n0 + nn], start=(c == 0), stop=(c == nK - 1)),
                   r=["b1", "W"], w=[dtag])

        def headnorm_rope(zsrc, ztag, NH, G, Gtag, tsel, dst, dtag, perm):
            n = NH * 64
            half = 32 if tsel == 0 else 16
            nb = 64 // (2 * half)
            act(lambda h: h.activation(out=f[0][:, 0:n], in_=zsrc, func=AF.Square), r=[ztag], w=["f0"])
            dve(lambda h: h.tensor_reduce(out=sm[:, 20:20 + NH], in_=f[0][:, 0:n].rearrange("p (h d) -> p h d", d=64), axis=AX.X, op=ALU.add),
                r=["f0"], w=["sm"])
            rstd_from_ss(sm[:, 20:20 + NH], sm[:, 20:20 + NH], 64.0, "sm", "sm")
            dve(lambda h: h.tensor_tensor(out=f[1][:, 0:n].rearrange("p (h d) -> p h d", d=64), in0=zsrc.rearrange("p (h d) -> p h d", d=64),
                                          in1=sm[:, 20:20 + NH].unsqueeze(2).to_broadcast([128, NH, 64]), op=ALU.mult),
                r=[ztag, "sm"], w=["f1"])
            dve(lambda h: h.tensor_tensor(out=f[1][:, 0:n].rearrange("p (h d) -> p h d", d=64), in0=f[1][:, 0:n].rearrange("p (h d) -> p h d", d=64),
                                          in1=G.unsqueeze(1).to_broadcast([128, NH, 64]), op=ALU.mult), r=["f1", Gtag], w=["f1"])
            y4 = f[1][:, 0:n].rearrange("p (h b t f) -> p h b t f", h=NH, b=nb, t=2)
            t24 = f[3][:, 0:n].rearrange("p (h b t f) -> p h b t f", h=NH, b=nb, t=2)
            C3 = tab[:, tsel, :].unsqueeze(1).to_broadcast([128, NH, 64])
            S4 = tab[:, tsel + 1, :].rearrange("p (b t f) -> p b t f", b=nb, t=2)
            dve(lambda h: h.tensor_tensor(out=f[2][:, 0:n].rearrange("p (h d) -> p h d", d=64), in0=f[1][:, 0:n].rearrange("p (h d) -> p h d", d=64),
                                          in1=C3, op=ALU.mult), r=["f1", "tab"], w=["f2"])
            for t in range(2):
                dve(lambda h, t=t: h.tensor_tensor(out=t24[:, :, :, t, :], in0=y4[:, :, :, 1 - t, :],
                                                   in1=S4[:, :, t, :].unsqueeze(1).to_broadcast([128, NH, nb, half]), op=ALU.mult),
                    r=["f1", "tab"], w=["f3"])
            if perm:
                o3 = dst.rearrange("p (g k d) -> p k g d", g=4, k=2)
                a3 = f[2][:, 0:n].rearrange("p (k g d) -> p k g d", k=2, g=4)
                c3 = f[3][:, 0:n].rearrange("p (k g d) -> p k g d", k=2, g=4)
                for kv in range(2):
                    dve(lambda h, kv=kv: h.tensor_tensor(out=o3[:, kv], in0=a3[:, kv], in1=c3[:, kv], op=ALU.add), r=["f2", "f3"], w=[dtag])
            else:
                dve(lambda h: h.tensor_tensor(out=dst, in0=f[2][:, 0:n], in1=f[3][:, 0:n], op=ALU.add), r=["f2", "f3"], w=[dtag])

        P.barrier()
        for si, (xap, memap, S, tok0, nq, oap, is_s) in enumerate(seqs):
            for ti in range(S // 128):
                t0 = tok0 + ti * 128
                gt = t0 // 128
                P.mark("pro_tile_%d_%d" % (si, ti))
                uv_iter()
                dma(X[:], xap[ti * 128:(ti + 1) * 128, :], w=["X"], slot="x")
                dma(tab[:], tabs[t0:t0 + 128], w=["tab"], slot="tab")
                norm_to_hT(X[:], "X")
                P.mark("pro_normT")
                proj(psA[:, 0, 0:256], w_in, 8, 512, 256, "psA")
                proj(psA[:, 1, 0:256], w_in, 8, 1280, 256, "psA")
                for ty in range(2):
                    z = psA[:, ty, 0:128]
                    headnorm_rope(z, "psA", 2, GK[:, ty, :], "GK", 2 * ty,
                                  b[2][:, ty * 128:(ty + 1) * 128], "b2", False)
                    pe(lambda h, ty=ty: h.transpose(out=psT[:, ty, :], in_=b[2][:, ty * 128:(ty + 1) * 128], identity=idb[:]),
                       r=["b2", "idb"], w=["psT"])
                    vdst = b[3][:, ty * 130:(ty + 1) * 130].rearrange("p (k c) -> p k c", k=2)
                    act(lambda h, ty=ty, vdst=vdst: h.activation(out=vdst[:, :, 0:64], in_=psA[:, ty, 128:256].rearrange("p (k c) -> p k c", k=2),
                                                                  func=AF.Copy), r=["psA"], w=["b3"])
                    pool(lambda h, vdst=vdst: h.memset(vdst[:, :, 64:65], 1.0), w=["b3"])
                act(lambda h: h.activation(out=b[4][:, 0:256], in_=psT[:, 0:2, :].rearrange("p a b -> p (a b)"), func=AF.Copy), r=["psT"], w=["b4"])
                dma(KTA_d[:, t0:t0 + 128], b[4][:, 0:128], r=["b4"], slot="st0")
                dma(KTB_d[:, t0:t0 + 128], b[4][:, 128:256], r=["b4"], slot="st1")
                dma(VA_d[gt], b[3][:, 0:130], r=["b3"], slot="st2")
                dma(VB_d[gt], b[3][:, 130:260], r=["b3"], slot="st3")
        while uv_next[0] < 128 and LVL >= 4 and not os.environ.get("KNOUV"):
            uv_iter()
        P.barrier()

        def mem_kv(memap):
            for mt in range(2):
                dma(X[:], memap[mt * 128:(mt + 1) * 128, :], w=["X"], slot="x")
                norm_to_hT(X[:], "X")
                proj(psA[:, 0, :], w_ckv, 8, 0, 512, "psA")
                proj(psA[:, 1, :], w_ckv, 8, 512, 512, "psA")
                act(lambda h: h.activation(out=f[0][:, 0:512], in_=psA[:, 0, :], func=AF.Square), r=["psA"], w=["f0"])
                dve(lambda h: h.tensor_reduce(out=sm[:, 20:24], in_=f[0][:, 0:512].rearrange("p (h d) -> p h d", d=128), axis=AX.X, op=ALU.add),
                    r=["f0"], w=["sm"])
                rstd_from_ss(sm[:, 20:24], sm[:, 20:24], 128.0, "sm", "sm")
                dve(lambda h: h.tensor_tensor(out=f[1][:, 0:512].rearrange("p (h d) -> p h d", d=128), in0=psA[:, 0, :].rearrange("p (h d) -> p h d", d=128),
                                              in1=sm[:, 20:24].unsqueeze(2).to_broadcast([128, 4, 128]), op=ALU.mult), r=["psA", "sm"], w=["f1"])
                dve(lambda h: h.tensor_tensor(out=b[2][:, 0:512].rearrange("p (h d) -> p h d", d=128), in0=f[1][:, 0:512].rearrange("p (h d) -> p h d", d=128),
                                              in1=GCK[:, :, :].to_broadcast([128, 4, 128]), op=ALU.mult),
                    r=["f1", "GCK"], w=["b2"])
                for hh in range(4):
                    pe(lambda h, hh=hh: h.transpose(out=psT[:, hh, :], in_=b[2][:, hh * 128:(hh + 1) * 128], identity=idb[:]),
                       r=["b2", "idb"], w=["psT"])
                act(lambda h, mt=mt: h.activation(out=memK[:, :, mt * 128:(mt + 1) * 128], in_=psT[:, 0:4, :], func=AF.Copy), r=["psT"], w=["memK"])
                act(lambda h, mt=mt: h.activation(out=memV[:, mt, :, 0:128], in_=psA[:, 1, :].rearrange("p (h d) -> p h d", d=128), func=AF.Copy),
                    r=["psA"], w=["memV"])
                pool(lambda h, mt=mt: h.memset(memV[:, mt, :, 128:129], 1.0), w=["memV"])

        st_par = [0]
        for si, (xap, memap, S, tok0, nq, oap, is_s) in enumerate(seqs):
            NT = S // 128
            KT = KBLK // 128
            nblk = S // KBLK
            P.mark("main_seq_%d" % si)
            P.barrier()
            load_w(w_ckv, W["w_ckv"], 1024, 1024, 3)
            mem_kv(memap)
            P.barrier()
            for qi in range(nq):
                t0 = tok0 + qi * 128
                dma(X[:], xap[qi * 128:(qi + 1) * 128, :], w=["X"], slot="x")
                dma(tab[:], tabs[t0:t0 + 128], w=["tab"], slot="tab")
                P.mark("main_tile_%d_%d" % (si, qi))
                norm_to_hT(X[:], "X")
                proj(psA[:, 0, :], w_in, 8, 0, 512, "psA")
                proj(psA[:, 1, :], w_in, 8, 768, 512, "psA")
                for ty in range(2):
                    headnorm_rope(psA[:, ty, :], "psA", 8, GQ[:, ty, :], "GQ", 2 * ty,
                                  b[2][:, ty * 512:(ty + 1) * 512], "b2", True)
                for c in range(8):
                    pe(lambda h, c=c: h.transpose(out=psT[:, c, :], in_=b[2][:, c * 128:(c + 1) * 128], identity=idb[:]),
                       r=["b2", "idb"], w=["psT"])
                act(lambda h: h.activation(out=b[3][:, :], in_=psT[:, :, :].rearrange("p a b -> p (a b)"), func=AF.Copy), r=["psT"], w=["b3"])
                QT = b[3]
                P.mark("QT")

                def attn_units(units, accs, tsel):
                    cnt = len(units)
                    started = {}
                    def scores(u, bi):
                        kv, tiles, pre = units[u]
                        for jj, (kfn, vfn, msel, ktag, vtag) in enumerate(tiles):
                            pe(lambda h, kv=kv, kfn=kfn, jj=jj, bi=bi: h.matmul(
                                psS[bi][:, jj, :], lhsT=kfn(kv), rhs=QT[kv * 64:(kv + 1) * 64, tsel * 512:(tsel + 1) * 512], start=True, stop=True),
                               r=["b3", ktag], w=["psS%d" % bi])
                        nt = len(tiles)
                        act(lambda h, bi=bi, nt=nt: h.activation(out=b[4 + bi][:, 0:nt * 512], in_=psS[bi][:, 0:nt, :].rearrange("p a b -> p (a b)"),
                                                                  func=AF.Exp, bias=negc[:, tsel:tsel + 1], scale=0.125),
                            r=["psS%d" % bi, "negc"], w=[bn_[4 + bi]])
                        for jj, (kfn, vfn, msel, ktag, vtag) in enumerate(tiles):
                            if msel is not None:
                                pool(lambda h, bi=bi, jj=jj, msel=msel: h.tensor_tensor(out=b[4 + bi][:, jj * 512:(jj + 1) * 512], in0=b[4 + bi][:, jj * 512:(jj + 1) * 512],
                                                                                        in1=masks[:, msel, :], op=ALU.mult),
                                     r=[bn_[4 + bi], "masks"], w=[bn_[4 + bi]])
                    def pv(u, bi):
                        kv, tiles, pre = units[u]
                        acc, atag = accs[kv]
                        for jj, (kfn, vfn, msel, ktag, vtag) in enumerate(tiles):
                            first = kv not in started
                            started[kv] = True
                            last = all(units[u2][0] != kv for u2 in range(u + 1, cnt)) and jj == len(tiles) - 1
                            pe(lambda h, kv=kv, vfn=vfn, jj=jj, bi=bi, first=first, last=last, acc=acc: h.matmul(
                                acc, lhsT=vfn(kv), rhs=b[4 + bi][:, jj * 512:(jj + 1) * 512], start=first, stop=last),
                               r=[bn_[4 + bi], vtag], w=[atag])
                    for u in range(cnt):
                        scores(u, u % 2)
                        if u > 0:
                            pv(u - 1, (u - 1) % 2)
                        if units[u][2] is not None:
                            units[u][2]()
                    pv(cnt - 1, (cnt - 1) % 2)

                def finish_attn(accs, ocol, with_sink):
                    for kv in range(2):
                        acc, atag = accs[kv]
                        act(lambda h, kv=kv, acc=acc: h.activation(out=f[4][0:65, kv * 512:(kv + 1) * 512], in_=acc, func=AF.Copy), r=[atag], w=["f4"])
                    for kv in range(2):
                        for g in range(4):
                            pe(lambda h, kv=kv, g=g: h.transpose(out=psA[:, kv, g * 65:(g + 1) * 65], in_=f[4][0:65, kv * 512 + g * 128:kv * 512 + (g + 1) * 128],
                                                                  identity=idf[0:65, 0:65]), r=["f4", "idf"], w=["psA"])
                    pv4 = psA[:, :, 0:260].rearrange("p k (g c) -> p k g c", g=4)
                    dve(lambda h: h.tensor_copy(out=sm2[:, 0:8].rearrange("p (k g) -> p k g", k=2), in_=pv4[:, :, :, 64]), r=["psA"], w=["sm2"])
                    if with_sink:
                        dve(lambda h: h.tensor_tensor(out=sm2[:, 0:8], in0=sm2[:, 0:8], in1=sinkexp[:], op=ALU.add), r=["sm2", "sinkexp"], w=["sm2"])
                    dve(lambda h: h.reciprocal(out=sm2[:, 8:16], in_=sm2[:, 0:8]), r=["sm2"], w=["sm2"])
                    for kv in range(2):
                        dve(lambda h, kv=kv: h.tensor_tensor(out=f[5][:, ocol + kv * 256:ocol + (kv + 1) * 256].rearrange("p (g d) -> p g d", g=4),
                                                             in0=pv4[:, kv, :, 0:64],
                                                             in1=sm2[:, 8 + kv * 4:12 + kv * 4].unsqueeze(2).to_broadcast([128, 4, 64]), op=ALU.mult),
                            r=["psA", "sm2"], w=["f5"])

                if is_s:
                    jl = [((qi - 1) % NT, 2 if qi == 0 else 0), (qi, None), ((qi + 1) % NT, 3 if qi == nq - 1 else 1)]
                else:
                    jl = []
                    if qi > 0:
                        jl.append((qi - 1, 0))
                    jl.append((qi, None))
                    if qi < NT - 1:
                        jl.append((qi + 1, 1))
                for jj, (kt, msel) in enumerate(jl):
                    kt0 = tok0 + kt * 128
                    dma(KA[:, jj, :], KTA_d[:, kt0:kt0 + 128], w=["KA%d" % jj], slot="ka%d" % jj)
                    dma(VA[:, jj, :], VA_d[kt0 // 128], w=["VA%d" % jj], slot="va%d" % jj)
                unitsA = []
                for kv in range(2):
                    tl = [((lambda kv, jj=jj: KA[kv * 64:(kv + 1) * 64, jj, :]), (lambda kv, jj=jj: VA[:, jj, kv * 65:(kv + 1) * 65]), msel,
                           "KA%d" % jj, "VA%d" % jj) for jj, (kt, msel) in enumerate(jl)]
                    unitsA.append((kv, tl[0:2], None))
                    if len(tl) > 2:
                        unitsA.append((kv, tl[2:3], None))
                accs = {0: (psO[0:65, :], "psO"), 1: (psA[0:65, 0, :], "psA")}
                attn_units(unitsA, accs, 0)
                P.mark("attnA_units")
                finish_attn(accs, 0, True)
                P.mark("attnA_fin")
                def load_blk(blk):
                    bb = blk % 2
                    k0 = tok0 + blk * KBLK
                    dma(KB[bb][:, :], KTB_d[:, k0:k0 + KBLK], w=["KB%d" % bb], slot="kb%d" % bb)
                    dma(VB[bb][:, :, :], VB_d[k0 // 128:k0 // 128 + KT].rearrange("j p c -> p j c"), w=["VB%d" % bb], slot="vb%d" % bb)
                load_blk(0)
                unitsB = []
                for blk in range(nblk):
                    bb = blk % 2
                    firstu = True
                    for kv in range(2):
                        for j0 in range(0, KT, 2):
                            tl = [((lambda kv, bb=bb, j=j: KB[bb][kv * 64:(kv + 1) * 64, j * 128:(j + 1) * 128]),
                                   (lambda kv, bb=bb, j=j: VB[bb][:, j, kv * 65:(kv + 1) * 65]), None, "KB%d" % bb, "VB%d" % bb)
                                  for j in range(j0, min(j0 + 2, KT))]
                            pre = None
                            if firstu and blk + 1 < nblk:
                                pre = (lambda blk=blk: load_blk(blk + 1))
                            firstu = False
                            unitsB.append((kv, tl, pre))
                attn_units(unitsB, accs, 1)
                finish_attn(accs, 512, False)
                P.mark("attnB_fin")

                act(lambda h: h.activation(out=f[0][:, 0:512], in_=f[5][:, 0:512], func=AF.Square, accum_out=sm[:, 30:31]), r=["f5"], w=["f0", "sm"])
                act(lambda h: h.activation(out=f[0][:, 512:1024], in_=f[5][:, 512:1024], func=AF.Square, accum_out=sm[:, 31:32]), r=["f5"], w=["f0", "sm"])
                rstd_from_ss(sm[:, 30:32], sm[:, 30:32], 512.0, "sm", "sm")
                dve(lambda h: h.tensor_tensor(out=b[0][:, :].rearrange("p (a d) -> p a d", a=2), in0=f[5][:, :].rearrange("p (a d) -> p a d", a=2),
                                              in1=sm[:, 30:32].unsqueeze(2).to_broadcast([128, 2, 512]), op=ALU.mult), r=["f5", "sm"], w=["b0"])
                for c in range(8):
                    pe(lambda h, c=c: h.transpose(out=psT[:, c, :], in_=b[0][:, c * 128:(c + 1) * 128], identity=idb[:]),
                       r=["b0", "idb"], w=["psT"])
                act(lambda h: h.activation(out=b[1][:, :], in_=psT[:, :, :].rearrange("p a b -> p (a b)"), func=AF.Copy), r=["psT"], w=["b1"])
                proj(psA[:, 0, :], w_out, 8, 0, 512, "psA")
                proj(psA[:, 1, :], w_out, 8, 512, 512, "psA")
                dve(lambda h: h.tensor_tensor(out=X[:], in0=X[:], in1=psA[:, :, :].rearrange("p a b -> p (a b)"), op=ALU.add), r=["X", "psA"], w=["X"])

                for _lvl2 in ([0] if LVL >= 2 else []):
                    norm_to_hT(X[:], "X")
                    proj(psA[:, 0, :], w_cq, 8, 0, 512, "psA")
                    act(lambda h: h.activation(out=f[0][:, 0:512], in_=psA[:, 0, :], func=AF.Square), r=["psA"], w=["f0"])
                    dve(lambda h: h.tensor_reduce(out=sm[:, 20:24], in_=f[0][:, 0:512].rearrange("p (h d) -> p h d", d=128), axis=AX.X, op=ALU.add),
                        r=["f0"], w=["sm"])
                    rstd_from_ss(sm[:, 20:24], sm[:, 20:24], 128.0, "sm", "sm")
                    dve(lambda h: h.tensor_tensor(out=f[1][:, 0:512].rearrange("p (h d) -> p h d", d=128), in0=psA[:, 0, :].rearrange("p (h d) -> p h d", d=128),
                                                  in1=sm[:, 20:24].unsqueeze(2).to_broadcast([128, 4, 128]), op=ALU.mult), r=["psA", "sm"], w=["f1"])
                    dve(lambda h: h.tensor_tensor(out=b[2][:, 0:512].rearrange("p (h d) -> p h d", d=128), in0=f[1][:, 0:512].rearrange("p (h d) -> p h d", d=128),
                                                  in1=GCQ[:, :, :].to_broadcast([128, 4, 128]), op=ALU.mult),
                        r=["f1", "GCQ"], w=["b2"])
                    for hh in range(4):
                        pe(lambda h, hh=hh: h.transpose(out=psT[:, hh, :], in_=b[2][:, hh * 128:(hh + 1) * 128], identity=idb[:]),
                           r=["b2", "idb"], w=["psT"])
                    act(lambda h: h.activation(out=b[3][:, 0:512], in_=psT[:, 0:4, :].rearrange("p a b -> p (a b)"), func=AF.Copy), r=["psT"], w=["b3"])
                    for mt in range(2):
                        for hh in range(4):
                            pe(lambda h, mt=mt, hh=hh: h.matmul(psS[0][:, mt, hh * 128:(hh + 1) * 128], lhsT=memK[:, hh, mt * 128:(mt + 1) * 128],
                                                                rhs=b[3][:, hh * 128:(hh + 1) * 128], start=True, stop=True),
                               r=["memK", "b3"], w=["psS0"])
                    act(lambda h: h.activation(out=b[4][:, :], in_=psS[0][:, :, :].rearrange("p a b -> p (a b)"), func=AF.Exp,
                                               bias=negc[:, 2:3], scale=float(128.0 ** -0.5)), r=["psS0", "negc"], w=["b4"])
                    for hh in range(4):
                        for mt in range(2):
                            pe(lambda h, mt=mt, hh=hh: h.matmul(psA[:, hh // 2, (hh % 2) * 129:(hh % 2 + 1) * 129],
                                                                lhsT=b[4][:, mt * 512 + hh * 128:mt * 512 + (hh + 1) * 128],
                                                                rhs=memV[:, mt, hh, :], start=(mt == 0), stop=(mt == 1)),
                               r=["b4", "memV"], w=["psA"])
                    px = psA[:, :, 0:258].rearrange("p a (h c) -> p a h c", h=2)
                    dve(lambda h: h.reciprocal(out=sm2[:, 16:20].rearrange("p (a h) -> p a h", a=2), in_=px[:, :, :, 128]), r=["psA"], w=["sm2"])
                    for a in range(2):
                        dve(lambda h, a=a: h.tensor_tensor(out=b[0][:, a * 256:(a + 1) * 256].rearrange("p (h d) -> p h d", h=2), in0=px[:, a, :, 0:128],
                                                           in1=sm2[:, 16 + 2 * a:18 + 2 * a].unsqueeze(2).to_broadcast([128, 2, 128]), op=ALU.mult),
                            r=["psA", "sm2"], w=["b0"])
                    for c in range(4):
                        pe(lambda h, c=c: h.transpose(out=psT[:, c, :], in_=b[0][:, c * 128:(c + 1) * 128], identity=idb[:]),
                           r=["b0", "idb"], w=["psT"])
                    act(lambda h: h.activation(out=b[1][:, 0:512], in_=psT[:, 0:4, :].rearrange("p a b -> p (a b)"), func=AF.Copy), r=["psT"], w=["b1"])
                    proj(psA[:, 0, :], w_co, 4, 0, 512, "psA")
                    proj(psA[:, 1, :], w_co, 4, 512, 512, "psA")
                    dve(lambda h: h.tensor_tensor(out=X[:], in0=X[:], in1=psA[:, :, :].rearrange("p a b -> p (a b)"), op=ALU.add), r=["X", "psA"], w=["X"])

                for _lvl3 in ([0] if LVL >= 3 else []):
                    norm_to_hT(X[:], "X", extra_f32=f[1][:])
                    for half in range(2):
                        for nb in range(8):
                            for c in range(8):
                                pe(lambda h, half=half, nb=nb, c=c: h.matmul(psA[:, nb // 4, (nb % 4) * 128:(nb % 4 + 1) * 128],
                                                                              lhsT=w_pq[:, c, (half * 8 + nb) * 128:(half * 8 + nb + 1) * 128],
                                                                              rhs=hT[:, c, :], start=(c == 0), stop=(c == 7)),
                                   r=["W", "b1"], w=["psA"])
                        act(lambda h, half=half: h.activation(out=b[2 + half][:, :], in_=psA[:, :, :].rearrange("p a b -> p (a b)"), func=AF.Copy),
                            r=["psA"], w=[bn_[2 + half]])
                    for hh in range(8):
                        for side in range(2):
                            blk = 2 * hh + side
                            src = b[2 + blk // 8][:, (blk % 8) * 128:(blk % 8 + 1) * 128]
                            pe(lambda h, hh=hh, side=side, src=src: h.matmul(psS[side][:, hh // 4, (hh % 4) * 128:(hh % 4 + 1) * 128], lhsT=src,
                                                                             rhs=pkT[:, 2 * hh + side, :], start=True, stop=True),
                               r=["b2", "b3", "W"], w=["psS%d" % side])
                    for side in range(2):
                        act(lambda h, side=side: h.activation(out=f[2 + side][:, :], in_=psS[side][:, :, :].rearrange("p a b -> p (a b)"), func=AF.Copy),
                            r=["psS%d" % side], w=[fn_[2 + side]])
                    def ct(nm, side, hh):
                        return "%s%d_%d" % (nm, side, hh)
                    chains = [(side, hh) for hh in range(8) for side in range(2)]
                    wktags = [ct("wk", side, hh) for (side, hh) in chains]
                    dve(lambda h: h.memset(dmy[:, 0:1], 0.0), w=["f4", "f5"] + wktags)
                    for step in range(5):
                        for (side, hh) in chains:
                            S_ = f[2 + side]
                            Wk = f[4 + side]
                            vv = v1 if side == 0 else v2
                            ii = i1 if side == 0 else i2
                            sl = slice(hh * 128, (hh + 1) * 128)
                            if step == 0:
                                dve(lambda h, S_=S_, vv=vv, hh=hh, sl=sl: h.max(out=vv[:, hh, 0:8], in_=S_[:, sl]), r=[fn_[2 + side]], w=[ct("va", side, hh)])
                            elif step == 1:
                                dve(lambda h, S_=S_, Wk=Wk, vv=vv, hh=hh, sl=sl: h.match_replace(out=Wk[:, sl], in_to_replace=vv[:, hh, 0:8], in_values=S_[:, sl], imm_value=-1e30),
                                    r=[fn_[2 + side], ct("va", side, hh)], w=[ct("wk", side, hh)])
                            elif step == 2:
                                dve(lambda h, Wk=Wk, vv=vv, hh=hh, sl=sl: h.max(out=vv[:, hh, 8:16], in_=Wk[:, sl]), r=[ct("wk", side, hh)], w=[ct("vb", side, hh)])
                            elif step == 3:
                                dve(lambda h, S_=S_, vv=vv, ii=ii, hh=hh, sl=sl: h.max_index(out=ii[:, hh, 0:8], in_max=vv[:, hh, 0:8], in_values=S_[:, sl]),
                                    r=[fn_[2 + side], ct("va", side, hh)], w=[ct("ia", side, hh)])
                            else:
                                dve(lambda h, Wk=Wk, vv=vv, ii=ii, hh=hh, sl=sl: h.max_index(out=ii[:, hh, 8:16], in_max=vv[:, hh, 8:16], in_values=Wk[:, sl]),
                                    r=[ct("wk", side, hh), ct("vb", side, hh)], w=[ct("ib", side, hh)])
                    dve(lambda h: h.memset(dmy[:, 1:2], 0.0), w=["f4", "f5"] + wktags)
                    vtags = [ct(nm, side, hh) for nm in ("va", "vb") for side in range(2) for hh in range(8)]
                    cwtags = ["cw%d" % k for k in range(4)]
                    dve(lambda h: h.memset(dmy[:, 2:3], 0.0), w=["candw"] + cwtags)
                    dve(lambda h: h.tensor_scalar(out=i1f[:], in0=i1[:], scalar1=128.0, scalar2=None, op0=ALU.mult),
                        r=[ct(nm, 0, hh) for nm in ("ia", "ib") for hh in range(8)], w=["i1f"])
                    dve(lambda h: h.tensor_copy(out=i2f[:], in_=i2[:]), r=[ct(nm, 1, hh) for nm in ("ia", "ib") for hh in range(8)], w=["i2f"])
                    for hg in range(2):
                        hs = slice(hg * 4, hg * 4 + 4)
                        dve(lambda h, hs=hs: h.tensor_tensor(out=cand[:, :, :].rearrange("p h (a b) -> p h a b", a=16),
                                                             in0=v1[:, hs, :].unsqueeze(3).to_broadcast([128, 4, 16, 16]),
                                                             in1=v2[:, hs, :].unsqueeze(2).to_broadcast([128, 4, 16, 16]), op=ALU.add),
                            r=vtags, w=["cand"])
                        for step in range(5):
                            for h4 in range(4):
                                hh = hg * 4 + h4
                                if step == 0:
                                    dve(lambda h, h4=h4, hh=hh: h.max(out=sc[:, hh, 0:8], in_=cand[:, h4, :]), r=["cand"], w=["sca%d" % hh])
                                elif step == 1:
                                    dve(lambda h, h4=h4, hh=hh: h.match_replace(out=candw[:, h4, :], in_to_replace=sc[:, hh, 0:8], in_values=cand[:, h4, :], imm_value=-1e30),
                                        r=["cand", "sca%d" % hh], w=["cw%d" % h4])
                                elif step == 2:
                                    dve(lambda h, h4=h4, hh=hh: h.max(out=sc[:, hh, 8:16], in_=candw[:, h4, :]), r=["cw%d" % h4], w=["scb%d" % hh])
                                elif step == 3:
                                    dve(lambda h, h4=h4, hh=hh: h.max_index(out=icu[:, hh, 0:8], in_max=sc[:, hh, 0:8], in_values=cand[:, h4, :]),
                                        r=["cand", "sca%d" % hh], w=["ica%d" % hh])
                                else:
                                    dve(lambda h, h4=h4, hh=hh: h.max_index(out=icu[:, hh, 8:16], in_max=sc[:, hh, 8:16], in_values=candw[:, h4, :]),
                                        r=["cw%d" % h4, "scb%d" % hh], w=["icb%d" % hh])
                    dve(lambda h: h.memset(dmy[:, 3:4], 0.0), w=["candw"] + cwtags)
                    sctags = ["sca%d" % hh for hh in range(8)] + ["scb%d" % hh for hh in range(8)]
                    dve(lambda h: h.tensor_copy(out=icf[:], in_=icu[:]), r=["ica%d" % hh for hh in range(8)] + ["icb%d" % hh for hh in range(8)], w=["icf"])
                    icf2 = icf[:, :, :].rearrange("p h k -> p (h k)")
                    thr = iota[:, :].rearrange("p (m s) -> p m s", s=16)[:, 1:16, 0]
                    for hv in range(2):
                        js = slice(hv * 64, (hv + 1) * 64)
                        tb, tbt = (f[0], "f0") if hv == 0 else (f[2], "f2")
                        t3 = tb[:, 0:960].rearrange("p (j m) -> p j m", m=15)
                        dve(lambda h, t3=t3, js=js: h.tensor_tensor(out=t3, in0=icf2[:, js].unsqueeze(2).to_broadcast([128, 64, 15]),
                                                                    in1=thr.unsqueeze(1).to_broadcast([128, 64, 15]), op=ALU.is_ge), r=["icf", "iota"], w=[tbt])
                        dve(lambda h, t3=t3, js=js: h.tensor_reduce(out=af[:, js], in_=t3, axis=AX.X, op=ALU.add), r=[tbt], w=["af%d" % hv])
                        dve(lambda h, js=js: h.scalar_tensor_tensor(out=bf_[:, js], in0=af[:, js], scalar=-16.0, in1=icf2[:, js], op0=ALU.mult, op1=ALU.add),
                            r=["af%d" % hv, "icf"], w=["bf%d" % hv])
                    k_ = 0
                    for hv in range(2):
                        js = slice(hv * 64, (hv + 1) * 64)
                        hsl = slice(hv * 4, hv * 4 + 4)
                        for (src, srct, tabf, tabt, dst, dstt) in ((af, "af%d" % hv, i1f, "i1f", ef, "ef"), (bf_, "bf%d" % hv, i2f, "i2f", eb, "eb")):
                            buf, buft = ((f[0], "f0"), (f[2], "f2"), (f[3], "f3"))[k_ % 3]
                            k_ += 1
                            oh3 = buf[:, 0:1024].rearrange("p (j a) -> p j a", a=16)
                            oh4 = buf[:, 0:1024].rearrange("p (h k a) -> p h k a", h=4, k=16)
                            dve(lambda h, oh3=oh3, src=src, js=js: h.tensor_tensor(out=oh3, in0=iota[:, 0:16].unsqueeze(1).to_broadcast([128, 64, 16]),
                                                                                   in1=src[:, js].unsqueeze(2).to_broadcast([128, 64, 16]), op=ALU.is_equal),
                                r=["iota", srct], w=[buft])
                            dve(lambda h, oh4=oh4, tabf=tabf, hsl=hsl: h.tensor_tensor(out=oh4, in0=oh4, in1=tabf[:, hsl, :].unsqueeze(2).to_broadcast([128, 4, 16, 16]), op=ALU.mult),
                                r=[buft, tabt], w=[buft])
                            dve(lambda h, oh3=oh3, dst=dst, js=js: h.tensor_reduce(out=dst[:, js], in_=oh3, axis=AX.X, op=ALU.add), r=[buft], w=[dstt])
                    dve(lambda h: h.tensor_tensor(out=ef[:], in0=ef[:], in1=eb[:], op=ALU.add), r=["ef", "eb"], w=["ef"])
                    dve(lambda h: h.tensor_copy(out=eu[:], in_=ef[:]), r=["ef"], w=["eu"])
                    dve(lambda h: h.tensor_tensor(out=gw[:, :].rearrange("p (h k) -> p h k", h=8), in0=sc[:, :, :],
                                                  in1=sc[:, :, 0:1].to_broadcast([128, 8, 16]), op=ALU.subtract), r=sctags, w=["gw"])
                    act(lambda h: h.activation(out=gw[:], in_=gw[:], func=AF.Exp), r=["gw"], w=["gw"])
                    dve(lambda h: h.tensor_reduce(out=sm2[:, 24:32], in_=gw[:, :].rearrange("p (h k) -> p h k", h=8), axis=AX.X, op=ALU.add), r=["gw"], w=["sm2"])
                    dve(lambda h: h.reciprocal(out=sm2[:, 32:40], in_=sm2[:, 24:32]), r=["sm2"], w=["sm2"])
                    dve(lambda h: h.tensor_tensor(out=gw[:, :].rearrange("p (h k) -> p h k", h=8), in0=gw[:, :].rearrange("p (h k) -> p h k", h=8),
                                                  in1=sm2[:, 32:40].unsqueeze(2).to_broadcast([128, 8, 16]), op=ALU.mult), r=["gw", "sm2"], w=["gw"])
                    NJ = 128 if LVL >= 4 else 0
                    gbufs = [(gbb[:, k, :], "gb%d" % k) for k in range(4)]
                    for tns, tg in ((cand, "cand"), (candw, "candw"), (candix, "candix")):
                        gbufs.append((tns[:, :, :].rearrange("p a b -> p (a b)").bitcast(BF16), tg))
                    for j in range(NJ):
                        s4 = j % 4
                        gbuf, gtag = gbufs[j % len(gbufs)]
                        P.op("pool", lambda h, j=j, gbuf=gbuf: h.indirect_dma_start(out=gbuf, out_offset=None, in_=UV_d,
                                                                                in_offset=bass.IndirectOffsetOnAxis(ap=eu[:, j:j + 1], axis=0)),
                             r=["eu"], w=[gtag], slot="gu%d" % (j % len(gbufs)))
                        dve(lambda h, j=j, gbuf=gbuf: h.scalar_tensor_tensor(out=f[0][:], in0=gbuf[:, 0:1024], scalar=1.0, in1=f[1][:], op0=ALU.mult, op1=ALU.mult,
                                                                         accum_out=aa[:, j:j + 1]), r=[gtag, "f1"], w=["f0", "aa%d" % s4])
                        act(lambda h, j=j: h.activation(out=ga[:, j:j + 1], in_=aa[:, j:j + 1], func=AF.Gelu), r=["aa%d" % s4], w=["ga%d" % s4])
                        act(lambda h, j=j: h.activation(out=ww[:, j:j + 1], in_=ga[:, j:j + 1], func=AF.Copy, scale=gw[:, j:j + 1]),
                            r=["ga%d" % s4, "gw"], w=["ww%d" % s4])
                        act(lambda h, j=j, s4=s4: h.activation(out=dg[:, s4, :], in_=idb[:], func=AF.Copy, scale=ww[:, j:j + 1]),
                            r=["ww%d" % s4, "idb"], w=["dg%d" % s4])
                        for hf in range(2):
                            pe(lambda h, j=j, s4=s4, hf=hf, gbuf=gbuf: h.matmul(psA[:, hf, :], lhsT=dg[:, s4, :], rhs=gbuf[:, 1024 + hf * 512:1024 + (hf + 1) * 512],
                                                                                start=(j == 0), stop=(j == NJ - 1)),
                               r=["dg%d" % s4, gtag], w=["psA"])
                    if NJ:
                        dve(lambda h: h.tensor_tensor(out=X[:], in0=X[:], in1=psA[:, :, :].rearrange("p a b -> p (a b)"), op=ALU.add), r=["X", "psA"], w=["X"])
                P.mark("pre_store")
                pr = st_par[0] % 2
                st_par[0] += 1
                dve(lambda h, pr=pr: h.tensor_copy(out=f[4 + pr][:], in_=X[:]), r=["X"], w=[fn_[4 + pr]])
                dma(oap[qi * 128:(qi + 1) * 128, :], f[4 + pr][:], r=[fn_[4 + pr]], slot="out%d" % pr)

        P.op("sp", lambda h: h.nop(), extra=list(P.slot_last.values()), force=True)
        if os.environ.get("KMARKS"):
            print("MARKS", P.marks, "TOTAL", P.nops)
        P.emit()
    return nc


def _rope_tabs(pos1d, posr, posc):
    T = pos1d.shape[0]
    out = np.zeros((T, 4, 64), np.float32)
    invA = (np.float32(10000.0) ** (-np.arange(0, 64, 2, dtype=np.float32) / np.float32(64))).astype(np.float32)
    invB = (np.float32(10000.0) ** (-np.arange(0, 32, 2, dtype=np.float32) / np.float32(32))).astype(np.float32)
    angA = pos1d.astype(np.float32)[:, None] * invA[None, :]
    cA, sA = np.cos(angA).astype(np.float32), np.sin(angA).astype(np.float32)
    out[:, 0, :32] = cA
    out[:, 0, 32:] = cA
    out[:, 1, :32] = -sA
    out[:, 1, 32:] = sA
    angR = posr.astype(np.float32)[:, None] * invB[None, :]
    angC = posc.astype(np.float32)[:, None] * invB[None, :]
    cR, sR = np.cos(angR).astype(np.float32), np.sin(angR).astype(np.float32)
    cC, sC = np.cos(angC).astype(np.float32), np.sin(angC).astype(np.float32)
    out[:, 2, 0:16] = cR
    out[:, 2, 16:32] = cR
    out[:, 2, 32:48] = cC
    out[:, 2, 48:64] = cC
    out[:, 3, 0:16] = -sR
    out[:, 3, 16:32] = sR
    out[:, 3, 32:48] = -sC
    out[:, 3, 48:64] = sC
    return out


def _consts(qoff, SS, SQ):
    kk = np.arange(128)[:, None]
    qq = np.arange(128)[None, :]
    prev = (qq <= kk).astype(np.float32)
    nxt = (kk <= qq).astype(np.float32)
    m = np.zeros((4, 128, 512), np.float32)
    m[0] = np.tile(prev, (1, 4))
    m[1] = np.tile(nxt, (1, 4))
    m[2] = m[0] if qoff > 0 else 0.0
    m[3] = m[1] if qoff + SQ < SS else 0.0
    return m


def make_in_maps(inputs, NP, SP, SS, SQ, n_cores, cores_per_sample):
    xp, xs, mp, ms = inputs["x_prompt"], inputs["x_sample"], inputs["mem_prompt"], inputs["mem_sample"]
    wts = {n: np.ascontiguousarray(np.asarray(inputs[n], np.float32).reshape(WSHAPES[n])) for n in WNAMES}
    ident = np.eye(128, dtype=np.float32)
    tp = np.arange(SP)
    tab_p = _rope_tabs(tp, tp // 64, tp % 64)
    maps = []
    for c in range(n_cores):
        sb_, part = c // cores_per_sample, c % cores_per_sample
        qoff = part * SQ
        order = (np.arange(SS) + qoff) % SS
        tab_s = _rope_tabs(order, order // 64, order % 64)
        m = dict(wts)
        m["xp"] = np.ascontiguousarray(xp[c * NP:(c + 1) * NP])
        m["memp"] = np.ascontiguousarray(mp[c * NP:(c + 1) * NP])
        m["xs"] = np.ascontiguousarray(xs[sb_][order])
        m["mems"] = np.ascontiguousarray(ms[sb_])
        m["tabs"] = np.ascontiguousarray(np.concatenate([tab_p] * NP + [tab_s], axis=0))
        m["cmask"] = _consts(qoff, SS, SQ)
        m["ident"] = ident
        m["iota"] = np.ascontiguousarray(np.tile(np.arange(256, dtype=np.float32)[None, :], (128, 1)))
        maps.append(m)
    return maps


_NC_CACHE = {}


def kernel(**inputs):
    NP, SP, SS, SQ, KBLK = 4, 2048, 16384, 4096, 2048
    n = 8
    key = (NP, SP, SS, SQ, KBLK)
    if key not in _NC_CACHE:
        _NC_CACHE[key] = build(*key)
    nc = _NC_CACHE[key]
    maps = make_in_maps(inputs, NP, SP, SS, SQ, n, 4)
    res = run_bass_kernel_spmd(nc, maps, core_ids=list(range(n)))
    yp = np.concatenate([r["yp"] for r in res.results], axis=0).astype(np.float32)
    ys = np.stack([np.concatenate([res.results[s * 4 + p]["ys"] for p in range(4)], axis=0) for s in range(2)], axis=0).astype(np.float32)
    return (yp, ys)
```

```python
import contextlib
import os
import numpy as np
import concourse.bass as bass
import concourse.mybir as mybir
from concourse.bass_utils import run_bass_kernel_spmd

F32 = mybir.dt.float32
BF16 = mybir.dt.bfloat16
U32 = mybir.dt.uint32
I32 = mybir.dt.int32
ALU = mybir.AluOpType
AF = mybir.ActivationFunctionType
AX = mybir.AxisListType

ENGS = ("pe", "act", "dve", "pool", "sp")
D = 1024
EPS = 1e-6


class _Op:
    __slots__ = ("fn", "deps", "dma_slot", "dma_val", "signal", "sigval")

    def __init__(self, fn, deps, dma_slot=None, dma_val=0):
        self.fn = fn
        self.deps = deps
        self.dma_slot = dma_slot
        self.dma_val = dma_val
        self.signal = False
        self.sigval = 0


class Prog:
    def __init__(self, nc):
        self.nc = nc
        self.q = {e: [] for e in ENGS}
        self.lastw = {}
        self.readers = {}
        self.slot_cnt = {}
        self.slot_last = {}
        self.locklast = {}
        self.nops = 0
        self.cut = int(os.environ.get("KCUT", "0"))
        self.marks = []

    def mark(self, name):
        self.marks.append((name, self.nops))

    def op(self, eng, fn, r=(), w=(), slot=None, extra=(), force=False):
        self.nops += 1
        if self.cut and self.nops > self.cut and not force:
            return None
        q = self.q[eng]
        idx = len(q)
        me = (eng, idx)
        key = eng if slot is None else ("dma", slot)
        deps = set(extra)
        for t in r:
            for x in self.lastw.get(t, {}).values():
                deps.add(x)
        for t in w:
            for x in self.lastw.get(t, {}).values():
                deps.add(x)
            for x in self.readers.get(t, {}).values():
                deps.add(x)
        for t in list(r) + list(w):
            if t.startswith("ps"):
                prev = self.locklast.get(t)
                if prev is not None and prev[0] != eng:
                    deps.add(prev)
                self.locklast[t] = me
        dma_val = 0
        if slot is not None:
            pl = self.slot_last.get(slot)
            if pl is not None:
                deps.add(pl)
            c = self.slot_cnt.get(slot, 0) + 1
            self.slot_cnt[slot] = c
            dma_val = 16 * c
            self.slot_last[slot] = me
        for t in w:
            if slot is None:
                self.lastw[t] = {key: me}
            else:
                d = {k: v for k, v in self.lastw.get(t, {}).items() if not isinstance(k, str)}
                d[key] = me
                self.lastw[t] = d
            self.readers[t] = {}
        for t in r:
            self.readers.setdefault(t, {})[key] = me
        deps.discard(me)
        q.append(_Op(fn, deps, slot, dma_val))
        return me

    def barrier(self):
        last = [(e, len(self.q[e]) - 1) for e in ENGS if self.q[e]]
        last += list(self.slot_last.values())
        for e in ENGS:
            self.op(e, lambda h: h.nop(), extra=last)
        self.lastw = {}
        self.readers = {}
        self.locklast = {}

    def emit(self):
        nc = self.nc
        for e in ENGS:
            for i, o in enumerate(self.q[e]):
                for (e2, i2) in o.deps:
                    o2 = self.q[e2][i2]
                    if o2.dma_slot is not None:
                        continue
                    if e2 == e and e in ("pe", "sp"):
                        continue
                    o2.signal = True
        for e in ENGS:
            c = 0
            for o in self.q[e]:
                if o.signal:
                    c += 1
                    o.sigval = c
        slots = sorted(self.slot_cnt.keys(), key=str)
        with contextlib.ExitStack() as st:
            esem = {e: st.enter_context(nc.semaphore("s_" + e)) for e in ENGS}
            ssem = {s: st.enter_context(nc.semaphore("d%d" % k)) for k, s in enumerate(slots)}
            block = st.enter_context(nc.Block())
            prog = self

            def run(e, h):
                seen = {}
                for i, o in enumerate(prog.q[e]):
                    need = {}
                    for (e2, i2) in o.deps:
                        o2 = prog.q[e2][i2]
                        if o2.dma_slot is not None:
                            sem = ssem[o2.dma_slot]
                            val = o2.dma_val
                        else:
                            if e2 == e and e in ("pe", "sp"):
                                continue
                            sem = esem[e2]
                            val = o2.sigval
                        k = id(sem)
                        if seen.get(k, 0) >= val:
                            continue
                        if need.get(k, (None, 0))[1] < val:
                            need[k] = (sem, val)
                    for k, (sem, val) in need.items():
                        if os.environ.get("KSKIP") and e == "act" and i == len(prog.q[e]) - 1 and sem.name in os.environ["KSKIP"].split(","):
                            print("SKIPPING WAIT", sem.name, val)
                            continue
                        h.wait_ge(sem, val)
                        seen[k] = val
                        if os.environ.get("KDBG") and (i >= len(prog.q[e]) - 3 or os.environ.get("KDBG") == e):
                            print("WAIT", e, i, sem.name if hasattr(sem, "name") else sem, val)
                    ins = o.fn(h)
                    if o.dma_slot is not None:
                        ins.then_inc(ssem[o.dma_slot], 16)
                    elif o.signal:
                        ins.then_inc(esem[e], 1)

            @block.sync
            def _(h):
                run("sp", h)

            @block.scalar
            def _(h):
                run("act", h)

            @block.vector
            def _(h):
                run("dve", h)

            @block.gpsimd
            def _(h):
                run("pool", h)

            @block.tensor
            def _(h):
                run("pe", h)


WNAMES = ["ln_mix", "w_in", "qn_a", "kn_a", "sink_a", "qn_b", "kn_b", "go_a", "go_b", "w_out",
          "ln_x", "ln_mem", "w_cq", "w_ckv", "cqn", "ckn", "w_co", "ln_ff", "w_pq", "pk1", "pk2",
          "peer_u", "peer_v"]
WSHAPES = {"ln_mix": [D], "w_in": [D, 1536], "qn_a": [64], "kn_a": [64], "sink_a": [8], "qn_b": [64],
           "kn_b": [64], "go_a": [512], "go_b": [512], "w_out": [D, D], "ln_x": [D], "ln_mem": [D],
           "w_cq": [D, 512], "w_ckv": [D, D], "cqn": [128], "ckn": [128], "w_co": [512, D],
           "ln_ff": [D], "w_pq": [D, 2048], "pk1": [8, 128, 128], "pk2": [8, 128, 128],
           "peer_u": [16384, D], "peer_v": [16384, D]}


def build(NP, SP, SS, SQ, KBLK, LVL=4):
    nc = bass.Bass("TRN2", target_bir_lowering=False)
    NTOK = NP * SP + SS
    NT_ALL = NTOK // 128

    def din(name, shape, dt=F32):
        return nc.dram_tensor(name, list(shape), dt, kind="ExternalInput").ap()

    xp = din("xp", [NP, SP, D])
    xs = din("xs", [SS, D])
    memp = din("memp", [NP, 256, D])
    mems = din("mems", [256, D])
    tabs = din("tabs", [NTOK, 4, 64])
    cmask = din("cmask", [4, 128, 512])
    ident_d = din("ident", [128, 128])
    iota_d = din("iota", [128, 256])
    W = {n: din(n, WSHAPES[n]) for n in WNAMES}
    yp = nc.dram_tensor("yp", [NP, SP, D], F32, kind="ExternalOutput").ap()
    ys = nc.dram_tensor("ys", [SQ, D], F32, kind="ExternalOutput").ap()
    KTA_d = nc.dram_tensor("KTA_d", [128, NTOK], BF16, kind="Internal").ap()
    KTB_d = nc.dram_tensor("KTB_d", [128, NTOK], BF16, kind="Internal").ap()
    VA_d = nc.dram_tensor("VA_d", [NT_ALL, 128, 130], BF16, kind="Internal").ap()
    VB_d = nc.dram_tensor("VB_d", [NT_ALL, 128, 130], BF16, kind="Internal").ap()
    UV_d = nc.dram_tensor("UV_d", [16384, 2048], BF16, kind="Internal").ap()

    seqs = []
    for i in range(NP):
        seqs.append((xp[i], memp[i], SP, i * SP, SP // 128, yp[i], False))
    seqs.append((xs, mems, SS, NP * SP, SQ // 128, ys, True))
    NSEQ = len(seqs)

    P = Prog(nc)
    with contextlib.ExitStack() as st:
        st.enter_context(nc.allow_non_contiguous_dma(reason="tiny gain-vector / V-tile layout loads"))

        def sb(name, shape, dt):
            return st.enter_context(nc.sbuf_tensor("sb_" + name, shape, dt))

        def ps(name, shape, dt):
            return st.enter_context(nc.psum_tensor("ps_" + name, shape, dt))

        def dve(fn, r=(), w=()):
            return P.op("dve", fn, r, w)

        def act(fn, r=(), w=()):
            return P.op("act", fn, r, w)

        def pe(fn, r=(), w=()):
            return P.op("pe", fn, r, w)

        def pool(fn, r=(), w=()):
            return P.op("pool", fn, r, w)

        def dma(out, in_, r=(), w=(), slot=None, eng="sp"):
            return P.op(eng, lambda h: h.dma_start(out=out, in_=in_), r, w, slot=slot)

        w_in = sb("w_in", [128, 8, 1536], BF16)
        w_out = sb("w_out", [128, 8, 1024], BF16)
        w_cq = sb("w_cq", [128, 8, 512], BF16)
        w_co = sb("w_co", [128, 4, 1024], BF16)
        w_pq = sb("w_pq", [128, 8, 2048], BF16)
        pkT = sb("pkT", [128, 16, 128], BF16)
        gb = sb("gb", [128, 4, 1024], F32)
        w_ckv = gb[:, :, :].rearrange("p a b -> p (a b)").bitcast(BF16).rearrange("p (c n) -> p c n", c=8)
        gbb = gb[:, :, :].rearrange("p a b -> p (a b)").bitcast(BF16).rearrange("p (s n) -> p s n", s=4)
        dmy = sb("dmy", [128, 8], F32)
        af = sb("af", [128, 128], F32)
        bf_ = sb("bf", [128, 128], F32)
        eb = sb("eb", [128, 128], F32)
        ga = sb("ga", [128, 128], F32)
        ww = sb("ww", [128, 128], F32)
        dg = sb("dg", [128, 4, 128], BF16)
        X = sb("X", [128, 1024], F32)
        f = [sb("f%d" % i, [128, 1024], F32) for i in range(6)]
        b = [sb("b%d" % i, [128, 1024], BF16) for i in range(6)]
        cand = sb("cand", [128, 4, 256], F32)
        candw = sb("candw", [128, 4, 256], F32)
        candix = sb("candix", [128, 4, 256], F32)
        lnff = sb("lnff", [128, 1024], F32)
        KB = [sb("KB%d" % i, [128, KBLK], BF16) for i in range(2)]
        VB = [sb("VB%d" % i, [128, KBLK // 128, 130], BF16) for i in range(2)]
        KA = sb("KA", [128, 3, 128], BF16)
        VA = sb("VA", [128, 3, 130], BF16)
        memK = sb("memK", [128, 4, 256], BF16)
        memV = sb("memV", [128, 2, 4, 129], BF16)
        idb = sb("idb", [128, 128], BF16)
        idf = sb("idf", [128, 128], F32)
        masks = sb("masks", [128, 4, 512], BF16)
        gT = sb("gT", [128, 6, 8], F32)
        GQ = sb("GQ", [128, 2, 64], F32)
        GK = sb("GK", [128, 2, 64], F32)
        GCQ = sb("GCQ", [128, 1, 128], F32)
        GCK = sb("GCK", [128, 1, 128], F32)
        sinkb = sb("sinkb", [128, 8], F32)
        sinkexp = sb("sinkexp", [128, 8], F32)
        negc = sb("negc", [128, 4], F32)
        tab = sb("tab", [128, 4, 64], F32)
        sm = sb("sm", [128, 64], F32)
        sm2 = sb("sm2", [128, 64], F32)
        v1 = sb("v1", [128, 8, 16], F32)
        v2 = sb("v2", [128, 8, 16], F32)
        i1 = sb("i1", [128, 8, 16], U32)
        i2 = sb("i2", [128, 8, 16], U32)
        i1f = sb("i1f", [128, 8, 16], F32)
        i2f = sb("i2f", [128, 8, 16], F32)
        sc = sb("sc", [128, 8, 16], F32)
        ef = sb("ef", [128, 128], F32)
        icu = sb("icu", [128, 8, 16], U32)
        icf = sb("icf", [128, 8, 16], F32)
        iota = sb("iota", [128, 256], F32)
        eu = sb("eu", [128, 128], U32)
        gw = sb("gw", [128, 128], F32)
        aa = sb("aa", [128, 128], F32)
        psA = ps("psA", [128, 2, 512], F32)
        psT = ps("psT", [128, 8, 128], BF16)
        psS = [ps("psS%d" % i, [128, 2, 512], F32) for i in range(2)]
        psO = ps("psO", [128, 512], F32)

        fn_ = ["f%d" % i for i in range(6)]
        bn_ = ["b%d" % i for i in range(6)]

        dma(idf[:], ident_d, w=["idf"], slot="c0")
        dma(iota[:], iota_d, w=["iota"], slot="c1")
        dve(lambda h: h.tensor_copy(out=idb[:], in_=idf[:]), r=["idf"], w=["idb"])
        for k in range(4):
            dma(f[k][:, 0:512], cmask[k], w=[fn_[k]], slot="c%d" % (1 + k))
            dve(lambda h, k=k: h.tensor_copy(out=masks[:, k, :], in_=f[k][:, 0:512]), r=[fn_[k]], w=["masks"])
        for k, nm in enumerate(["ln_mix", "go_a", "ln_x", "ln_mem", "ln_ff"]):
            if nm == "go_a":
                dma(gT[:, 1, 0:4].unsqueeze(2), W["go_a"].rearrange("(c p o) -> p c o", p=128, o=1), w=["gT"], slot="c5")
                dma(gT[:, 1, 4:8].unsqueeze(2), W["go_b"].rearrange("(c p o) -> p c o", p=128, o=1), w=["gT"], slot="c6")
            else:
                dma(gT[:, k, :].unsqueeze(2), W[nm].rearrange("(c p o) -> p c o", p=128, o=1), w=["gT"], slot="c5")
        dma(lnff[:], W["ln_ff"].partition_broadcast(128), w=["lnff"], slot="c6")
        dma(GQ[:, 0, :], W["qn_a"].partition_broadcast(128), w=["GQ"], slot="c5")
        dma(GQ[:, 1, :], W["qn_b"].partition_broadcast(128), w=["GQ"], slot="c6")
        dma(GK[:, 0, :], W["kn_a"].partition_broadcast(128), w=["GK"], slot="c5")
        dma(GK[:, 1, :], W["kn_b"].partition_broadcast(128), w=["GK"], slot="c6")
        dma(GCQ[:, 0, :], W["cqn"].partition_broadcast(128), w=["GCQ"], slot="c5")
        dma(GCK[:, 0, :], W["ckn"].partition_broadcast(128), w=["GCK"], slot="c6")
        dma(sinkb[:], W["sink_a"].partition_broadcast(128), w=["sinkb"], slot="c5")

        def absmax(dst, src, r):
            dve(lambda h: h.tensor_reduce(out=dst, in_=src, axis=AX.X, op=ALU.max, apply_absolute_value=True),
                r=r, w=["sm"])
        absmax(sm[:, 0:1], GQ[:, 0, :], ["GQ"])
        absmax(sm[:, 1:2], GK[:, 0, :], ["GK"])
        absmax(sm[:, 2:3], GQ[:, 1, :], ["GQ"])
        absmax(sm[:, 3:4], GK[:, 1, :], ["GK"])
        absmax(sm[:, 4:5], GCQ[:, 0, :], ["GCQ"])
        absmax(sm[:, 5:6], GCK[:, 0, :], ["GCK"])
        dve(lambda h: h.tensor_reduce(out=sm[:, 6:7], in_=sinkb[:], axis=AX.X, op=ALU.max), r=["sinkb"], w=["sm"])
        dve(lambda h: h.tensor_tensor(out=sm[:, 8:9], in0=sm[:, 0:1], in1=sm[:, 1:2], op=ALU.mult), r=["sm"], w=["sm"])
        dve(lambda h: h.tensor_tensor(out=sm[:, 9:10], in0=sm[:, 2:3], in1=sm[:, 3:4], op=ALU.mult), r=["sm"], w=["sm"])
        dve(lambda h: h.tensor_tensor(out=sm[:, 10:11], in0=sm[:, 4:5], in1=sm[:, 5:6], op=ALU.mult), r=["sm"], w=["sm"])
        dve(lambda h: h.tensor_scalar(out=sm[:, 8:10], in0=sm[:, 8:10], scalar1=8.0, scalar2=None, op0=ALU.mult), r=["sm"], w=["sm"])
        dve(lambda h: h.tensor_scalar(out=sm[:, 10:11], in0=sm[:, 10:11], scalar1=float(np.sqrt(128.0)), scalar2=None, op0=ALU.mult), r=["sm"], w=["sm"])
        dve(lambda h: h.tensor_tensor(out=sm[:, 8:9], in0=sm[:, 8:9], in1=sm[:, 6:7], op=ALU.max), r=["sm"], w=["sm"])
        dve(lambda h: h.tensor_scalar(out=negc[:, 0:3], in0=sm[:, 8:11], scalar1=-1.0, scalar2=None, op0=ALU.mult), r=["sm"], w=["negc"])
        act(lambda h: h.activation(out=sinkexp[:], in_=sinkb[:], func=AF.Exp, bias=negc[:, 0:1], scale=1.0),
            r=["sinkb", "negc"], w=["sinkexp"])

        P.mark("consts")
        uv_next = [0]

        def uv_iter():
            rblk = uv_next[0]
            if rblk >= 128 or LVL < 4 or os.environ.get("KNOUV"):
                return
            uv_next[0] += 1
            k2 = rblk % 2
            r0 = rblk * 128
            ub, ubt = (cand, "cand") if k2 == 0 else (candix, "candix")
            ubf = ub[:, :, :].rearrange("p a b -> p (a b)")
            vbf, vbt = (candw[:, :, :].rearrange("p a b -> p (a b)"), "candw") if k2 == 0 else (f[5][:, :], "f5")
            dma(ubf, W["peer_u"][r0:r0 + 128, :], w=[ubt], slot="uvl%d" % (2 * k2), eng="pool")
            dma(vbf, W["peer_v"][r0:r0 + 128, :], w=[vbt], slot="uvl%d" % (2 * k2 + 1), eng="pool")
            pool(lambda h: h.tensor_copy(out=gbb[:, k2, 0:1024], in_=ubf), r=[ubt], w=["gb%d" % k2])
            pool(lambda h: h.tensor_copy(out=gbb[:, k2, 1024:2048], in_=vbf), r=[vbt], w=["gb%d" % k2])
            dma(UV_d[r0:r0 + 128, :], gbb[:, k2, :], r=["gb%d" % k2], slot="uvs%d" % k2, eng="pool")
        stg_i = [0]

        def load_w(dst, src, K, N, grow):
            for c in range(K // 128):
                for n0 in range(0, N, 1024):
                    nn = min(1024, N - n0)
                    k = stg_i[0] % 2
                    stg_i[0] += 1
                    dma(f[k][:, 0:nn], src[c * 128:(c + 1) * 128, n0:n0 + nn], w=[fn_[k]], slot="stg%d" % k)
                    if grow is None:
                        dve(lambda h, k=k, c=c, n0=n0, nn=nn: h.tensor_copy(out=dst[:, c, n0:n0 + nn], in_=f[k][:, 0:nn]),
                            r=[fn_[k]], w=["W"])
                    else:
                        dve(lambda h, k=k, c=c, n0=n0, nn=nn: h.tensor_scalar(
                            out=dst[:, c, n0:n0 + nn], in0=f[k][:, 0:nn], scalar1=gT[:, grow, c:c + 1], scalar2=None, op0=ALU.mult),
                            r=[fn_[k], "gT"], w=["W"])
        load_w(w_in, W["w_in"], 1024, 1536, 0)
        load_w(w_out, W["w_out"], 1024, 1024, 1)
        load_w(w_cq, W["w_cq"], 1024, 512, 2)
        load_w(w_co, W["w_co"], 512, 1024, None)
        load_w(w_pq, W["w_pq"], 1024, 2048, 4)
        for side, nm in enumerate(["pk1", "pk2"]):
            dma(f[2 + side][:, :].rearrange("p (h d) -> p h d", h=8), W[nm].rearrange("h n d -> n h d"),
                w=[fn_[2 + side]], slot="c%d" % (1 + side))
            dve(lambda h, side=side: h.tensor_copy(out=b[side][:, :], in_=f[2 + side][:, :]), r=[fn_[2 + side]], w=[bn_[side]])
            for hh in range(8):
                pe(lambda h, side=side, hh=hh: h.transpose(out=psT[:, hh, :], in_=b[side][:, hh * 128:(hh + 1) * 128], identity=idb[:]),
                   r=[bn_[side], "idb"], w=["psT"])
            act(lambda h, side=side: h.activation(out=pkT[:, :, :].rearrange("p (h s) k -> p s h k", s=2)[:, side, :, :],
                                                   in_=psT[:, :, :], func=AF.Copy), r=["psT"], w=["W"])

        P.mark("weights")
        def rstd_from_ss(dst, src, n, rtag, wtag):
            act(lambda h: h.activation(out=dst, in_=src, func=AF.Ln, scale=1.0 / n, bias=EPS), r=[rtag], w=[wtag])
            act(lambda h: h.activation(out=dst, in_=dst, func=AF.Ln if False else AF.Exp, scale=-0.5), r=[wtag], w=[wtag])

        def norm_to_hT(src, srctag, extra_f32=None):
            act(lambda h: h.activation(out=f[0][:], in_=src, func=AF.Square, accum_out=sm[:, 16:17]), r=[srctag], w=["f0", "sm"])
            rstd_from_ss(sm[:, 17:18], sm[:, 16:17], 1024.0, "sm", "sm")
            dve(lambda h: h.tensor_scalar(out=b[0][:], in0=src, scalar1=sm[:, 17:18], scalar2=None, op0=ALU.mult),
                r=[srctag, "sm"], w=["b0"])
            if extra_f32 is not None:
                dve(lambda h: h.scalar_tensor_tensor(out=extra_f32, in0=src, scalar=sm[:, 17:18], in1=lnff[:],
                                                     op0=ALU.mult, op1=ALU.mult), r=[srctag, "sm", "lnff"], w=["f1"])
            for c in range(8):
                pe(lambda h, c=c: h.transpose(out=psT[:, c, :], in_=b[0][:, c * 128:(c + 1) * 128], identity=idb[:]),
                   r=["b0", "idb"], w=["psT"])
            act(lambda h: h.activation(out=b[1][:, :], in_=psT[:, :, :].rearrange("p a b -> p (a b)"), func=AF.Copy), r=["psT"], w=["b1"])

        hT = b[1][:, :].rearrange("p (c t) -> p c t", c=8)

        def proj(dst_ps, wt, nK, n0, nn, dtag):
            for c in range(nK):
                pe(lambda h, c=c: h.matmul(dst_ps, lhsT=hT[:, c, :], rhs=wt[:, c, n0:n0 + nn], start=(c == 0), stop=(c == nK - 1)),
                   r=["b1", "W"], w=[dtag])

        def headnorm_rope(zsrc, ztag, NH, G, Gtag, tsel, dst, dtag, perm):
            n = NH * 64
            half = 32 if tsel == 0 else 16
            nb = 64 // (2 * half)
            act(lambda h: h.activation(out=f[0][:, 0:n], in_=zsrc, func=AF.Square), r=[ztag], w=["f0"])
            dve(lambda h: h.tensor_reduce(out=sm[:, 20:20 + NH], in_=f[0][:, 0:n].rearrange("p (h d) -> p h d", d=64), axis=AX.X, op=ALU.add),
                r=["f0"], w=["sm"])
            rstd_from_ss(sm[:, 20:20 + NH], sm[:, 20:20 + NH], 64.0, "sm", "sm")
            dve(lambda h: h.tensor_tensor(out=f[1][:, 0:n].rearrange("p (h d) -> p h d", d=64), in0=zsrc.rearrange("p (h d) -> p h d", d=64),
                                          in1=sm[:, 20:20 + NH].unsqueeze(2).to_broadcast([128, NH, 64]), op=ALU.mult),
                r=[ztag, "sm"], w=["f1"])
            dve(lambda h: h.tensor_tensor(out=f[1][:, 0:n].rearrange("p (h d) -> p h d", d=64), in0=f[1][:, 0:n].rearrange("p (h d) -> p h d", d=64),
                                          in1=G.unsqueeze(1).to_broadcast([128, NH, 64]), op=ALU.mult), r=["f1", Gtag], w=["f1"])
            y4 = f[1][:, 0:n].rearrange("p (h b t f) -> p h b t f", h=NH, b=nb, t=2)
            t24 = f[3][:, 0:n].rearrange("p (h b t f) -> p h b t f", h=NH, b=nb, t=2)
            C3 = tab[:, tsel, :].unsqueeze(1).to_broadcast([128, NH, 64])
            S4 = tab[:, tsel + 1, :].rearrange("p (b t f) -> p b t f", b=nb, t=2)
            dve(lambda h: h.tensor_tensor(out=f[2][:, 0:n].rearrange("p (h d) -> p h d", d=64), in0=f[1][:, 0:n].rearrange("p (h d) -> p h d", d=64),
                                          in1=C3, op=ALU.mult), r=["f1", "tab"], w=["f2"])
            for t in range(2):
                dve(lambda h, t=t: h.tensor_tensor(out=t24[:, :, :, t, :], in0=y4[:, :, :, 1 - t, :],
                                                   in1=S4[:, :, t, :].unsqueeze(1).to_broadcast([128, NH, nb, half]), op=ALU.mult),
                    r=["f1", "tab"], w=["f3"])
            if perm:
                o3 = dst.rearrange("p (g k d) -> p k g d", g=4, k=2)
                a3 = f[2][:, 0:n].rearrange("p (k g d) -> p k g d", k=2, g=4)
                c3 = f[3][:, 0:n].rearrange("p (k g d) -> p k g d", k=2, g=4)
                for kv in range(2):
                    dve(lambda h, kv=kv: h.tensor_tensor(out=o3[:, kv], in0=a3[:, kv], in1=c3[:, kv], op=ALU.add), r=["f2", "f3"], w=[dtag])
            else:
                dve(lambda h: h.tensor_tensor(out=dst, in0=f[2][:, 0:n], in1=f[3][:, 0:n], op=ALU.add), r=["f2", "f3"], w=[dtag])

        P.barrier()
        for si, (xap, memap, S, tok0, nq, oap, is_s) in enumerate(seqs):
            for ti in range(S // 128):
                t0 = tok0 + ti * 128
                gt = t0 // 128
                P.mark("pro_tile_%d_%d" % (si, ti))
                uv_iter()
                dma(X[:], xap[ti * 128:(ti + 1) * 128, :], w=["X"], slot="x")
                dma(tab[:], tabs[t0:t0 + 128], w=["tab"], slot="tab")
                norm_to_hT(X[:], "X")
                P.mark("pro_normT")
                needA = (not is_s) or ti <= nq or ti == S // 128 - 1
                if needA:
                    proj(psA[:, 0, 0:256], w_in, 8, 512, 256, "psA")
                proj(psA[:, 1, 0:256], w_in, 8, 1280, 256, "psA")
                for ty in range(2):
                    if ty == 0 and not needA:
                        continue
                    z = psA[:, ty, 0:128]
                    headnorm_rope(z, "psA", 2, GK[:, ty, :], "GK", 2 * ty,
                                  b[2][:, ty * 128:(ty + 1) * 128], "b2", False)
                    pe(lambda h, ty=ty: h.transpose(out=psT[:, ty, :], in_=b[2][:, ty * 128:(ty + 1) * 128], identity=idb[:]),
                       r=["b2", "idb"], w=["psT"])
                    vdst = b[3][:, ty * 130:(ty + 1) * 130].rearrange("p (k c) -> p k c", k=2)
                    act(lambda h, ty=ty, vdst=vdst: h.activation(out=vdst[:, :, 0:64], in_=psA[:, ty, 128:256].rearrange("p (k c) -> p k c", k=2),
                                                                  func=AF.Copy), r=["psA"], w=["b3"])
                    pool(lambda h, vdst=vdst: h.memset(vdst[:, :, 64:65], 1.0), w=["b3"])
                if needA:
                    act(lambda h: h.activation(out=b[4][:, 0:256], in_=psT[:, 0:2, :].rearrange("p a b -> p (a b)"), func=AF.Copy), r=["psT"], w=["b4"])
                    dma(KTA_d[:, t0:t0 + 128], b[4][:, 0:128], r=["b4"], slot="st0")
                    dma(VA_d[gt], b[3][:, 0:130], r=["b3"], slot="st2")
                else:
                    act(lambda h: h.activation(out=b[4][:, 128:256], in_=psT[:, 1, :], func=AF.Copy), r=["psT"], w=["b4"])
                dma(KTB_d[:, t0:t0 + 128], b[4][:, 128:256], r=["b4"], slot="st1")
                dma(VB_d[gt], b[3][:, 130:260], r=["b3"], slot="st3")
        while uv_next[0] < 128 and LVL >= 4 and not os.environ.get("KNOUV"):
            uv_iter()
        P.barrier()

        def mem_kv(memap):
            for mt in range(2):
                dma(X[:], memap[mt * 128:(mt + 1) * 128, :], w=["X"], slot="x")
                norm_to_hT(X[:], "X")
                proj(psA[:, 0, :], w_ckv, 8, 0, 512, "psA")
                proj(psA[:, 1, :], w_ckv, 8, 512, 512, "psA")
                act(lambda h: h.activation(out=f[0][:, 0:512], in_=psA[:, 0, :], func=AF.Square), r=["psA"], w=["f0"])
                dve(lambda h: h.tensor_reduce(out=sm[:, 20:24], in_=f[0][:, 0:512].rearrange("p (h d) -> p h d", d=128), axis=AX.X, op=ALU.add),
                    r=["f0"], w=["sm"])
                rstd_from_ss(sm[:, 20:24], sm[:, 20:24], 128.0, "sm", "sm")
                dve(lambda h: h.tensor_tensor(out=f[1][:, 0:512].rearrange("p (h d) -> p h d", d=128), in0=psA[:, 0, :].rearrange("p (h d) -> p h d", d=128),
                                              in1=sm[:, 20:24].unsqueeze(2).to_broadcast([128, 4, 128]), op=ALU.mult), r=["psA", "sm"], w=["f1"])
                dve(lambda h: h.tensor_tensor(out=b[2][:, 0:512].rearrange("p (h d) -> p h d", d=128), in0=f[1][:, 0:512].rearrange("p (h d) -> p h d", d=128),
                                              in1=GCK[:, :, :].to_broadcast([128, 4, 128]), op=ALU.mult),
                    r=["f1", "GCK"], w=["b2"])
                for hh in range(4):
                    pe(lambda h, hh=hh: h.transpose(out=psT[:, hh, :], in_=b[2][:, hh * 128:(hh + 1) * 128], identity=idb[:]),
                       r=["b2", "idb"], w=["psT"])
                act(lambda h, mt=mt: h.activation(out=memK[:, :, mt * 128:(mt + 1) * 128], in_=psT[:, 0:4, :], func=AF.Copy), r=["psT"], w=["memK"])
                act(lambda h, mt=mt: h.activation(out=memV[:, mt, :, 0:128], in_=psA[:, 1, :].rearrange("p (h d) -> p h d", d=128), func=AF.Copy),
                    r=["psA"], w=["memV"])
                pool(lambda h, mt=mt: h.memset(memV[:, mt, :, 128:129], 1.0), w=["memV"])

        st_par = [0]
        for si, (xap, memap, S, tok0, nq, oap, is_s) in enumerate(seqs):
            NT = S // 128
            KT = KBLK // 128
            nblk = S // KBLK
            P.mark("main_seq_%d" % si)
            P.barrier()
            load_w(w_ckv, W["w_ckv"], 1024, 1024, 3)
            mem_kv(memap)
            P.barrier()
            for qi in range(nq):
                t0 = tok0 + qi * 128
                dma(X[:], xap[qi * 128:(qi + 1) * 128, :], w=["X"], slot="x")
                dma(tab[:], tabs[t0:t0 + 128], w=["tab"], slot="tab")
                P.mark("main_tile_%d_%d" % (si, qi))
                norm_to_hT(X[:], "X")
                proj(psA[:, 0, :], w_in, 8, 0, 512, "psA")
                proj(psA[:, 1, :], w_in, 8, 768, 512, "psA")
                for ty in range(2):
                    headnorm_rope(psA[:, ty, :], "psA", 8, GQ[:, ty, :], "GQ", 2 * ty,
                                  b[2][:, ty * 512:(ty + 1) * 512], "b2", True)
                for c in range(8):
                    pe(lambda h, c=c: h.transpose(out=psT[:, c, :], in_=b[2][:, c * 128:(c + 1) * 128], identity=idb[:]),
                       r=["b2", "idb"], w=["psT"])
                act(lambda h: h.activation(out=b[3][:, :], in_=psT[:, :, :].rearrange("p a b -> p (a b)"), func=AF.Copy), r=["psT"], w=["b3"])
                QT = b[3]
                P.mark("QT")

                def attn_units(units, accs, tsel):
                    cnt = len(units)
                    started = {}
                    def scores(u, bi):
                        kv, tiles, pre = units[u]
                        for jj, (kfn, vfn, msel, ktag, vtag) in enumerate(tiles):
                            pe(lambda h, kv=kv, kfn=kfn, jj=jj, bi=bi: h.matmul(
                                psS[bi][:, jj, :], lhsT=kfn(kv), rhs=QT[kv * 64:(kv + 1) * 64, tsel * 512:(tsel + 1) * 512], start=True, stop=True),
                               r=["b3", ktag], w=["psS%d" % bi])
                        nt = len(tiles)
                        act(lambda h, bi=bi, nt=nt: h.activation(out=b[4 + bi][:, 0:nt * 512], in_=psS[bi][:, 0:nt, :].rearrange("p a b -> p (a b)"),
                                                                  func=AF.Exp, bias=negc[:, tsel:tsel + 1], scale=0.125),
                            r=["psS%d" % bi, "negc"], w=[bn_[4 + bi]])
                        for jj, (kfn, vfn, msel, ktag, vtag) in enumerate(tiles):
                            if msel is not None:
                                pool(lambda h, bi=bi, jj=jj, msel=msel: h.tensor_tensor(out=b[4 + bi][:, jj * 512:(jj + 1) * 512], in0=b[4 + bi][:, jj * 512:(jj + 1) * 512],
                                                                                        in1=masks[:, msel, :], op=ALU.mult),
                                     r=[bn_[4 + bi], "masks"], w=[bn_[4 + bi]])
                    def pv(u, bi):
                        kv, tiles, pre = units[u]
                        acc, atag = accs[kv]
                        for jj, (kfn, vfn, msel, ktag, vtag) in enumerate(tiles):
                            first = kv not in started
                            started[kv] = True
                            last = all(units[u2][0] != kv for u2 in range(u + 1, cnt)) and jj == len(tiles) - 1
                            pe(lambda h, kv=kv, vfn=vfn, jj=jj, bi=bi, first=first, last=last, acc=acc: h.matmul(
                                acc, lhsT=vfn(kv), rhs=b[4 + bi][:, jj * 512:(jj + 1) * 512], start=first, stop=last),
                               r=[bn_[4 + bi], vtag], w=[atag])
                    for u in range(cnt):
                        scores(u, u % 2)
                        if u > 0:
                            pv(u - 1, (u - 1) % 2)
                        if units[u][2] is not None:
                            units[u][2]()
                    pv(cnt - 1, (cnt - 1) % 2)

                def finish_attn(accs, ocol, with_sink):
                    for kv in range(2):
                        acc, atag = accs[kv]
                        act(lambda h, kv=kv, acc=acc: h.activation(out=f[4][0:65, kv * 512:(kv + 1) * 512], in_=acc, func=AF.Copy), r=[atag], w=["f4"])
                    for kv in range(2):
                        for g in range(4):
                            pe(lambda h, kv=kv, g=g: h.transpose(out=psA[:, kv, g * 65:(g + 1) * 65], in_=f[4][0:65, kv * 512 + g * 128:kv * 512 + (g + 1) * 128],
                                                                  identity=idf[0:65, 0:65]), r=["f4", "idf"], w=["psA"])
                    pv4 = psA[:, :, 0:260].rearrange("p k (g c) -> p k g c", g=4)
                    dve(lambda h: h.tensor_copy(out=sm2[:, 0:8].rearrange("p (k g) -> p k g", k=2), in_=pv4[:, :, :, 64]), r=["psA"], w=["sm2"])
                    if with_sink:
                        dve(lambda h: h.tensor_tensor(out=sm2[:, 0:8], in0=sm2[:, 0:8], in1=sinkexp[:], op=ALU.add), r=["sm2", "sinkexp"], w=["sm2"])
                    dve(lambda h: h.reciprocal(out=sm2[:, 8:16], in_=sm2[:, 0:8]), r=["sm2"], w=["sm2"])
                    for kv in range(2):
                        dve(lambda h, kv=kv: h.tensor_tensor(out=f[5][:, ocol + kv * 256:ocol + (kv + 1) * 256].rearrange("p (g d) -> p g d", g=4),
                                                             in0=pv4[:, kv, :, 0:64],
                                                             in1=sm2[:, 8 + kv * 4:12 + kv * 4].unsqueeze(2).to_broadcast([128, 4, 64]), op=ALU.mult),
                            r=["psA", "sm2"], w=["f5"])

                if is_s:
                    jl = [((qi - 1) % NT, 2 if qi == 0 else 0), (qi, None), ((qi + 1) % NT, 3 if qi == nq - 1 else 1)]
                else:
                    jl = []
                    if qi > 0:
                        jl.append((qi - 1, 0))
                    jl.append((qi, None))
                    if qi < NT - 1:
                        jl.append((qi + 1, 1))
                for jj, (kt, msel) in enumerate(jl):
                    kt0 = tok0 + kt * 128
                    dma(KA[:, jj, :], KTA_d[:, kt0:kt0 + 128], w=["KA%d" % jj], slot="ka%d" % jj)
                    dma(VA[:, jj, :], VA_d[kt0 // 128], w=["VA%d" % jj], slot="va%d" % jj)
                unitsA = []
                for kv in range(2):
                    tl = [((lambda kv, jj=jj: KA[kv * 64:(kv + 1) * 64, jj, :]), (lambda kv, jj=jj: VA[:, jj, kv * 65:(kv + 1) * 65]), msel,
                           "KA%d" % jj, "VA%d" % jj) for jj, (kt, msel) in enumerate(jl)]
                    unitsA.append((kv, tl[0:2], None))
                    if len(tl) > 2:
                        unitsA.append((kv, tl[2:3], None))
                accs = {0: (psO[0:65, :], "psO"), 1: (psA[0:65, 0, :], "psA")}
                attn_units(unitsA, accs, 0)
                P.mark("attnA_units")
                finish_attn(accs, 0, True)
                P.mark("attnA_fin")
                def load_blk(blk):
                    bb = blk % 2
                    k0 = tok0 + blk * KBLK
                    dma(KB[bb][:, :], KTB_d[:, k0:k0 + KBLK], w=["KB%d" % bb], slot="kb%d" % bb)
                    dma(VB[bb][:, :, :], VB_d[k0 // 128:k0 // 128 + KT].rearrange("j p c -> p j c"), w=["VB%d" % bb], slot="vb%d" % bb)
                load_blk(0)
                unitsB = []
                for blk in range(nblk):
                    bb = blk % 2
                    firstu = True
                    for kv in range(2):
                        for j0 in range(0, KT, 2):
                            tl = [((lambda kv, bb=bb, j=j: KB[bb][kv * 64:(kv + 1) * 64, j * 128:(j + 1) * 128]),
                                   (lambda kv, bb=bb, j=j: VB[bb][:, j, kv * 65:(kv + 1) * 65]), None, "KB%d" % bb, "VB%d" % bb)
                                  for j in range(j0, min(j0 + 2, KT))]
                            pre = None
                            if firstu and blk + 1 < nblk:
                                pre = (lambda blk=blk: load_blk(blk + 1))
                            firstu = False
                            unitsB.append((kv, tl, pre))
                attn_units(unitsB, accs, 1)
                finish_attn(accs, 512, False)
                P.mark("attnB_fin")

                act(lambda h: h.activation(out=f[0][:, 0:512], in_=f[5][:, 0:512], func=AF.Square, accum_out=sm[:, 30:31]), r=["f5"], w=["f0", "sm"])
                act(lambda h: h.activation(out=f[0][:, 512:1024], in_=f[5][:, 512:1024], func=AF.Square, accum_out=sm[:, 31:32]), r=["f5"], w=["f0", "sm"])
                rstd_from_ss(sm[:, 30:32], sm[:, 30:32], 512.0, "sm", "sm")
                dve(lambda h: h.tensor_tensor(out=b[0][:, :].rearrange("p (a d) -> p a d", a=2), in0=f[5][:, :].rearrange("p (a d) -> p a d", a=2),
                                              in1=sm[:, 30:32].unsqueeze(2).to_broadcast([128, 2, 512]), op=ALU.mult), r=["f5", "sm"], w=["b0"])
                for c in range(8):
                    pe(lambda h, c=c: h.transpose(out=psT[:, c, :], in_=b[0][:, c * 128:(c + 1) * 128], identity=idb[:]),
                       r=["b0", "idb"], w=["psT"])
                act(lambda h: h.activation(out=b[1][:, :], in_=psT[:, :, :].rearrange("p a b -> p (a b)"), func=AF.Copy), r=["psT"], w=["b1"])
                proj(psA[:, 0, :], w_out, 8, 0, 512, "psA")
                proj(psA[:, 1, :], w_out, 8, 512, 512, "psA")
                dve(lambda h: h.tensor_tensor(out=X[:], in0=X[:], in1=psA[:, :, :].rearrange("p a b -> p (a b)"), op=ALU.add), r=["X", "psA"], w=["X"])

                for _lvl2 in ([0] if LVL >= 2 else []):
                    norm_to_hT(X[:], "X")
                    proj(psA[:, 0, :], w_cq, 8, 0, 512, "psA")
                    act(lambda h: h.activation(out=f[0][:, 0:512], in_=psA[:, 0, :], func=AF.Square), r=["psA"], w=["f0"])
                    dve(lambda h: h.tensor_reduce(out=sm[:, 20:24], in_=f[0][:, 0:512].rearrange("p (h d) -> p h d", d=128), axis=AX.X, op=ALU.add),
                        r=["f0"], w=["sm"])
                    rstd_from_ss(sm[:, 20:24], sm[:, 20:24], 128.0, "sm", "sm")
                    dve(lambda h: h.tensor_tensor(out=f[1][:, 0:512].rearrange("p (h d) -> p h d", d=128), in0=psA[:, 0, :].rearrange("p (h d) -> p h d", d=128),
                                                  in1=sm[:, 20:24].unsqueeze(2).to_broadcast([128, 4, 128]), op=ALU.mult), r=["psA", "sm"], w=["f1"])
                    dve(lambda h: h.tensor_tensor(out=b[2][:, 0:512].rearrange("p (h d) -> p h d", d=128), in0=f[1][:, 0:512].rearrange("p (h d) -> p h d", d=128),
                                                  in1=GCQ[:, :, :].to_broadcast([128, 4, 128]), op=ALU.mult),
                        r=["f1", "GCQ"], w=["b2"])
                    for hh in range(4):
                        pe(lambda h, hh=hh: h.transpose(out=psT[:, hh, :], in_=b[2][:, hh * 128:(hh + 1) * 128], identity=idb[:]),
                           r=["b2", "idb"], w=["psT"])
                    act(lambda h: h.activation(out=b[3][:, 0:512], in_=psT[:, 0:4, :].rearrange("p a b -> p (a b)"), func=AF.Copy), r=["psT"], w=["b3"])
                    for mt in range(2):
                        for hh in range(4):
                            pe(lambda h, mt=mt, hh=hh: h.matmul(psS[0][:, mt, hh * 128:(hh + 1) * 128], lhsT=memK[:, hh, mt * 128:(mt + 1) * 128],
                                                                rhs=b[3][:, hh * 128:(hh + 1) * 128], start=True, stop=True),
                               r=["memK", "b3"], w=["psS0"])
                    act(lambda h: h.activation(out=b[4][:, :], in_=psS[0][:, :, :].rearrange("p a b -> p (a b)"), func=AF.Exp,
                                               bias=negc[:, 2:3], scale=float(128.0 ** -0.5)), r=["psS0", "negc"], w=["b4"])
                    for hh in range(4):
                        for mt in range(2):
                            pe(lambda h, mt=mt, hh=hh: h.matmul(psA[:, hh // 2, (hh % 2) * 129:(hh % 2 + 1) * 129],
                                                                lhsT=b[4][:, mt * 512 + hh * 128:mt * 512 + (hh + 1) * 128],
                                                                rhs=memV[:, mt, hh, :], start=(mt == 0), stop=(mt == 1)),
                               r=["b4", "memV"], w=["psA"])
                    px = psA[:, :, 0:258].rearrange("p a (h c) -> p a h c", h=2)
                    dve(lambda h: h.reciprocal(out=sm2[:, 16:20].rearrange("p (a h) -> p a h", a=2), in_=px[:, :, :, 128]), r=["psA"], w=["sm2"])
                    for a in range(2):
                        dve(lambda h, a=a: h.tensor_tensor(out=b[0][:, a * 256:(a + 1) * 256].rearrange("p (h d) -> p h d", h=2), in0=px[:, a, :, 0:128],
                                                           in1=sm2[:, 16 + 2 * a:18 + 2 * a].unsqueeze(2).to_broadcast([128, 2, 128]), op=ALU.mult),
                            r=["psA", "sm2"], w=["b0"])
                    for c in range(4):
                        pe(lambda h, c=c: h.transpose(out=psT[:, c, :], in_=b[0][:, c * 128:(c + 1) * 128], identity=idb[:]),
                           r=["b0", "idb"], w=["psT"])
                    act(lambda h: h.activation(out=b[1][:, 0:512], in_=psT[:, 0:4, :].rearrange("p a b -> p (a b)"), func=AF.Copy), r=["psT"], w=["b1"])
                    proj(psA[:, 0, :], w_co, 4, 0, 512, "psA")
                    proj(psA[:, 1, :], w_co, 4, 512, 512, "psA")
                    dve(lambda h: h.tensor_tensor(out=X[:], in0=X[:], in1=psA[:, :, :].rearrange("p a b -> p (a b)"), op=ALU.add), r=["X", "psA"], w=["X"])

                for _lvl3 in ([0] if LVL >= 3 else []):
                    norm_to_hT(X[:], "X", extra_f32=f[1][:])
                    for half in range(2):
                        for nb in range(8):
                            for c in range(8):
                                pe(lambda h, half=half, nb=nb, c=c: h.matmul(psA[:, nb // 4, (nb % 4) * 128:(nb % 4 + 1) * 128],
                                                                              lhsT=w_pq[:, c, (half * 8 + nb) * 128:(half * 8 + nb + 1) * 128],
                                                                              rhs=hT[:, c, :], start=(c == 0), stop=(c == 7)),
                                   r=["W", "b1"], w=["psA"])
                        act(lambda h, half=half: h.activation(out=b[2 + half][:, :], in_=psA[:, :, :].rearrange("p a b -> p (a b)"), func=AF.Copy),
                            r=["psA"], w=[bn_[2 + half]])
                    for hh in range(8):
                        for side in range(2):
                            blk = 2 * hh + side
                            src = b[2 + blk // 8][:, (blk % 8) * 128:(blk % 8 + 1) * 128]
                            pe(lambda h, hh=hh, side=side, src=src: h.matmul(psS[side][:, hh // 4, (hh % 4) * 128:(hh % 4 + 1) * 128], lhsT=src,
                                                                             rhs=pkT[:, 2 * hh + side, :], start=True, stop=True),
                               r=["b2", "b3", "W"], w=["psS%d" % side])
                    for side in range(2):
                        act(lambda h, side=side: h.activation(out=f[2 + side][:, :], in_=psS[side][:, :, :].rearrange("p a b -> p (a b)"), func=AF.Copy),
                            r=["psS%d" % side], w=[fn_[2 + side]])
                    def ct(nm, side, hh):
                        return "%s%d_%d" % (nm, side, hh)
                    chains = [(side, hh) for hh in range(8) for side in range(2)]
                    wktags = [ct("wk", side, hh) for (side, hh) in chains]
                    dve(lambda h: h.memset(dmy[:, 0:1], 0.0), w=["f4", "f5"] + wktags)
                    for step in range(5):
                        for (side, hh) in chains:
                            S_ = f[2 + side]
                            Wk = f[4 + side]
                            vv = v1 if side == 0 else v2
                            ii = i1 if side == 0 else i2
                            sl = slice(hh * 128, (hh + 1) * 128)
                            if step == 0:
                                dve(lambda h, S_=S_, vv=vv, hh=hh, sl=sl: h.max(out=vv[:, hh, 0:8], in_=S_[:, sl]), r=[fn_[2 + side]], w=[ct("va", side, hh)])
                            elif step == 1:
                                dve(lambda h, S_=S_, Wk=Wk, vv=vv, hh=hh, sl=sl: h.match_replace(out=Wk[:, sl], in_to_replace=vv[:, hh, 0:8], in_values=S_[:, sl], imm_value=-1e30),
                                    r=[fn_[2 + side], ct("va", side, hh)], w=[ct("wk", side, hh)])
                            elif step == 2:
                                dve(lambda h, Wk=Wk, vv=vv, hh=hh, sl=sl: h.max(out=vv[:, hh, 8:16], in_=Wk[:, sl]), r=[ct("wk", side, hh)], w=[ct("vb", side, hh)])
                            elif step == 3:
                                dve(lambda h, S_=S_, vv=vv, ii=ii, hh=hh, sl=sl: h.max_index(out=ii[:, hh, 0:8], in_max=vv[:, hh, 0:8], in_values=S_[:, sl]),
                                    r=[fn_[2 + side], ct("va", side, hh)], w=[ct("ia", side, hh)])
                            else:
                                dve(lambda h, Wk=Wk, vv=vv, ii=ii, hh=hh, sl=sl: h.max_index(out=ii[:, hh, 8:16], in_max=vv[:, hh, 8:16], in_values=Wk[:, sl]),
                                    r=[ct("wk", side, hh), ct("vb", side, hh)], w=[ct("ib", side, hh)])
                    dve(lambda h: h.memset(dmy[:, 1:2], 0.0), w=["f4", "f5"] + wktags)
                    vtags = [ct(nm, side, hh) for nm in ("va", "vb") for side in range(2) for hh in range(8)]
                    cwtags = ["cw%d" % k for k in range(4)]
                    PRE = 5
                    gbufs = [(gbb[:, k, :], "gb%d" % k) for k in range(4)]
                    for tns, tg in ((candix, "candix"), (cand, "cand"), (candw, "candw")):
                        gbufs.append((tns[:, :, :].rearrange("p a b -> p (a b)").bitcast(BF16), tg))
                    dve(lambda h: h.memset(dmy[:, 2:3], 0.0), w=["candw"] + cwtags)
                    dve(lambda h: h.tensor_scalar(out=i1f[:], in0=i1[:], scalar1=128.0, scalar2=None, op0=ALU.mult),
                        r=[ct(nm, 0, hh) for nm in ("ia", "ib") for hh in range(8)], w=["i1f"])
                    dve(lambda h: h.tensor_copy(out=i2f[:], in_=i2[:]), r=[ct(nm, 1, hh) for nm in ("ia", "ib") for hh in range(8)], w=["i2f"])
                    for hg in range(2):
                        hs = slice(hg * 4, hg * 4 + 4)
                        dve(lambda h, hs=hs: h.tensor_tensor(out=cand[:, :, :].rearrange("p h (a b) -> p h a b", a=16),
                                                             in0=v1[:, hs, :].unsqueeze(3).to_broadcast([128, 4, 16, 16]),
                                                             in1=v2[:, hs, :].unsqueeze(2).to_broadcast([128, 4, 16, 16]), op=ALU.add),
                            r=vtags, w=["cand"])
                        for step in range(5):
                            for h4 in range(4):
                                hh = hg * 4 + h4
                                if step == 0:
                                    dve(lambda h, h4=h4, hh=hh: h.max(out=sc[:, hh, 0:8], in_=cand[:, h4, :]), r=["cand"], w=["sca%d" % hh])
                                elif step == 1:
                                    dve(lambda h, h4=h4, hh=hh: h.match_replace(out=candw[:, h4, :], in_to_replace=sc[:, hh, 0:8], in_values=cand[:, h4, :], imm_value=-1e30),
                                        r=["cand", "sca%d" % hh], w=["cw%d" % h4])
                                elif step == 2:
                                    dve(lambda h, h4=h4, hh=hh: h.max(out=sc[:, hh, 8:16], in_=candw[:, h4, :]), r=["cw%d" % h4], w=["scb%d" % hh])
                                elif step == 3:
                                    dve(lambda h, h4=h4, hh=hh: h.max_index(out=icu[:, hh, 0:8], in_max=sc[:, hh, 0:8], in_values=cand[:, h4, :]),
                                        r=["cand", "sca%d" % hh], w=["ica%d" % hh])
                                else:
                                    dve(lambda h, h4=h4, hh=hh: h.max_index(out=icu[:, hh, 8:16], in_max=sc[:, hh, 8:16], in_values=candw[:, h4, :]),
                                        r=["cw%d" % h4, "scb%d" % hh], w=["icb%d" % hh])
                        hv = hg
                        js = slice(hv * 64, (hv + 1) * 64)
                        hsl = slice(hv * 4, hv * 4 + 4)
                        dve(lambda h, hsl=hsl: h.tensor_copy(out=icf[:, hsl, :], in_=icu[:, hsl, :]),
                            r=["ica%d" % hh for hh in range(hv * 4, hv * 4 + 4)] + ["icb%d" % hh for hh in range(hv * 4, hv * 4 + 4)], w=["icf%d" % hv])
                        icf2 = icf[:, :, :].rearrange("p h k -> p (h k)")
                        thr = iota[:, :].rearrange("p (m s) -> p m s", s=16)[:, 1:16, 0]
                        tb, tbt = (f[0], "f0") if hv == 0 else (f[2], "f2")
                        t3 = tb[:, 0:960].rearrange("p (j m) -> p j m", m=15)
                        dve(lambda h, t3=t3, js=js: h.tensor_tensor(out=t3, in0=icf2[:, js].unsqueeze(2).to_broadcast([128, 64, 15]),
                                                                    in1=thr.unsqueeze(1).to_broadcast([128, 64, 15]), op=ALU.is_ge), r=["icf%d" % hv, "iota"], w=[tbt])
                        dve(lambda h, t3=t3, js=js: h.tensor_reduce(out=af[:, js], in_=t3, axis=AX.X, op=ALU.add), r=[tbt], w=["af%d" % hv])
                        dve(lambda h, js=js: h.scalar_tensor_tensor(out=bf_[:, js], in0=af[:, js], scalar=-16.0, in1=icf2[:, js], op0=ALU.mult, op1=ALU.add),
                            r=["af%d" % hv, "icf%d" % hv], w=["bf%d" % hv])
                        for kk, (src, srct, tabf, tabt, dst, dstt) in enumerate(((af, "af%d" % hv, i1f, "i1f", ef, "ef%d" % hv), (bf_, "bf%d" % hv, i2f, "i2f", eb, "eb%d" % hv))):
                            buf, buft = ((f[3], "f3"), (f[0], "f0")) [kk] if hv == 0 else ((f[3], "f3"), (f[0], "f0"))[kk]
                            oh3 = buf[:, 0:1024].rearrange("p (j a) -> p j a", a=16)
                            oh4 = buf[:, 0:1024].rearrange("p (h k a) -> p h k a", h=4, k=16)
                            dve(lambda h, oh3=oh3, src=src, js=js: h.tensor_tensor(out=oh3, in0=iota[:, 0:16].unsqueeze(1).to_broadcast([128, 64, 16]),
                                                                                   in1=src[:, js].unsqueeze(2).to_broadcast([128, 64, 16]), op=ALU.is_equal),
                                r=["iota", srct], w=[buft])
                            dve(lambda h, oh4=oh4, tabf=tabf, hsl=hsl: h.tensor_tensor(out=oh4, in0=oh4, in1=tabf[:, hsl, :].unsqueeze(2).to_broadcast([128, 4, 16, 16]), op=ALU.mult),
                                r=[buft, tabt], w=[buft])
                            dve(lambda h, oh3=oh3, dst=dst, js=js: h.tensor_reduce(out=dst[:, js], in_=oh3, axis=AX.X, op=ALU.add), r=[buft], w=[dstt])
                        dve(lambda h, js=js: h.tensor_tensor(out=ef[:, js], in0=ef[:, js], in1=eb[:, js], op=ALU.add), r=["ef%d" % hv, "eb%d" % hv], w=["ef%d" % hv])
                        dve(lambda h, js=js: h.tensor_copy(out=eu[:, js], in_=ef[:, js]), r=["ef%d" % hv], w=["eu%d" % hv])
                        if hg == 0:
                            for j in range(PRE if LVL >= 4 else 0):
                                gbuf, gtag = gbufs[j % len(gbufs)]
                                P.op("pool", lambda h, j=j, gbuf=gbuf: h.indirect_dma_start(out=gbuf, out_offset=None, in_=UV_d,
                                                                                        in_offset=bass.IndirectOffsetOnAxis(ap=eu[:, j:j + 1], axis=0)),
                                     r=["eu0"], w=[gtag], slot="gu%d" % (j % len(gbufs)))
                    dve(lambda h: h.memset(dmy[:, 3:4], 0.0), w=["candw"] + cwtags)
                    sctags = ["sca%d" % hh for hh in range(8)] + ["scb%d" % hh for hh in range(8)]
                    dve(lambda h: h.tensor_tensor(out=gw[:, :].rearrange("p (h k) -> p h k", h=8), in0=sc[:, :, :],
                                                  in1=sc[:, :, 0:1].to_broadcast([128, 8, 16]), op=ALU.subtract), r=sctags, w=["gw"])
                    act(lambda h: h.activation(out=gw[:], in_=gw[:], func=AF.Exp), r=["gw"], w=["gw"])
                    dve(lambda h: h.tensor_reduce(out=sm2[:, 24:32], in_=gw[:, :].rearrange("p (h k) -> p h k", h=8), axis=AX.X, op=ALU.add), r=["gw"], w=["sm2"])
                    dve(lambda h: h.reciprocal(out=sm2[:, 32:40], in_=sm2[:, 24:32]), r=["sm2"], w=["sm2"])
                    dve(lambda h: h.tensor_tensor(out=gw[:, :].rearrange("p (h k) -> p h k", h=8), in0=gw[:, :].rearrange("p (h k) -> p h k", h=8),
                                                  in1=sm2[:, 32:40].unsqueeze(2).to_broadcast([128, 8, 16]), op=ALU.mult), r=["gw", "sm2"], w=["gw"])
                    NJ = 128 if LVL >= 4 else 0
                    for j in range(NJ):
                        s4 = j % 4
                        gbuf, gtag = gbufs[j % len(gbufs)]
                        if j >= PRE:
                            P.op("pool", lambda h, j=j, gbuf=gbuf: h.indirect_dma_start(out=gbuf, out_offset=None, in_=UV_d,
                                                                                    in_offset=bass.IndirectOffsetOnAxis(ap=eu[:, j:j + 1], axis=0)),
                                 r=["eu%d" % (j // 64)], w=[gtag], slot="gu%d" % (j % len(gbufs)))
                        dve(lambda h, j=j, gbuf=gbuf: h.scalar_tensor_tensor(out=f[0][:], in0=gbuf[:, 0:1024], scalar=1.0, in1=f[1][:], op0=ALU.mult, op1=ALU.mult,
                                                                         accum_out=aa[:, j:j + 1]), r=[gtag, "f1"], w=["f0", "aa%d" % s4])
                        act(lambda h, j=j: h.activation(out=ga[:, j:j + 1], in_=aa[:, j:j + 1], func=AF.Gelu), r=["aa%d" % s4], w=["ga%d" % s4])
                        act(lambda h, j=j: h.activation(out=ww[:, j:j + 1], in_=ga[:, j:j + 1], func=AF.Copy, scale=gw[:, j:j + 1]),
                            r=["ga%d" % s4, "gw"], w=["ww%d" % s4])
                        act(lambda h, j=j, s4=s4: h.activation(out=dg[:, s4, :], in_=idb[:], func=AF.Copy, scale=ww[:, j:j + 1]),
                            r=["ww%d" % s4, "idb"], w=["dg%d" % s4])
                        for hf in range(2):
                            pe(lambda h, j=j, s4=s4, hf=hf, gbuf=gbuf: h.matmul(psA[:, hf, :], lhsT=dg[:, s4, :], rhs=gbuf[:, 1024 + hf * 512:1024 + (hf + 1) * 512],
                                                                                start=(j == 0), stop=(j == NJ - 1)),
                               r=["dg%d" % s4, gtag], w=["psA"])
                    if NJ:
                        dve(lambda h: h.tensor_tensor(out=X[:], in0=X[:], in1=psA[:, :, :].rearrange("p a b -> p (a b)"), op=ALU.add), r=["X", "psA"], w=["X"])
                P.mark("pre_store")
                pr = st_par[0] % 2
                st_par[0] += 1
                dve(lambda h, pr=pr: h.tensor_copy(out=f[4 + pr][:], in_=X[:]), r=["X"], w=[fn_[4 + pr]])
                dma(oap[qi * 128:(qi + 1) * 128, :], f[4 + pr][:], r=[fn_[4 + pr]], slot="out%d" % pr)

        P.op("sp", lambda h: h.nop(), extra=list(P.slot_last.values()), force=True)
        if os.environ.get("KMARKS"):
            print("MARKS", P.marks, "TOTAL", P.nops)
        P.emit()
    return nc


def _rope_tabs(pos1d, posr, posc):
    T = pos1d.shape[0]
    out = np.zeros((T, 4, 64), np.float32)
    invA = (np.float32(10000.0) ** (-np.arange(0, 64, 2, dtype=np.float32) / np.float32(64))).astype(np.float32)
    invB = (np.float32(10000.0) ** (-np.arange(0, 32, 2, dtype=np.float32) / np.float32(32))).astype(np.float32)
    angA = pos1d.astype(np.float32)[:, None] * invA[None, :]
    cA, sA = np.cos(angA).astype(np.float32), np.sin(angA).astype(np.float32)
    out[:, 0, :32] = cA
    out[:, 0, 32:] = cA
    out[:, 1, :32] = -sA
    out[:, 1, 32:] = sA
    angR = posr.astype(np.float32)[:, None] * invB[None, :]
    angC = posc.astype(np.float32)[:, None] * invB[None, :]
    cR, sR = np.cos(angR).astype(np.float32), np.sin(angR).astype(np.float32)
    cC, sC = np.cos(angC).astype(np.float32), np.sin(angC).astype(np.float32)
    out[:, 2, 0:16] = cR
    out[:, 2, 16:32] = cR
    out[:, 2, 32:48] = cC
    out[:, 2, 48:64] = cC
    out[:, 3, 0:16] = -sR
    out[:, 3, 16:32] = sR
    out[:, 3, 32:48] = -sC
    out[:, 3, 48:64] = sC
    return out


def _consts(qoff, SS, SQ):
    kk = np.arange(128)[:, None]
    qq = np.arange(128)[None, :]
    prev = (qq <= kk).astype(np.float32)
    nxt = (kk <= qq).astype(np.float32)
    m = np.zeros((4, 128, 512), np.float32)
    m[0] = np.tile(prev, (1, 4))
    m[1] = np.tile(nxt, (1, 4))
    m[2] = m[0] if qoff > 0 else 0.0
    m[3] = m[1] if qoff + SQ < SS else 0.0
    return m


def make_in_maps(inputs, NP, SP, SS, SQ, n_cores, cores_per_sample):
    xp, xs, mp, ms = inputs["x_prompt"], inputs["x_sample"], inputs["mem_prompt"], inputs["mem_sample"]
    wts = {n: np.ascontiguousarray(np.asarray(inputs[n], np.float32).reshape(WSHAPES[n])) for n in WNAMES}
    ident = np.eye(128, dtype=np.float32)
    tp = np.arange(SP)
    tab_p = _rope_tabs(tp, tp // 64, tp % 64)
    maps = []
    for c in range(n_cores):
        sb_, part = c // cores_per_sample, c % cores_per_sample
        qoff = part * SQ
        order = (np.arange(SS) + qoff) % SS
        tab_s = _rope_tabs(order, order // 64, order % 64)
        m = dict(wts)
        m["xp"] = np.ascontiguousarray(xp[c * NP:(c + 1) * NP])
        m["memp"] = np.ascontiguousarray(mp[c * NP:(c + 1) * NP])
        m["xs"] = np.ascontiguousarray(xs[sb_][order])
        m["mems"] = np.ascontiguousarray(ms[sb_])
        m["tabs"] = np.ascontiguousarray(np.concatenate([tab_p] * NP + [tab_s], axis=0))
        m["cmask"] = _consts(qoff, SS, SQ)
        m["ident"] = ident
        m["iota"] = np.ascontiguousarray(np.tile(np.arange(256, dtype=np.float32)[None, :], (128, 1)))
        maps.append(m)
    return maps


_NC_CACHE = {}


def kernel(**inputs):
    NP, SP, SS, SQ, KBLK = 4, 2048, 16384, 4096, 2048
    n = 8
    key = (NP, SP, SS, SQ, KBLK)
    if key not in _NC_CACHE:
        _NC_CACHE[key] = build(*key)
    nc = _NC_CACHE[key]
    maps = make_in_maps(inputs, NP, SP, SS, SQ, n, 4)
    res = run_bass_kernel_spmd(nc, maps, core_ids=list(range(n)))
    yp = np.concatenate([r["yp"] for r in res.results], axis=0).astype(np.float32)
    ys = np.stack([np.concatenate([res.results[s * 4 + p]["ys"] for p in range(4)], axis=0) for s in range(2)], axis=0).astype(np.float32)
    return (yp, ys)
```
